# Optimizing a Trainium2 kernel written in Bass

```python
import jax, jax.numpy as jnp
from jax import lax
import numpy as np

D_MODEL = 2048
BATCH = 4
SEQ = 2048
DEPTH = 1
DEC_BATCH = 8
DEC_SEQ = 8
PAST_LEN = 16384
PAGE_SIZE = 128

N_HEADS_A = 8
HEAD_DIM = 128
D_ATTN = N_HEADS_A * HEAD_DIM
D_CONV = D_MODEL - D_ATTN
CONV_WIDTH = 31
DILATED_PATTERNS = ((128, 1), (512, 4), (2048, 16))
MAX_WINDOW = max(w for w, _ in DILATED_PATTERNS)
BAND_BLOCK = 128
D_FF = 4 * D_MODEL
D_IN = 3 * D_ATTN + 2 * D_CONV
LN_EPS = 1e-5
DN_ALPHA = (2 * DEPTH) ** 0.25
DN_BETA = (8 * DEPTH) ** -0.25
ATTN_SCALE = HEAD_DIM ** -0.5

kernel_name = 'hymba_dilated_attn_conformer_conv_decode_step'


def layer_norm(x, g, b):
    xf = x.astype(jnp.float32)
    mu = jnp.mean(xf, axis=-1, keepdims=True)
    var = jnp.mean(jnp.square(xf - mu), axis=-1, keepdims=True)
    y = (xf - mu) * lax.rsqrt(var + LN_EPS) * g.astype(jnp.float32) + b.astype(jnp.float32)
    return y.astype(x.dtype)


def split_proj(x, w_in):
    p = jnp.einsum('ntd,de->nte', x, w_in)
    q, k, v, a, g = jnp.split(p, [D_ATTN, 2 * D_ATTN, 3 * D_ATTN, 3 * D_ATTN + D_CONV], axis=-1)
    heads = lambda t: t.reshape(t.shape[0], t.shape[1], N_HEADS_A, HEAD_DIM)
    return heads(q), heads(k), heads(v), a * jax.nn.sigmoid(g)


def dilated_band_attention(q, k, v, dil, band):
    B, S, H, Dh = q.shape
    L = S // dil
    nb = -(-L // BAND_BLOCK)
    Lp = nb * BAND_BLOCK

    def residues(t):
        t = t.reshape(B, L, dil, H, Dh).transpose(0, 2, 1, 3, 4)
        t = jnp.pad(t, ((0, 0), (0, 0), (0, Lp - L), (0, 0), (0, 0)))
        return t.reshape(B, dil, nb, BAND_BLOCK, H, Dh)

    def with_prev(t):
        prev = jnp.pad(t[:, :, :-1], ((0, 0), (0, 0), (1, 0), (0, 0), (0, 0), (0, 0)))
        return jnp.concatenate([prev, t], axis=3)

    qb = residues(q).astype(jnp.float32)
    kk = with_prev(residues(k)).astype(jnp.float32)
    vv = with_prev(residues(v)).astype(jnp.float32)
    s = jnp.einsum('brnqhd,brnkhd->brnhqk', qb, kk) * ATTN_SCALE
    qi = jnp.arange(BAND_BLOCK)[:, None]
    ki = jnp.arange(2 * BAND_BLOCK)[None, :]
    delta = BAND_BLOCK + qi - ki
    kpos = jnp.arange(nb)[:, None, None] * BAND_BLOCK - BAND_BLOCK + ki[None]
    mask = (delta >= 0) & (delta <= band) & (kpos >= 0)
    s = jnp.where(mask[:, None], s, -jnp.inf)
    lse = jax.nn.logsumexp(s, axis=-1)
    p = jnp.exp(s - lse[..., None])
    o = jnp.einsum('brnhqk,brnkhd->brnqhd', p, vv)
    o = o.reshape(B, dil, Lp, H, Dh)[:, :, :L].transpose(0, 2, 1, 3, 4).reshape(B, S, H, Dh)
    lse = lse.transpose(0, 1, 2, 4, 3).reshape(B, dil, Lp, H)[:, :, :L]
    lse = lse.transpose(0, 2, 1, 3).reshape(B, S, H)
    return o, lse


def dilated_gather_attention(q, ext_k, ext_v, dil, band, n_buf):
    T = q.shape[1]
    j = jnp.arange(band + 1)
    idx = n_buf + jnp.arange(T)[:, None] - dil * j[None, :]
    valid = idx >= 0
    idx = jnp.maximum(idx, 0)
    kg = ext_k[:, idx].astype(jnp.float32)
    vg = ext_v[:, idx].astype(jnp.float32)
    s = jnp.einsum('nthd,ntjhd->nhtj', q.astype(jnp.float32), kg) * ATTN_SCALE
    s = jnp.where(valid[None, None], s, -jnp.inf)
    lse = jax.nn.logsumexp(s, axis=-1)
    p = jnp.exp(s - lse[..., None])
    o = jnp.einsum('nhtj,ntjhd->nthd', p, vg)
    return o, lse.transpose(0, 2, 1)


def combine_patterns(outs, lses):
    w = jax.nn.softmax(jnp.stack(lses), axis=0)
    return jnp.einsum('pnth,pnthd->nthd', w, jnp.stack(outs))


def conv_tail(u_ext, w_dw, b_dw, ln_g, ln_b):
    y = lax.conv_general_dilated(u_ext, w_dw[:, None, :], window_strides=(1,), padding='VALID',
                                 dimension_numbers=('NWC', 'WIO', 'NWC'),
                                 feature_group_count=D_CONV) + b_dw
    return jax.nn.silu(layer_norm(y, ln_g, ln_b))


def block_out(x, o_attn, c, w_out, ln1_g, ln1_b, w_up, w_down, ln2_g, ln2_b):
    n, t = x.shape[0], x.shape[1]
    mixed = jnp.concatenate([o_attn.astype(x.dtype).reshape(n, t, D_ATTN), c], axis=-1)
    z = jnp.einsum('nte,ed->ntd', mixed, w_out)
    h = layer_norm(DN_ALPHA * x + z, ln1_g, ln1_b)
    f = jnp.einsum('ntf,fd->ntd', jnp.square(jax.nn.relu(jnp.einsum('ntd,df->ntf', h, w_up))), w_down)
    return layer_norm(DN_ALPHA * h + f, ln2_g, ln2_b)


def setup_inputs(seed: int = 0) -> dict:
    key = jax.random.key(seed)
    ks = jax.random.split(key, 18)
    n_buf = min(MAX_WINDOW, PAST_LEN)
    nrm = lambda k, shape, scale: jax.random.normal(k, shape, jnp.float32) * scale
    return {
        'x_prompt': nrm(ks[0], (BATCH, SEQ, D_MODEL), 1.0),
        'x_sample': nrm(ks[1], (DEC_BATCH, DEC_SEQ, D_MODEL), 1.0),
        'cache_k': nrm(ks[2], (DEPTH, DEC_BATCH, n_buf, N_HEADS_A, HEAD_DIM), 1.0),
        'cache_v': nrm(ks[3], (DEPTH, DEC_BATCH, n_buf, N_HEADS_A, HEAD_DIM), 1.0),
        'state_conv': nrm(ks[4], (DEPTH, DEC_BATCH, CONV_WIDTH - 1, D_CONV), 0.5),
        'w_in': nrm(ks[5], (DEPTH, D_MODEL, D_IN), D_MODEL ** -0.5),
        'w_dw': nrm(ks[6], (DEPTH, CONV_WIDTH, D_CONV), CONV_WIDTH ** -0.5),
        'b_dw': nrm(ks[7], (DEPTH, D_CONV), 0.02),
        'ln_conv_g': 1.0 + nrm(ks[8], (DEPTH, D_CONV), 0.02),
        'ln_conv_b': nrm(ks[9], (DEPTH, D_CONV), 0.02),
        'w_out': nrm(ks[10], (DEPTH, D_ATTN + D_CONV, D_MODEL), (D_ATTN + D_CONV) ** -0.5 * DN_BETA),
        'ln1_g': 1.0 + nrm(ks[11], (DEPTH, D_MODEL), 0.02),
        'ln1_b': nrm(ks[12], (DEPTH, D_MODEL), 0.02),
        'w_up': nrm(ks[13], (DEPTH, D_MODEL, D_FF), D_MODEL ** -0.5),
        'w_down': nrm(ks[14], (DEPTH, D_FF, D_MODEL), D_FF ** -0.5 * DN_BETA),
        'ln2_g': 1.0 + nrm(ks[15], (DEPTH, D_MODEL), 0.02),
        'ln2_b': nrm(ks[16], (DEPTH, D_MODEL), 0.02),
    }


def reference(x_prompt, x_sample, cache_k, cache_v, state_conv, w_in, w_dw, b_dw, ln_conv_g,
              ln_conv_b, w_out, ln1_g, ln1_b, w_up, w_down, ln2_g, ln2_b):
    n_buf = cache_k.shape[2]
    keep_p = min(MAX_WINDOW, x_prompt.shape[1])
    xp, xs = x_prompt, x_sample
    kp_l, vp_l, cp_l, ks_l, vs_l, cs_l = [], [], [], [], [], []
    for l in range(DEPTH):
        q, k, v, u = split_proj(xp, w_in[l])
        outs, lses = zip(*[dilated_band_attention(q, k, v, d, w // d) for w, d in DILATED_PATTERNS])
        o = combine_patterns(outs, lses)
        u_ext = jnp.pad(u, ((0, 0), (CONV_WIDTH - 1, 0), (0, 0)))
        c = conv_tail(u_ext, w_dw[l], b_dw[l], ln_conv_g[l], ln_conv_b[l])
        kp_l.append(k[:, -keep_p:])
        vp_l.append(v[:, -keep_p:])
        cp_l.append(u_ext[:, -(CONV_WIDTH - 1):])
        xp = block_out(xp, o, c, w_out[l], ln1_g[l], ln1_b[l], w_up[l], w_down[l], ln2_g[l], ln2_b[l])
        q, k, v, u = split_proj(xs, w_in[l])
        ext_k = jnp.concatenate([cache_k[l], k], axis=1)
        ext_v = jnp.concatenate([cache_v[l], v], axis=1)
        outs, lses = zip(*[dilated_gather_attention(q, ext_k, ext_v, d, w // d, n_buf)
                           for w, d in DILATED_PATTERNS])
        o = combine_patterns(outs, lses)
        u_ext = jnp.concatenate([state_conv[l], u], axis=1)
        c = conv_tail(u_ext, w_dw[l], b_dw[l], ln_conv_g[l], ln_conv_b[l])
        ks_l.append(ext_k[:, -n_buf:])
        vs_l.append(ext_v[:, -n_buf:])
        cs_l.append(u_ext[:, -(CONV_WIDTH - 1):])
        xs = block_out(xs, o, c, w_out[l], ln1_g[l], ln1_b[l], w_up[l], w_down[l], ln2_g[l], ln2_b[l])
    return (xp, xs, jnp.stack(kp_l), jnp.stack(vp_l), jnp.stack(cp_l),
            jnp.stack(ks_l), jnp.stack(vs_l), jnp.stack(cs_l))
```

```python
import numpy as np
import concourse.bass as bass
import concourse.mybir as mybir
from concourse.bass_utils import run_bass_kernel_spmd

F32 = mybir.dt.float32
BF16 = mybir.dt.bfloat16
AF = mybir.ActivationFunctionType
ALU = mybir.AluOpType

D = 2048
NP = 1024
NS = 8
NT = NP + NS
DIN = 5120
DFF = 8192
ALPHA = float(2.0 ** 0.25)
EPS = 1e-5
SCALE = float(128.0 ** -0.5)
TB = [(0, 512), (512, 512), (1024, 8)]
TT = [(i * 128, 128) for i in range(8)] + [(1024, 8)]
NSLOT = 2
STOP = 99
GLIMIT = 10 ** 9


class _Stop(Exception):
    pass
SLOT_BYTES = 16384


class Tk:
    def __init__(self, name, inherit=None, strict=False):
        self.name = name
        self.strict = strict
        self.w = None
        self.r = {}
        self.dsem = None
        self.dcnt = 0
        if inherit:
            for d in inherit:
                self._addr(d)

    def _addr(self, d):
        k = d[0]
        if k not in self.r or self.r[k][2] < d[2]:
            self.r[k] = d

    def alldeps(self):
        out = list(self.r.values())
        if self.w is not None:
            out.append(self.w)
        return out


class KB:
    def __init__(self, nc):
        self.nc = nc
        self.semkey = 0
        self.eng = {}
        for name, h in (("pe", nc.tensor), ("act", nc.scalar), ("dve", nc.vector),
                        ("pool", nc.gpsimd), ("sp", nc.sync)):
            self.eng[name] = dict(h=h, sem=nc.alloc_semaphore(name="sem_" + name), cnt=0,
                                  waited={}, key=self._newkey())
        self.final = []
        self.nsem = 5

    def _newkey(self):
        self.semkey += 1
        return self.semkey

    def _collect(self, engname, reads, writes):
        deps = {}

        def add(d, skip_same):
            if d is None:
                return
            if d[3] == engname and (skip_same or engname == "pe"):
                return
            k = d[0]
            if k not in deps or deps[k][2] < d[2]:
                deps[k] = d

        for t in reads:
            add(t.w, False)
        for t in writes:
            add(t.w, not t.strict)
            for d in t.r.values():
                add(d, not t.strict)
        return deps

    def _wait(self, engname, deps):
        e = self.eng[engname]
        for k, d in deps.items():
            if e["waited"].get(k, 0) < d[2]:
                e["h"].wait_ge(d[1], d[2])
                e["waited"][k] = d[2]

    def emit(self, engname, fn, reads=(), writes=(), signal=True):
        e = self.eng[engname]
        self._wait(engname, self._collect(engname, reads, writes))
        inst = fn()
        if signal:
            e["cnt"] += 1
            inst.then_inc(e["sem"], 1)
            val = e["cnt"]
        else:
            val = e["cnt"] + 1
        d = (e["key"], e["sem"], val, engname)
        for t in reads:
            t._addr(d)
        for t in writes:
            t.w = d
            t.r = {}
        return inst

    def dma(self, queue, pairs, reads=(), writes=(), owner=None, final=False, after=()):
        e = self.eng[queue]
        deps = self._collect(queue + "_dma", reads, writes)
        for d in after:
            if d is not None and (d[0] not in deps or deps[d[0]][2] < d[2]):
                deps[d[0]] = d
        self._wait(queue, deps)
        if owner is None:
            owner = writes[0] if writes else reads[0]
        if owner.dsem is None:
            owner.dsem = self.nc.alloc_semaphore(name="dsem_%s_%d" % (owner.name, self.nsem))
            owner.dkey = self._newkey()
            self.nsem += 1
        for (o, i) in pairs:
            e["h"].dma_start(out=o, in_=i).then_inc(owner.dsem, 16)
            owner.dcnt += 1
        d = (owner.dkey, owner.dsem, 16 * owner.dcnt, "dma")
        for t in reads:
            t._addr(d)
        for t in writes:
            t.w = d
            t.r = {}
        if final:
            self.final.append(d)

    def finish(self):
        deps = {}
        for d in self.final:
            if d[0] not in deps or deps[d[0]][2] < d[2]:
                deps[d[0]] = d
        for k, d in deps.items():
            self.eng["sp"]["h"].wait_ge(d[1], d[2])


class Rot:
    def __init__(self, items):
        self.items = items
        self.i = 0

    def next(self):
        x = self.items[self.i % len(self.items)]
        self.i += 1
        return x


def build_program():
    nc = bass.Bass("TRN2", target_bir_lowering=False)
    kb = KB(nc)
    try:
        _build(nc, kb)
    except _Stop:
        pass
    kb.finish()
    return nc


def _build(nc, kb):

    def din(name, shape):
        return nc.dram_tensor(name, list(shape), F32, kind="ExternalInput").ap()

    def dout(name, shape):
        return nc.dram_tensor(name, list(shape), F32, kind="ExternalOutput").ap()

    xT_d = din("xT", [D, NT])
    xcT_d = din("xcT", [D, NP])
    xc32_d = din("xc32", [D, 32])
    xres_d = din("xres", [NT, D])
    w_in_d = din("w_in", [D, DIN])
    w_out_d = din("w_out", [D, D])
    w_up_d = din("w_up", [D, DFF])
    w_down_d = din("w_down", [DFF, D])
    pch_d = din("pch", [128, 8, 34])
    lngb_d = din("lngb", [128, 4, D])
    mown_d = din("mown", [128, 8 * 128])
    mctx_d = din("mctx", [128, 15 * 128])
    msam_d = din("msam", [128, 17 * 8])
    ident_d = din("ident", [128, 128])
    ln1T_d = din("ln1T", [128, 2, 16])
    ckT_d = din("ckT", [1024, 2048])
    ck_d = din("ck", [2048, 1024])
    cv_d = din("cv", [2048, 1024])
    scT_d = din("scT", [1024, 30])
    sc_d = din("sc", [30, 1024])

    y_d = dout("y", [NT, D])
    kout_d = dout("kout", [NP, 1024])
    vout_d = dout("vout", [NP, 1024])
    convp_d = dout("convp", [30, 1024])
    ks_d = dout("ks", [2048, 1024])
    vs_d = dout("vs", [2048, 1024])
    convs_d = dout("convs", [30, 1024])

    base = (nc.sbuf_base + 31) // 32 * 32
    cur = [base]
    cnt = [0]

    def region(nbytes):
        o = cur[0]
        cur[0] += (nbytes + 31) // 32 * 32
        return o

    def at(off, shape, dt):
        cnt[0] += 1
        return nc.alloc_sbuf_tensor_at("sb%d" % cnt[0], list(shape), dt, offset=off)

    W_OFF = region(NSLOT * SLOT_BYTES)
    BIG = region(73728)
    RX = region(33280)
    RQ = region(17664)
    RM = region(33024)
    TMP = region(13312)
    SM = region(8192)
    assert cur[0] <= nc.sbuf_top, (cur[0], nc.sbuf_top)

    slots = [at(W_OFF + i * SLOT_BYTES, [128, 8192], BF16) for i in range(NSLOT)]
    slot_t = [Tk("slot%d" % i) for i in range(NSLOT)]
    slot_rot = Rot(list(range(NSLOT)))

    so = [SM]

    def small(shape, dt, nbytes):
        t = at(so[0], shape, dt)
        so[0] += (nbytes + 31) // 32 * 32
        return t

    ident_b = small([128, 128], BF16, 256); ident_b_t = Tk("identb")
    ident_f = small([128, 128], F32, 512); ident_f_t = Tk("identf")
    ones_f = small([128, 128], F32, 512); ones_t = Tk("ones")
    pch = small([128, 8, 34], F32, 1088); pch_t = Tk("pch")
    eps_c = small([128, 1], F32, 4); eps_t = Tk("eps")
    xc32_b = small([128, 16, 32], BF16, 1024); xc32_t = Tk("xc32")
    u32 = small([128, 8, 40], F32, 1280); u32_t = Tk("u32")
    KsT = small([128, 8, 8], BF16, 128); KsT_t = Tk("KsT")
    Vs_new = small([8, 8, 129], BF16, 2064 + 16); Vs_t = Tk("Vsnew")
    st6 = [small([128, 4, 6], F32, 96) for _ in range(4)]
    st6_t = [Tk("st6_%d" % i, strict=True) for i in range(4)]
    mv = [small([128, 2], F32, 8) for _ in range(4)]
    mv_t = [Tk("mv%d" % i, strict=True) for i in range(4)]
    sd = [small([128, 2], F32, 8) for _ in range(4)]
    sd_t = [Tk("sd%d" % i, strict=True) for i in range(4)]
    nm = [small([128, 1], F32, 4) for _ in range(4)]
    nm_t = [Tk("nm%d" % i, strict=True) for i in range(4)]
    ln1T = small([128, 2, 16], F32, 128); ln1T_t = Tk("ln1T")
    sd9 = small([128, 9, 2], F32, 72)
    nm9 = small([128, 9], F32, 36)
    sd9_t = [Tk("sd9_%d" % i, strict=True) for i in range(9)]
    nm9_t = [Tk("nm9_%d" % i, strict=True) for i in range(9)]
    rc = [small([128, 1], F32, 4) for _ in range(4)]
    rc_t = [Tk("rc%d" % i, strict=True) for i in range(4)]
    assert so[0] <= SM + 8192, so[0]

    banks = [nc.alloc_psum_tensor("psb%d" % i, [128, 512], F32) for i in range(8)]
    bank_t = [Tk("bank%d" % i) for i in range(8)]

    pe, act, dve, pool = nc.tensor, nc.scalar, nc.vector, nc.gpsimd

    kb.dma("sp", [(ident_f.ap(), ident_d)], writes=[ident_f_t])
    kb.dma("pool", [(ident_b.ap(), ident_d)], writes=[ident_b_t])
    kb.dma("sp", [(pch.ap(), pch_d)], writes=[pch_t])
    kb.dma("sp", [(ln1T.ap(), ln1T_d)], writes=[ln1T_t])
    kb.emit("dve", lambda: dve.memset(ones_f.ap(), 1.0), writes=[ones_t])
    kb.emit("dve", lambda: dve.memset(eps_c.ap(), EPS), writes=[eps_t])
    kb.emit("dve", lambda: dve.memset(Vs_new.ap()[:, :, 128:129], 1.0), writes=[Vs_t])

    copy_t = Tk("dramcopy")

    def cache_copies_part(i):
        prs = [(dst[i * 255:(i + 1) * 255, :], src[8 + i * 255: 8 + (i + 1) * 255, :])
               for (dst, src) in ((ks_d, ck_d), (vs_d, cv_d))]
        if i == 0:
            prs.append((convs_d[0:22, :], sc_d[8:30, :]))
        kb.dma("sp", prs, owner=copy_t, final=True)

    KT = at(BIG, [128, 8, 2048], BF16); KT_t = Tk("KT")
    Vall = at(BIG + 32768, [128, 16, 8, 129], BF16)
    V_t = [Tk("V%d" % i) for i in range(16)]
    mown = at(BIG + 65792, [128, 8 * 128], BF16); mown_t = Tk("mown")
    mctx = at(BIG + 67840, [128, 15 * 128], BF16); mctx_t = Tk("mctx")
    msam = at(BIG + 71680, [128, 17 * 8], BF16); msam_t = Tk("msam")
    kb.emit("dve", lambda: dve.memset(Vall.ap()[:, :, :, 128:129], 1.0), writes=V_t)

    xcT_b = at(RM, [128, 16, NP], BF16); xcT_t = [Tk("xcT0"), Tk("xcT1")]
    xcv = xcT_d.rearrange("(c p) n -> p c n", p=128)

    w_in_v = w_in_d.rearrange("(c p) n -> p c n", p=128)
    w_out_v = w_out_d.rearrange("(c p) n -> p c n", p=128)
    w_up_v = w_up_d.rearrange("(c p) n -> p c n", p=128)
    w_down_v = w_down_d.rearrange("(c p) n -> p c n", p=128)

    def load_slot(pairs_fn):
        si = slot_rot.next()
        sv = slots[si].ap()
        kb.dma("pool", pairs_fn(sv), writes=[slot_t[si]])
        return si

    def load_w512(view, col0):
        def f(sv):
            s3 = sv.rearrange("p (c n) -> p c n", c=16)
            return [(s3[:, 0:8, :], view[:, 0:8, col0:col0 + 512]),
                    (s3[:, 8:16, :], view[:, 8:16, col0:col0 + 512])]
        si = load_slot(f)
        return si, slots[si].ap().rearrange("p (c n) -> p c n", c=16)

    mm_rot = Rot([0, 1, 2, 3, 4, 5])

    def add_slot(off, inherit):
        slots.append(at(off, [128, 8192], BF16))
        slot_t.append(Tk("slotx%d" % len(slots), inherit=inherit))
        slot_rot.items = list(range(len(slots)))

    def drop_slots():
        dead = sum([t.alldeps() for t in slot_t[NSLOT:]], [])
        del slots[NSLOT:]
        del slot_t[NSLOT:]
        slot_rot.items = list(range(NSLOT))
        return dead


    gcount = [0]

    def group(bank_i, out_ap, pairs, extra_reads):
        gcount[0] += 1
        if gcount[0] > GLIMIT:
            raise _Stop()
        n = len(pairs)
        for i, (l, r) in enumerate(pairs):
            kb.emit("pe", lambda l=l, r=r, i=i: pe.matmul(out_ap, l, r, start=(i == 0), stop=(i == n - 1)),
                    reads=extra_reads, writes=[bank_t[bank_i]], signal=(i == n - 1))

    if STOP <= 0:
        return
    pre_blk = load_w512(w_in_v, 1024)
    for hf in range(2):
        kb.dma("pool", [(xcT_b.ap()[:, 0:8, hf * 512:(hf + 1) * 512], xcv[:, 0:8, hf * 512:(hf + 1) * 512]),
                        (xcT_b.ap()[:, 8:16, hf * 512:(hf + 1) * 512], xcv[:, 8:16, hf * 512:(hf + 1) * 512])],
               writes=[xcT_t[hf]])
    kb.dma("pool", [(mown.ap(), mown_d)], writes=[mown_t])
    kb.dma("pool", [(mctx.ap(), mctx_d)], writes=[mctx_t])
    kb.dma("pool", [(msam.ap(), msam_d)], writes=[msam_t])
    kb.dma("pool", [(xc32_b.ap(), xc32_d.rearrange("(c p) n -> p c n", p=128))], writes=[xc32_t])
    xT_b = at(RX, [128, 16, NT], BF16); xT_t = Tk("xT")
    xv = xT_d.rearrange("(c p) n -> p c n", p=128)

    def load_xT():
        kb.dma("pool", [(xT_b.ap()[:, 0:8, :], xv[:, 0:8, :]), (xT_b.ap()[:, 8:16, :], xv[:, 8:16, :])],
               writes=[xT_t])
    xT_loaded = [False]

    for (kind, col0) in (("k", 1024), ("k", 1536), ("v", 2048), ("v", 2560)):
        si, sv = pre_blk if col0 == 1024 else load_w512(w_in_v, col0)
        if col0 == 1536:
            load_xT()
        if kind == "k":
            for ct in range(4):
                head = (col0 - 1024) // 128 + ct
                for (t0, n) in ((0, 512), (512, 512)):
                    bi = mm_rot.next()
                    group(bi, banks[bi].ap()[:, :n],
                          [(sv[:, c, ct * 128:(ct + 1) * 128], xcT_b.ap()[:, c, t0:t0 + n]) for c in range(16)],
                          [slot_t[si], xcT_t[t0 // 512]])
                    kb.emit("act", lambda bi=bi, head=head, t0=t0, n=n: act.copy(
                        out=KT.ap()[:, head, t0:t0 + n], in_=banks[bi].ap()[:, :n]),
                        reads=[bank_t[bi]], writes=[KT_t])
        else:
            h0 = (col0 - 2048) // 128
            for tt in range(8):
                bi = mm_rot.next()
                group(bi, banks[bi].ap()[:, :512],
                      [(xcT_b.ap()[:, c, tt * 128:(tt + 1) * 128], sv[:, c, :]) for c in range(16)],
                      [slot_t[si], xcT_t[tt // 4]])
                kb.emit("dve", lambda bi=bi, tt=tt, h0=h0: dve.tensor_copy(
                    out=Vall.ap()[:, tt, h0:h0 + 4, 0:128],
                    in_=banks[bi].ap().rearrange("p (h d) -> p h d", h=4)),
                    reads=[bank_t[bi]], writes=[V_t[tt]])

    if STOP <= 1:
        return
    QT = at(RQ, [128, 8, NT], BF16); QT_t = Tk("QT")
    kvst = [at(TMP + i * 2048, [128, 512], F32) for i in range(4)]
    kvst_t = [Tk("kvst%d" % i) for i in range(4)]
    kvst_rot = Rot(list(range(4)))
    kbf = [at(TMP + 8192 + i * 1024, [128, 512], BF16) for i in range(4)]
    kbf_t = [Tk("kbf%d" % i) for i in range(4)]
    kbf_rot = Rot(list(range(4)))
    ktr_rot = Rot([6, 7])

    for (kind, col0) in (("k", 1024), ("k", 1536), ("v", 2048), ("v", 2560), ("q", 0), ("q", 512)):
        si, sv = load_w512(w_in_v, col0)
        if kind == "q":
            for ct in range(4):
                head = (col0 % 1024) // 128 + ct
                for (t0, n) in TB:
                    bi = mm_rot.next()
                    group(bi, banks[bi].ap()[:, :n],
                          [(sv[:, c, ct * 128:(ct + 1) * 128], xT_b.ap()[:, c, t0:t0 + n]) for c in range(16)],
                          [slot_t[si], xT_t])
                    if kind == "q":
                        dst, dt_ = QT.ap()[:, head, t0:t0 + n], QT_t
                    elif t0 < NP:
                        dst, dt_ = KT.ap()[:, head, NP + t0:NP + t0 + n], KT_t
                    else:
                        dst, dt_ = KsT.ap()[:, head, 0:8], KsT_t
                    kb.emit("act", lambda bi=bi, dst=dst, n=n: act.copy(out=dst, in_=banks[bi].ap()[:, :n]),
                            reads=[bank_t[bi]], writes=[dt_])
        if kind in ("k", "v"):
            cb = (col0 % 1024)
            h0 = cb // 128
            pend = []

            def k_transposes(ti, t0, rows, qi, h0=h0):
                tb_ = ktr_rot.next()
                trv = banks[tb_].ap().bitcast(BF16)
                for hh in range(4):
                    kb.emit("pe", lambda hh=hh: pe.transpose(trv[:, hh * 128:hh * 128 + rows],
                                                             kbf[qi].ap()[:rows, hh * 128:(hh + 1) * 128],
                                                             ident_b.ap()[:rows, :rows]),
                            reads=[kbf_t[qi], ident_b_t], writes=[bank_t[tb_]], signal=(hh == 3))
                src = trv[:, 0:512].rearrange("p (h q) -> p h q", h=4)[:, :, :rows]
                if ti < 8:
                    dst, dt_ = KT.ap()[:, h0:h0 + 4, NP + t0:NP + t0 + rows], KT_t
                else:
                    dst, dt_ = KsT.ap()[:, h0:h0 + 4, 0:8], KsT_t
                kb.emit("act", lambda: act.copy(out=dst, in_=src), reads=[bank_t[tb_]], writes=[dt_])

            for ti, (t0, rows) in enumerate(TT):
                bi = mm_rot.next()
                group(bi, banks[bi].ap()[:rows, :512],
                      [(xT_b.ap()[:, c, t0:t0 + rows], sv[:, c, :]) for c in range(16)],
                      [slot_t[si], xT_t])
                if kind == "v":
                    if ti < 8:
                        dst, dt_ = Vall.ap()[:, 8 + ti, h0:h0 + 4, 0:128], V_t[8 + ti]
                    else:
                        dst, dt_ = Vs_new.ap()[0:8, h0:h0 + 4, 0:128], Vs_t
                ki = kvst_rot.next()
                kb.emit("act", lambda bi=bi, ki=ki, rows=rows: act.copy(
                    out=kvst[ki].ap()[:rows, :], in_=banks[bi].ap()[:rows, :]),
                    reads=[bank_t[bi]], writes=[kvst_t[ki]])
                if kind == "v":
                    kb.emit("dve", lambda ki=ki, dst=dst, rows=rows: dve.tensor_copy(
                        out=dst, in_=kvst[ki].ap()[:rows, :].rearrange("p (h d) -> p h d", h=4)),
                        reads=[kvst_t[ki]], writes=[dt_])
                if ti < 8:
                    od = (kout_d if kind == "k" else vout_d)[t0:t0 + 128, cb:cb + 512]
                else:
                    od = (ks_d if kind == "k" else vs_d)[2040:2048, cb:cb + 512]
                kb.dma("sp", [(od, kvst[ki].ap()[:rows, :])], reads=[kvst_t[ki]], final=True)
                if kind == "k":
                    qi = kbf_rot.next()
                    kb.emit("dve", lambda ki=ki, qi=qi, rows=rows: dve.tensor_copy(
                        out=kbf[qi].ap()[:rows, :], in_=kvst[ki].ap()[:rows, :]),
                        reads=[kvst_t[ki]], writes=[kbf_t[qi]])
                    pend.append((ti, t0, rows, qi))
                    if len(pend) > 2:
                        k_transposes(*pend.pop(0))
            while pend:
                k_transposes(*pend.pop(0))

    if STOP <= 2:
        return
    mixed = at(RM, [128, 16, NT], BF16); mixed_t = Tk("mixed", inherit=drop_slots() + xcT_t[0].alldeps() + xcT_t[1].alldeps())
    EP = [at(TMP + 6656 + i * 1024, [128, 512], BF16) for i in range(6)]
    EP_t = [Tk("EP%d" % i, inherit=sum([k.alldeps() for k in kvst_t + kbf_t], [])) for i in range(6)]
    EP_rot = Rot(list(range(6)))
    ao = [at(TMP + i * 2048, [128, 1024], BF16) for i in range(2)]
    ao_t = [Tk("ao%d" % i, inherit=sum([k.alldeps() for k in kvst_t + kbf_t], [])) for i in range(2)]
    S_rot = Rot([0, 1, 2, 3])
    O_banks = [4, 5]
    TR_bank = 6

    units = []
    for t in range(8):
        g = 8 + t
        for h in range(8):
            groups = [(0, 3, "c"), (4, 7, "c"), (8, min(11, g), "o")]
            if g >= 12:
                groups.append((12, g, "o"))
            for gi, (j0, j1, kind) in enumerate(groups):
                units.append(dict(t=t, g=g, h=h, j0=j0, j1=j1, kind=kind, first=(gi == 0),
                                  last=(gi == len(groups) - 1)))

    def emit_S(u):
        bi = S_rot.next()
        u["sb"] = bi
        n = u["j1"] - u["j0"] + 1
        t, h = u["t"], u["h"]
        for jj in range(n):
            j = u["j0"] + jj
            kb.emit("pe", lambda jj=jj, j=j: pe.matmul(
                banks[bi].ap()[:, jj * 128:(jj + 1) * 128], KT.ap()[:, h, j * 128:(j + 1) * 128],
                QT.ap()[:, h, t * 128:(t + 1) * 128], start=True, stop=True),
                reads=[KT_t, QT_t], writes=[bank_t[bi]], signal=(jj == n - 1))
        ei = EP_rot.next()
        u["ep"] = ei
        kb.emit("act", lambda: act.activation(out=EP[ei].ap()[:, :n * 128], in_=banks[bi].ap()[:, :n * 128],
                                              func=AF.Exp, scale=SCALE),
                reads=[bank_t[bi]], writes=[EP_t[ei]])
        if u["kind"] == "o":
            m0 = 7 - u["g"] + u["j0"]
            msk, mt = mown.ap()[:, m0 * 128:(m0 + n) * 128], mown_t
        else:
            m0 = 15 - u["g"] + u["j0"]
            msk, mt = mctx.ap()[:, m0 * 128:(m0 + n) * 128], mctx_t
        kb.emit("dve", lambda: dve.tensor_tensor(out=EP[ei].ap()[:, :n * 128], in0=EP[ei].ap()[:, :n * 128],
                                                 in1=msk, op=ALU.mult),
                reads=[EP_t[ei], mt], writes=[EP_t[ei]])

    def emit_PV(u):
        t, h = u["t"], u["h"]
        ob = O_banks[(t * 8 + h) % 2]
        n = u["j1"] - u["j0"] + 1
        ei = u["ep"]
        for jj in range(n):
            j = u["j0"] + jj
            kb.emit("pe", lambda jj=jj, j=j: pe.matmul(
                banks[ob].ap()[:, 0:129], EP[ei].ap()[:, jj * 128:(jj + 1) * 128], Vall.ap()[:, j, h, :],
                start=(u["first"] and jj == 0), stop=(u["last"] and jj == n - 1)),
                reads=[EP_t[ei], V_t[j]], writes=[bank_t[ob]], signal=(jj == n - 1))
        if u["last"]:
            ri = (t * 8 + h) % 4
            kb.emit("dve", lambda: dve.reciprocal(out=rc[ri].ap(), in_=banks[ob].ap()[:, 128:129]),
                    reads=[bank_t[ob]], writes=[rc_t[ri]])
            kb.emit("dve", lambda: dve.tensor_scalar(out=ao[t % 2].ap()[:, h * 128:(h + 1) * 128],
                                                     in0=banks[ob].ap()[:, 0:128], scalar1=rc[ri].ap()[:, 0:1],
                                                     scalar2=None, op0=ALU.mult),
                    reads=[bank_t[ob], rc_t[ri]], writes=[ao_t[t % 2]])
            if h == 7:
                trv = banks[TR_bank].ap().bitcast(BF16)
                for hh in range(8):
                    kb.emit("pe", lambda hh=hh: pe.transpose(trv[:, hh * 128:(hh + 1) * 128],
                                                             ao[t % 2].ap()[:, hh * 128:(hh + 1) * 128],
                                                             ident_b.ap()),
                            reads=[ao_t[t % 2], ident_b_t], writes=[bank_t[TR_bank]], signal=(hh == 7))
                kb.emit("act", lambda: act.copy(out=mixed.ap()[:, 0:8, t * 128:(t + 1) * 128],
                                                in_=trv.rearrange("p (h q) -> p h q", h=8)),
                        reads=[bank_t[TR_bank]], writes=[mixed_t])

    def prompt_attention():
        LAG = 4
        sample_load(0)
        for i, u in enumerate(units):
            emit_S(u)
            if i >= LAG:
                emit_PV(units[i - LAG])
            if u["h"] == 7 and u["last"]:
                sample_head(u["t"])
                if u["t"] < 7:
                    sample_load(u["t"] + 1)
        for u in units[len(units) - LAG:]:
            emit_PV(u)

    kc = at(W_OFF + SLOT_BYTES, [128, 2048], BF16)
    vc = at(W_OFF + SLOT_BYTES + 4096, [128, 16, 129], BF16)
    kv_t = slot_t[1]
    kb.emit("dve", lambda: dve.memset(vc.ap()[:, :, 128:129], 1.0), writes=[kv_t])
    Es = at(TMP + 4096, [128, 144], BF16)
    Es_t = Tk("Es", inherit=sum([k.alldeps() for k in kvst_t + kbf_t], []))
    ao_s = at(TMP + 4416, [8, 1024], BF16)
    ao_s_t = Tk("aos", inherit=sum([k.alldeps() for k in kvst_t + kbf_t], []))
    cv_v = cv_d.rearrange("(t p) (h d) -> p t h d", p=128, h=8)
    SO_bank = 7

    def sample_load(h):
        kb.dma("pool", [(kc.ap(), ckT_d[h * 128:(h + 1) * 128, :]),
                        (vc.ap()[:, :, 0:128], cv_v[:, :, h, :])], writes=[kv_t])

    def sample_head(h):
        bi = S_rot.next()
        for j in range(16):
            kb.emit("pe", lambda j=j: pe.matmul(banks[bi].ap()[:, j * 8:(j + 1) * 8],
                                                kc.ap()[:, j * 128:(j + 1) * 128],
                                                QT.ap()[:, h, NP:NT], start=True, stop=True),
                    reads=[kv_t, QT_t], writes=[bank_t[bi]], signal=False)
        kb.emit("pe", lambda: pe.matmul(banks[bi].ap()[0:8, 128:136], KsT.ap()[:, h, 0:8],
                                        QT.ap()[:, h, NP:NT], start=True, stop=True),
                reads=[KsT_t, QT_t], writes=[bank_t[bi]])
        kb.emit("act", lambda: act.activation(out=Es.ap()[:, 0:128], in_=banks[bi].ap()[:, 0:128],
                                              func=AF.Exp, scale=SCALE),
                reads=[bank_t[bi]], writes=[Es_t])
        kb.emit("act", lambda: act.activation(out=Es.ap()[0:8, 128:136], in_=banks[bi].ap()[0:8, 128:136],
                                              func=AF.Exp, scale=SCALE),
                reads=[bank_t[bi]], writes=[Es_t])
        kb.emit("dve", lambda: dve.tensor_tensor(out=Es.ap()[:, 0:128], in0=Es.ap()[:, 0:128],
                                                 in1=msam.ap()[:, 0:128], op=ALU.mult),
                reads=[Es_t, msam_t], writes=[Es_t])
        kb.emit("dve", lambda: dve.tensor_tensor(out=Es.ap()[0:8, 128:136], in0=Es.ap()[0:8, 128:136],
                                                 in1=msam.ap()[0:8, 128:136], op=ALU.mult),
                reads=[Es_t, msam_t], writes=[Es_t])
        ob = SO_bank
        for j in range(16):
            kb.emit("pe", lambda j=j: pe.matmul(banks[ob].ap()[0:8, 0:129], Es.ap()[:, j * 8:(j + 1) * 8],
                                                vc.ap()[:, j, :], start=(j == 0), stop=False),
                    reads=[Es_t, kv_t], writes=[bank_t[ob]], signal=False)
        kb.emit("pe", lambda: pe.matmul(banks[ob].ap()[0:8, 0:129], Es.ap()[0:8, 128:136],
                                        Vs_new.ap()[0:8, h, :], start=False, stop=True),
                reads=[Es_t, Vs_t], writes=[bank_t[ob]])
        ri = h % 4
        kb.emit("dve", lambda: dve.reciprocal(out=rc[ri].ap()[0:8, :], in_=banks[ob].ap()[0:8, 128:129]),
                reads=[bank_t[ob]], writes=[rc_t[ri]])
        kb.emit("dve", lambda: dve.tensor_scalar(out=ao_s.ap()[0:8, h * 128:(h + 1) * 128],
                                                 in0=banks[ob].ap()[0:8, 0:128], scalar1=rc[ri].ap()[0:8, 0:1],
                                                 scalar2=None, op0=ALU.mult),
                reads=[bank_t[ob], rc_t[ri]], writes=[ao_s_t])

    def sample_finish():
        trv = banks[TR_bank].ap().bitcast(BF16)
        for hh in range(8):
            kb.emit("pe", lambda hh=hh: pe.transpose(trv[:, hh * 8:(hh + 1) * 8],
                                                     ao_s.ap()[0:8, hh * 128:(hh + 1) * 128],
                                                     ident_b.ap()[0:8, 0:8]),
                    reads=[ao_s_t, ident_b_t], writes=[bank_t[TR_bank]], signal=(hh == 7))
        kb.emit("act", lambda: act.copy(out=mixed.ap()[:, 0:8, NP:NT],
                                        in_=trv[:, 0:64].rearrange("p (h q) -> p h q", h=8)),
                reads=[bank_t[TR_bank]], writes=[mixed_t])

    if STOP <= 3:
        return
    prompt_attention()
    sample_finish()
    if STOP <= 4:
        return
    u_bf = at(RQ, [128, 8, 1062], BF16)
    us_ext = at(RQ + 16992, [128, 8, 38], BF16)
    ubf_t = [Tk("ubf%d" % i, inherit=QT_t.alldeps()) for i in range(8)]
    us_t = Tk("usext", inherit=QT_t.alldeps())
    big_dead = sum([k.alldeps() for k in [KT_t, mown_t, mctx_t, msam_t] + V_t], [])
    diag = [at(BIG + i * 7936, [128, 31, 128], BF16) for i in range(2)]
    diag_t = [Tk("diag%d" % i, inherit=big_dead) for i in range(2)]
    sig = [at(BIG + 16384 + i * 2080, [128, 520], F32) for i in range(2)]
    sig_t = [Tk("sig%d" % i, inherit=big_dead) for i in range(2)]
    sig_rot = Rot([0, 1])
    sq = [at(BIG + 20736 + i * 2048, [128, 512], F32) for i in range(2)]
    sq_t = [Tk("sq%d" % i, inherit=big_dead) for i in range(2)]
    mean_s = at(BIG + 24832, [128, 512], F32); mean_t = Tk("mean", inherit=big_dead)
    var_s = at(BIG + 26880, [128, 512], F32); var_t = Tk("var", inherit=big_dead)
    rstd_s = at(BIG + 28928, [128, 512], F32); rstd_t = Tk("rstd", inherit=big_dead)
    t1 = [at(BIG + 30976 + i * 2048, [128, 512], F32) for i in range(2)]
    t1_t = [Tk("t1_%d" % i, inherit=big_dead) for i in range(2)]
    cst = at(BIG + 35072, [32, 1024], F32); cst_t = Tk("cst", inherit=big_dead)
    cst_s = at(BIG + 39168, [8, 1024], F32); csts_t = Tk("csts", inherit=big_dead)
    add_slot(BIG + 44032, big_dead)

    for blk in range(4):
        def f(sv, blk=blk):
            s3 = sv.rearrange("p (c n) -> p c n", c=16)
            return [(s3[:, :, 0:256], w_in_v[:, :, 3072 + 256 * blk: 3072 + 256 * blk + 256]),
                    (s3[:, :, 256:512], w_in_v[:, :, 4096 + 256 * blk: 4096 + 256 * blk + 256])]
        si = load_slot(f)
        sv = slots[si].ap().rearrange("p (c n) -> p c n", c=16)
        for ct in range(2):
            ch = 2 * blk + ct
            for (t0, n, src) in [(0, 512, "x"), (512, 512, "x"), (1024, 8, "x"), (0, 32, "c")]:
                ba, bg = mm_rot.next(), mm_rot.next()
                if src == "x":
                    rhs = lambda c: xT_b.ap()[:, c, t0:t0 + n]
                    rt = xT_t
                else:
                    rhs = lambda c: xc32_b.ap()[:, c, 0:32]
                    rt = xc32_t
                group(ba, banks[ba].ap()[:, :n],
                      [(sv[:, c, ct * 128:(ct + 1) * 128], rhs(c)) for c in range(16)], [slot_t[si], rt])
                group(bg, banks[bg].ap()[:, :n],
                      [(sv[:, c, 256 + ct * 128:256 + (ct + 1) * 128], rhs(c)) for c in range(16)],
                      [slot_t[si], rt])
                sgi = sig_rot.next()
                kb.emit("act", lambda: act.activation(out=sig[sgi].ap()[:, :n], in_=banks[bg].ap()[:, :n],
                                                      func=AF.Sigmoid),
                        reads=[bank_t[bg]], writes=[sig_t[sgi]])
                if src == "c":
                    kb.emit("dve", lambda: dve.tensor_tensor(out=u_bf.ap()[:, ch, 0:30], in0=banks[ba].ap()[:, 2:32],
                                                             in1=sig[sgi].ap()[:, 2:32], op=ALU.mult),
                            reads=[bank_t[ba], sig_t[sgi]], writes=[ubf_t[ch]])
                elif t0 < NP:
                    kb.emit("dve", lambda: dve.tensor_tensor(out=u_bf.ap()[:, ch, 30 + t0:30 + t0 + n],
                                                             in0=banks[ba].ap()[:, :n], in1=sig[sgi].ap()[:, :n],
                                                             op=ALU.mult),
                            reads=[bank_t[ba], sig_t[sgi]], writes=[ubf_t[ch]])
                    if t0 == 512:
                        kb.emit("dve", lambda: dve.tensor_tensor(out=u32.ap()[:, ch, 0:32],
                                                                 in0=banks[ba].ap()[:, 480:512],
                                                                 in1=sig[sgi].ap()[:, 480:512], op=ALU.mult),
                                reads=[bank_t[ba], sig_t[sgi]], writes=[u32_t])
                else:
                    kb.emit("dve", lambda: dve.tensor_tensor(out=u32.ap()[:, ch, 32:40], in0=banks[ba].ap()[:, :8],
                                                             in1=sig[sgi].ap()[:, :8], op=ALU.mult),
                            reads=[bank_t[ba], sig_t[sgi]], writes=[u32_t])
                    kb.emit("dve", lambda: dve.tensor_tensor(out=us_ext.ap()[:, ch, 30:38],
                                                             in0=banks[ba].ap()[:, :8],
                                                             in1=sig[sgi].ap()[:, :8], op=ALU.mult),
                            reads=[bank_t[ba], sig_t[sgi]], writes=[us_t])

    if STOP <= 5:
        return
    for ch in range(8):
        bi = 6 + ch // 4
        kb.emit("pe", lambda ch=ch, bi=bi: pe.transpose(banks[bi].ap()[0:32, (ch % 4) * 128:(ch % 4 + 1) * 128],
                                                        u32.ap()[:, ch, 0:32], ident_f.ap()),
                reads=[u32_t, ident_f_t], writes=[bank_t[bi]], signal=(ch % 4 == 3))
    for half in range(2):
        kb.emit("act", lambda half=half: act.copy(out=cst.ap()[0:32, half * 512:(half + 1) * 512],
                                                  in_=banks[6 + half].ap()[0:32, :]),
                reads=[bank_t[6 + half]], writes=[cst_t])
    kb.dma("sp", [(convp_d[0:30, :], cst.ap()[2:32, :])], reads=[cst_t], final=True)
    for ch in range(8):
        bi = 6 + ch // 4
        kb.emit("pe", lambda ch=ch, bi=bi: pe.transpose(banks[bi].ap()[0:8, (ch % 4) * 128:(ch % 4 + 1) * 128],
                                                        u32.ap()[:, ch, 32:40], ident_f.ap()),
                reads=[u32_t, ident_f_t], writes=[bank_t[bi]], signal=(ch % 4 == 3))
    for half in range(2):
        kb.emit("act", lambda half=half: act.copy(out=cst_s.ap()[0:8, half * 512:(half + 1) * 512],
                                                  in_=banks[6 + half].ap()[0:8, :]),
                reads=[bank_t[6 + half]], writes=[csts_t])
    kb.dma("sp", [(convs_d[22:30, :], cst_s.ap()[0:8, :])], reads=[csts_t], final=True)

    if STOP <= 6:
        return
    kb.dma("pool", [(us_ext.ap()[:, :, 0:30], scT_d.rearrange("(c p) j -> p c j", p=128))], writes=[us_t])
    cacc = at(RX, [128, 8, NT], F32); cacc_t = [Tk("cacc%d" % i, inherit=xT_t.alldeps()) for i in range(8)]
    for ch in range(8):
        dg = diag[ch % 2]
        dgt = diag_t[ch % 2]
        for j in range(31):
            kb.emit("dve", lambda j=j: dve.tensor_scalar(out=dg.ap()[:, j, :], in0=ident_b.ap(),
                                                         scalar1=pch.ap()[:, ch, j:j + 1], scalar2=None,
                                                         op0=ALU.mult),
                    reads=[ident_b_t, pch_t], writes=[dgt])
        for (t0, n) in TB:
            bi = mm_rot.next()
            if t0 < NP:
                prs = [(dg.ap()[:, j, :], u_bf.ap()[:, ch, t0 + j:t0 + j + n]) for j in range(31)]
                rr = [dgt, ubf_t[ch]]
            else:
                prs = [(dg.ap()[:, j, :], us_ext.ap()[:, ch, j:j + 8]) for j in range(31)]
                rr = [dgt, us_t]
            group(bi, banks[bi].ap()[:, :n], prs, rr)
            kb.emit("act", lambda bi=bi, t0=t0, n=n: act.activation(out=cacc.ap()[:, ch, t0:t0 + n],
                                                                    in_=banks[bi].ap()[:, :n], func=AF.Identity,
                                                                    bias=pch.ap()[:, ch, 31:32]),
                    reads=[bank_t[bi], pch_t], writes=[cacc_t[ch]])

    if STOP <= 7:
        return
    sq_rot = Rot([0, 1])
    t1_rot = Rot([0, 1])
    stat_sets = [(mean_s, var_s, rstd_s, mean_t, var_t, rstd_t)]
    m1 = at(BIG + 60416, [128, 512], F32); v1 = at(BIG + 62464, [128, 512], F32); r1 = at(BIG + 64512, [128, 512], F32)
    stat_sets.append((m1, v1, r1, Tk("mean1", inherit=big_dead), Tk("var1", inherit=big_dead),
                      Tk("rstd1", inherit=big_dead)))
    m2 = at(BIG + 66560, [128, 8], F32); v2 = at(BIG + 66592, [128, 8], F32); r2 = at(BIG + 66624, [128, 8], F32)
    stat_sets.append((m2, v2, r2, Tk("mean2", inherit=big_dead), Tk("var2", inherit=big_dead),
                      Tk("rstd2", inherit=big_dead)))
    stat_banks = [(6, 7), (0, 1), (2, 3)]

    def cln_stats(i):
        (t0, n) = TB[i]
        b1, b2_ = stat_banks[i]
        for ch in range(8):
            qi = sq_rot.next()
            kb.emit("act", lambda: act.activation(out=sq[qi].ap()[:, :n], in_=cacc.ap()[:, ch, t0:t0 + n],
                                                  func=AF.Square),
                    reads=[cacc_t[ch]], writes=[sq_t[qi]])
            kb.emit("pe", lambda: pe.matmul(banks[b1].ap()[:, :n], ones_f.ap(), cacc.ap()[:, ch, t0:t0 + n],
                                            start=(ch == 0), stop=(ch == 7)),
                    reads=[ones_t, cacc_t[ch]], writes=[bank_t[b1]])
            kb.emit("pe", lambda: pe.matmul(banks[b2_].ap()[:, :n], ones_f.ap(), sq[qi].ap()[:, :n],
                                            start=(ch == 0), stop=(ch == 7)),
                    reads=[ones_t, sq_t[qi]], writes=[bank_t[b2_]])

    def cln_chain(i):
        (t0, n) = TB[i]
        b1, b2_ = stat_banks[i]
        mean_x, var_x, rstd_x, mean_xt, var_xt, rstd_xt = stat_sets[i]
        kb.emit("act", lambda: act.mul(out=mean_x.ap()[:, :n], in_=banks[b1].ap()[:, :n], mul=1.0 / 1024.0),
                reads=[bank_t[b1]], writes=[mean_xt])
        kb.emit("dve", lambda: dve.tensor_tensor(out=var_x.ap()[:, :n], in0=mean_x.ap()[:, :n],
                                                 in1=mean_x.ap()[:, :n], op=ALU.mult),
                reads=[mean_xt], writes=[var_xt])
        kb.emit("dve", lambda: dve.scalar_tensor_tensor(out=var_x.ap()[:, :n], in0=banks[b2_].ap()[:, :n],
                                                        scalar=1.0 / 1024.0, in1=var_x.ap()[:, :n],
                                                        op0=ALU.mult, op1=ALU.subtract),
                reads=[bank_t[b2_], var_xt], writes=[var_xt])
        kb.emit("act", lambda: act.activation(out=rstd_x.ap()[:, :n], in_=var_x.ap()[:, :n], func=AF.Sqrt,
                                              bias=eps_c.ap()[:, 0:1]),
                reads=[var_xt, eps_t], writes=[rstd_xt])
        kb.emit("dve", lambda: dve.reciprocal(out=rstd_x.ap()[:, :n], in_=rstd_x.ap()[:, :n]),
                reads=[rstd_xt], writes=[rstd_xt])

    def cln_norm(i):
        (t0, n) = TB[i]
        mean_x, var_x, rstd_x, mean_xt, var_xt, rstd_xt = stat_sets[i]
        for ch in range(8):
            ti = t1_rot.next()
            kb.emit("dve", lambda: dve.tensor_tensor(out=t1[ti].ap()[:, :n], in0=cacc.ap()[:, ch, t0:t0 + n],
                                                     in1=mean_x.ap()[:, :n], op=ALU.subtract),
                    reads=[cacc_t[ch], mean_xt], writes=[t1_t[ti]])
            kb.emit("dve", lambda: dve.tensor_tensor(out=t1[ti].ap()[:, :n], in0=t1[ti].ap()[:, :n],
                                                     in1=rstd_x.ap()[:, :n], op=ALU.mult),
                    reads=[t1_t[ti], rstd_xt], writes=[t1_t[ti]])
            kb.emit("act", lambda: act.activation(out=mixed.ap()[:, 8 + ch, t0:t0 + n], in_=t1[ti].ap()[:, :n],
                                                  func=AF.Silu, bias=pch.ap()[:, ch, 33:34],
                                                  scale=pch.ap()[:, ch, 32:33]),
                    reads=[t1_t[ti], pch_t], writes=[mixed_t])

    cln_stats(0)
    cln_chain(0)
    cln_stats(1)
    cln_norm(0)
    cln_chain(1)
    cln_stats(2)
    cln_norm(1)
    cln_chain(2)
    cln_norm(2)

    if STOP <= 8:
        return
    big_dead2 = sum([k.alldeps() for k in diag_t + sig_t + sq_t + [mean_t, var_t, rstd_t, cst_t, csts_t] + t1_t
                     + [x for st in stat_sets[1:] for x in st[3:]]], [])
    big_dead2 = big_dead2 + drop_slots()
    add_slot(RQ, sum([k.alldeps() for k in ubf_t + [us_t]], []))
    acc = at(BIG, [128, 9, D], F32)
    acc_t = [Tk("acc%d" % i, inherit=big_dead + big_dead2) for i in range(9)]
    for ti, (t0, rows) in enumerate(TT):
        kb.dma("sp", [(acc.ap()[:rows, ti, :], xres_d[t0:t0 + rows, :])], writes=[acc_t[ti]])
    rx_dead2 = sum([k.alldeps() for k in cacc_t], [])
    hidden = at(RX, [128, 8, NT], BF16); hidden_t = Tk("hidden", inherit=rx_dead2)
    gb = at(RX + 16512, [128, 2, D], F32); gb_t = Tk("gb", inherit=rx_dead2)
    kb.dma("sp", [(gb.ap(), lngb_d[:, 0:2, :])], writes=[gb_t])
    hb = [at(TMP + i * 4096, [128, D], BF16) for i in range(2)]
    tmp_dead = sum([k.alldeps() for k in ao_t + EP_t + [Es_t, ao_s_t]], [])
    hb_t = [Tk("hb%d" % i, inherit=tmp_dead) for i in range(2)]

    class LNPipe:
        NPH = 5
        OFF = [0, 1, 2, 3, 3]

        def __init__(self, tail, gmul_eng):
            self.q = []
            self.step = 0
            self.tail = tail
            self.gmul_eng = gmul_eng

        def push(self, ti, rows, t0):
            self.q.append((ti, rows, t0))
            self._step()

        def flush(self):
            for _ in range(self.OFF[-1]):
                self._step()

        def _step(self):
            sidx = self.step
            self.step += 1
            for p in sorted(range(self.NPH), key=lambda p: (-self.OFF[p], p)):
                idx = sidx - self.OFF[p]
                if 0 <= idx < len(self.q):
                    self._phase(p, *self.q[idx])

        def _phase(self, p, ti, rows, t0):
            k = ti % 4
            a = acc.ap()[:rows, ti, :]
            if p == 0:
                for q in range(4):
                    kb.emit("dve", lambda q=q: dve.bn_stats(out=st6[k].ap()[:rows, q, :],
                                                            in_=acc.ap()[:rows, ti, q * 512:(q + 1) * 512]),
                            reads=[acc_t[ti]], writes=[st6_t[k]])
                kb.emit("dve", lambda: dve.bn_aggr(out=mv[k].ap()[:rows, :],
                                                   in_=st6[k].ap()[:rows].rearrange("p q s -> p (q s)")),
                        reads=[st6_t[k]], writes=[mv_t[k]])
                kb.emit("act", lambda: act.activation(out=sd[k].ap()[:rows, 0:1], in_=mv[k].ap()[:rows, 1:2],
                                                      func=AF.Sqrt, bias=eps_c.ap()[:rows, 0:1]),
                        reads=[mv_t[k], eps_t], writes=[sd_t[k]])
                kb.emit("dve", lambda: dve.reciprocal(out=sd[k].ap()[:rows, 1:2], in_=sd[k].ap()[:rows, 0:1]),
                        reads=[sd_t[k]], writes=[sd_t[k]])
                kb.emit("dve", lambda: dve.tensor_scalar(out=nm[k].ap()[:rows, 0:1], in0=mv[k].ap()[:rows, 0:1],
                                                         scalar1=sd[k].ap()[:rows, 1:2], scalar2=-1.0,
                                                         op0=ALU.mult, op1=ALU.mult),
                        reads=[mv_t[k], sd_t[k]], writes=[nm_t[k]])
            elif p == 1:
                kb.emit("act", lambda: act.activation(out=a, in_=a, func=AF.Identity,
                                                      scale=sd[k].ap()[:rows, 1:2], bias=nm[k].ap()[:rows, 0:1]),
                        reads=[acc_t[ti], nm_t[k], sd_t[k]], writes=[acc_t[ti]])
            elif p == 2:
                ge = pool if self.gmul_eng == "pool" else dve
                kb.emit(self.gmul_eng, lambda: ge.tensor_tensor(out=a, in0=a, in1=gb.ap()[:rows, 0, :],
                                                                op=ALU.mult),
                        reads=[acc_t[ti], gb_t], writes=[acc_t[ti]])
            elif p == 3:
                en = "pool" if ti % 2 == 0 else "dve"
                be = pool if en == "pool" else dve
                kb.emit(en, lambda: be.tensor_tensor(out=a, in0=a, in1=gb.ap()[:rows, 1, :], op=ALU.add),
                        reads=[acc_t[ti], gb_t], writes=[acc_t[ti]])
            else:
                self.tail(ti, rows, t0)

    hT = at(RM, [128, 16, NT], BF16)
    hT_t = Tk("hT", inherit=mixed_t.alldeps())

    class LN1Pipe:
        OFF = [0, 1, 2, 3]

        def __init__(self):
            self.q = []
            self.step = 0

        def push(self, ti, rows, t0):
            self.q.append((ti, rows, t0))
            self._step()

        def flush(self):
            for _ in range(self.OFF[-1]):
                self._step()

        def _step(self):
            sidx = self.step
            self.step += 1
            for p in sorted(range(len(self.OFF)), key=lambda p: (-self.OFF[p], p)):
                idx = sidx - self.OFF[p]
                if 0 <= idx < len(self.q):
                    self._phase(p, *self.q[idx])

        def _phase(self, p, ti, rows, t0):
            k = ti % 4
            k2 = ti % 2
            if p == 0:
                for q in range(4):
                    kb.emit("dve", lambda q=q: dve.bn_stats(out=st6[k].ap()[:rows, q, :],
                                                            in_=acc.ap()[:rows, ti, q * 512:(q + 1) * 512]),
                            reads=[acc_t[ti]], writes=[st6_t[k]])
                kb.emit("dve", lambda: dve.bn_aggr(out=mv[k].ap()[:rows, :],
                                                   in_=st6[k].ap()[:rows].rearrange("p q s -> p (q s)")),
                        reads=[st6_t[k]], writes=[mv_t[k]])
            elif p == 1:
                kb.emit("act", lambda: act.activation(out=sd9.ap()[:rows, ti, 0:1], in_=mv[k].ap()[:rows, 1:2],
                                                      func=AF.Sqrt, bias=eps_c.ap()[:rows, 0:1]),
                        reads=[mv_t[k], eps_t], writes=[sd9_t[ti]])
                kb.emit("dve", lambda: dve.reciprocal(out=sd9.ap()[:rows, ti, 1:2], in_=sd9.ap()[:rows, ti, 0:1]),
                        reads=[sd9_t[ti]], writes=[sd9_t[ti]])
                kb.emit("dve", lambda: dve.tensor_scalar(out=nm9.ap()[:rows, ti:ti + 1], in0=mv[k].ap()[:rows, 0:1],
                                                         scalar1=sd9.ap()[:rows, ti, 1:2], scalar2=-1.0,
                                                         op0=ALU.mult, op1=ALU.mult),
                        reads=[mv_t[k], sd9_t[ti]], writes=[nm9_t[ti]])
            elif p == 2:
                kb.emit("act", lambda: act.activation(out=hb[k2].ap()[:rows, :], in_=acc.ap()[:rows, ti, :],
                                                      func=AF.Identity, scale=sd9.ap()[:rows, ti, 1:2],
                                                      bias=nm9.ap()[:rows, ti:ti + 1]),
                        reads=[acc_t[ti], sd9_t[ti], nm9_t[ti]], writes=[hb_t[k2]])
            else:
                for half in range(2):
                    bi = 6 + half
                    trv = banks[bi].ap().bitcast(BF16)
                    for q in range(8):
                        cidx = half * 8 + q
                        kb.emit("pe", lambda q=q, cidx=cidx, trv=trv: pe.transpose(
                            trv[:, q * 128:q * 128 + rows], hb[k2].ap()[:rows, cidx * 128:(cidx + 1) * 128],
                            ident_b.ap()[:rows, :rows]),
                            reads=[hb_t[k2], ident_b_t], writes=[bank_t[bi]], signal=(q == 7))
                    for d in mixed_t.alldeps():
                        hT_t._addr(d)
                    for q in range(8):
                        cidx = half * 8 + q
                        src = trv[:, q * 128:q * 128 + rows]
                        dst = hT.ap()[:, cidx, t0:t0 + rows]
                        gcol = ln1T.ap()[:, 0, cidx:cidx + 1]
                        bcol = ln1T.ap()[:, 1, cidx:cidx + 1]
                        if half == 0:
                            kb.emit("act", lambda src=src, dst=dst, gcol=gcol, bcol=bcol: act.activation(
                                out=dst, in_=src, func=AF.Identity, scale=gcol, bias=bcol),
                                reads=[bank_t[bi], ln1T_t], writes=[hT_t])
                        else:
                            kb.emit("dve", lambda src=src, dst=dst, gcol=gcol, bcol=bcol: dve.tensor_scalar(
                                out=dst, in0=src, scalar1=gcol, scalar2=bcol, op0=ALU.mult, op1=ALU.add),
                                reads=[bank_t[bi], ln1T_t], writes=[hT_t])

    def ln1_deferred(ti, rows):
        a = acc.ap()[:rows, ti, :]
        kb.emit("act", lambda: act.activation(out=a, in_=a, func=AF.Identity, scale=sd9.ap()[:rows, ti, 1:2],
                                              bias=nm9.ap()[:rows, ti:ti + 1]),
                reads=[acc_t[ti], sd9_t[ti], nm9_t[ti]], writes=[acc_t[ti]])
        kb.emit("dve", lambda: dve.tensor_tensor(out=a, in0=a, in1=gb.ap()[:rows, 0, :], op=ALU.mult),
                reads=[acc_t[ti], gb_t], writes=[acc_t[ti]])
        kb.emit("dve", lambda: dve.tensor_tensor(out=a, in0=a, in1=gb.ap()[:rows, 1, :], op=ALU.add),
                reads=[acc_t[ti], gb_t], writes=[acc_t[ti]])

    ln1 = LN1Pipe()
    for cb in range(4):
        si, sv = load_w512(w_out_v, cb * 512)
        for ti, (t0, rows) in enumerate(TT):
            bi = mm_rot.next()
            group(bi, banks[bi].ap()[:rows, :512],
                  [(mixed.ap()[:, e, t0:t0 + rows], sv[:, e, :]) for e in range(16)], [slot_t[si], mixed_t])
            kb.emit("dve", lambda bi=bi, ti=ti, rows=rows, cb=cb: dve.scalar_tensor_tensor(
                out=acc.ap()[:rows, ti, cb * 512:(cb + 1) * 512], in0=acc.ap()[:rows, ti, cb * 512:(cb + 1) * 512],
                scalar=ALPHA, in1=banks[bi].ap()[:rows, :512], op0=ALU.mult, op1=ALU.add),
                reads=[bank_t[bi], acc_t[ti]], writes=[acc_t[ti]])
            if cb == 3:
                ln1.push(ti, rows, t0)
    ln1.flush()

    def ln2_tail(ti, rows, t0):
        kb.dma("sp", [(y_d[t0:t0 + rows, :], acc.ap()[:rows, ti, :])], reads=[acc_t[ti]], final=True)

    ln2 = LNPipe(ln2_tail, "pool")

    if STOP <= 9:
        return
    rt = [at(TMP + 8192 + i * 2048, [128, 512], F32) for i in range(2)]
    rt_t = [Tk("rt%d" % i, inherit=tmp_dead) for i in range(2)]
    rt_rot = Rot([0, 1])
    for b in range(8):
        cache_copies_part(b)
        if b == 1:
            kb.dma("sp", [(gb.ap(), lngb_d[:, 2:4, :])], writes=[gb_t])
        for ub in range(2):
            si, sv = load_w512(w_up_v, b * 1024 + ub * 512)
            for ft in range(4):
                fc = ub * 4 + ft
                for (t0, n) in TB:
                    bi = mm_rot.next()
                    group(bi, banks[bi].ap()[:, :n],
                          [(sv[:, c, ft * 128:(ft + 1) * 128], hT.ap()[:, c, t0:t0 + n]) for c in range(16)],
                          [slot_t[si], hT_t])
                    ri = rt_rot.next()
                    kb.emit("act", lambda bi=bi, ri=ri, n=n: act.activation(out=rt[ri].ap()[:, :n],
                                                                            in_=banks[bi].ap()[:, :n], func=AF.Relu),
                            reads=[bank_t[bi]], writes=[rt_t[ri]])
                    kb.emit("dve", lambda ri=ri, fc=fc, t0=t0, n=n: dve.tensor_tensor(
                        out=hidden.ap()[:, fc, t0:t0 + n], in0=rt[ri].ap()[:, :n], in1=rt[ri].ap()[:, :n],
                        op=ALU.mult),
                        reads=[rt_t[ri]], writes=[hidden_t])
                if b == 0:
                    ln1_deferred(fc, TT[fc][1])
                    if fc == 7:
                        ln1_deferred(8, TT[8][1])
        def load_down(dh, b=b):
            def f(sv):
                s3 = sv.rearrange("p (c n) -> p c n", c=8)
                return [(s3[:, 0:4, :], w_down_v[:, b * 8:b * 8 + 4, dh * 1024:(dh + 1) * 1024]),
                        (s3[:, 4:8, :], w_down_v[:, b * 8 + 4:b * 8 + 8, dh * 1024:(dh + 1) * 1024])]
            si = load_slot(f)
            return si, slots[si].ap().rearrange("p (c n) -> p c n", c=8)

        def down_group(si, sv, dh, cg, ti, t0, rows, b=b):
            bi = mm_rot.next()
            group(bi, banks[bi].ap()[:rows, :512],
                  [(hidden.ap()[:, fc, t0:t0 + rows], sv[:, fc, cg * 512:(cg + 1) * 512]) for fc in range(8)],
                  [slot_t[si], hidden_t])
            c0 = dh * 1024 + cg * 512
            if b == 0:
                kb.emit("dve", lambda: dve.scalar_tensor_tensor(
                    out=acc.ap()[:rows, ti, c0:c0 + 512], in0=acc.ap()[:rows, ti, c0:c0 + 512],
                    scalar=ALPHA, in1=banks[bi].ap()[:rows, :512], op0=ALU.mult, op1=ALU.add),
                    reads=[bank_t[bi], acc_t[ti]], writes=[acc_t[ti]])
            else:
                kb.emit("dve", lambda: dve.tensor_tensor(
                    out=acc.ap()[:rows, ti, c0:c0 + 512], in0=banks[bi].ap()[:rows, :512],
                    in1=acc.ap()[:rows, ti, c0:c0 + 512], op=ALU.add),
                    reads=[bank_t[bi], acc_t[ti]], writes=[acc_t[ti]])

        if b < 7:
            for dh in range(2):
                si, sv = load_down(dh)
                for ti, (t0, rows) in enumerate(TT):
                    for cg in range(2):
                        down_group(si, sv, dh, cg, ti, t0, rows)
        else:
            dl = [load_down(0), load_down(1)]
            for ti, (t0, rows) in enumerate(TT):
                for dh in range(2):
                    for cg in range(2):
                        down_group(dl[dh][0], dl[dh][1], dh, cg, ti, t0, rows)
                ln2.push(ti, rows, t0)

    ln2.flush()


def _mult(delta):
    delta = np.asarray(delta)
    m = ((delta >= 0) & (delta <= 128)).astype(np.float32)
    m += ((delta >= 0) & (delta <= 512) & (delta % 4 == 0)).astype(np.float32)
    m += ((delta >= 0) & (delta <= 2048) & (delta % 16 == 0)).astype(np.float32)
    return m


def _masks():
    p = np.arange(128)[:, None]
    c = np.arange(128)[None, :]
    mown = np.zeros((128, 8, 128), np.float32)
    for dl in range(8):
        mown[:, 7 - dl, :] = _mult(dl * 128 + c - p)
    mctx = np.zeros((128, 15, 128), np.float32)
    for dl in range(1, 16):
        mctx[:, 15 - dl, :] = _mult(dl * 128 + c - p)
    msam = np.zeros((128, 17, 8), np.float32)
    q = np.arange(8)[None, :]
    for j in range(16):
        msam[:, j, :] = _mult(2048 + q - (j * 128 + p))
    msam[:, 16, :] = _mult(q - p) * (p < 8)
    return mown.reshape(128, -1), mctx.reshape(128, -1), msam.reshape(128, -1)


def kernel(x_prompt, x_sample, cache_k, cache_v, state_conv, w_in, w_dw, b_dw, ln_conv_g, ln_conv_b,
           w_out, ln1_g, ln1_b, w_up, w_down, ln2_g, ln2_b):
    f = lambda a: np.ascontiguousarray(np.asarray(a, dtype=np.float32))
    x_prompt, x_sample = f(x_prompt), f(x_sample)
    cache_k, cache_v, state_conv = f(cache_k), f(cache_v), f(state_conv)
    w_in0, w_out0, w_up0, w_down0 = f(w_in)[0], f(w_out)[0], f(w_up)[0], f(w_down)[0]
    pc = np.concatenate([f(w_dw)[0].T, f(b_dw)[0][:, None], f(ln_conv_g)[0][:, None], f(ln_conv_b)[0][:, None]],
                        axis=1)
    pch = np.ascontiguousarray(pc.reshape(8, 128, 34).transpose(1, 0, 2))
    lngb = np.stack([f(ln1_g)[0], f(ln1_b)[0], f(ln2_g)[0], f(ln2_b)[0]])
    lngb = np.ascontiguousarray(np.broadcast_to(lngb[None], (128, 4, D)))
    mown, mctx, msam = _masks()
    ident = np.eye(128, dtype=np.float32)
    ln1T = np.ascontiguousarray(np.stack([f(ln1_g)[0].reshape(16, 128).T, f(ln1_b)[0].reshape(16, 128).T], axis=1))

    in_maps = []
    for c in range(8):
        b, half = c // 2, c % 2
        own = x_prompt[b, half * NP:(half + 1) * NP]
        xs = x_sample[c]
        xres = np.concatenate([own, xs], axis=0)
        xT = np.ascontiguousarray(xres.T)
        if half == 1:
            ctx = x_prompt[b, 0:NP]
            xcT = np.ascontiguousarray(ctx.T)
            mc = mctx
        else:
            xcT = np.zeros((D, NP), np.float32)
            mc = np.zeros_like(mctx)
        ck = cache_k[0, c].reshape(2048, 1024)
        cv = cache_v[0, c].reshape(2048, 1024)
        in_maps.append(dict(
            xT=xT, xcT=xcT, xc32=np.ascontiguousarray(xcT[:, NP - 32:NP]), xres=xres,
            w_in=w_in0, w_out=w_out0, w_up=w_up0, w_down=w_down0, pch=pch, lngb=lngb,
            mown=mown, mctx=mc, msam=msam, ident=ident, ln1T=ln1T,
            ckT=np.ascontiguousarray(ck.T), ck=ck, cv=cv,
            scT=np.ascontiguousarray(state_conv[0, c].T), sc=state_conv[0, c],
        ))
    nc = build_program()
    res = run_bass_kernel_spmd(nc, in_maps, core_ids=list(range(8)))
    R = res.results
    y_prompt = np.zeros((4, 2048, D), np.float32)
    y_sample = np.zeros((8, 8, D), np.float32)
    kp = np.zeros((1, 4, 2048, 8, 128), np.float32)
    vp = np.zeros((1, 4, 2048, 8, 128), np.float32)
    cp = np.zeros((1, 4, 30, 1024), np.float32)
    ksw = np.zeros((1, 8, 2048, 8, 128), np.float32)
    vsw = np.zeros((1, 8, 2048, 8, 128), np.float32)
    cs = np.zeros((1, 8, 30, 1024), np.float32)
    for c in range(8):
        b, half = c // 2, c % 2
        r = R[c]
        y = np.asarray(r["y"])
        y_prompt[b, half * NP:(half + 1) * NP] = y[:NP]
        y_sample[c] = y[NP:NT]
        kp[0, b, half * NP:(half + 1) * NP] = np.asarray(r["kout"]).reshape(NP, 8, 128)
        vp[0, b, half * NP:(half + 1) * NP] = np.asarray(r["vout"]).reshape(NP, 8, 128)
        if half == 1:
            cp[0, b] = np.asarray(r["convp"])
        ksw[0, c] = np.asarray(r["ks"]).reshape(2048, 8, 128)
        vsw[0, c] = np.asarray(r["vs"]).reshape(2048, 8, 128)
        cs[0, c] = np.asarray(r["convs"])
    return (y_prompt, y_sample, kp, vp, cp, ksw, vsw, cs)
```

```python
import numpy as np
import concourse.bass as bass
import concourse.mybir as mybir
from concourse.bass_utils import run_bass_kernel_spmd

F32 = mybir.dt.float32
BF16 = mybir.dt.bfloat16
AF = mybir.ActivationFunctionType
ALU = mybir.AluOpType

D = 2048
NP = 1024
NS = 8
NT = NP + NS
DIN = 5120
DFF = 8192
ALPHA = float(2.0 ** 0.25)
EPS = 1e-5
SCALE = float(128.0 ** -0.5)
TB = [(0, 512), (512, 512), (1024, 8)]
TT = [(i * 128, 128) for i in range(8)] + [(1024, 8)]
NSLOT = 2
STOP = 99
GLIMIT = 10 ** 9


class _Stop(Exception):
    pass
SLOT_BYTES = 16384


class Tk:
    def __init__(self, name, inherit=None, strict=False):
        self.name = name
        self.strict = strict
        self.w = None
        self.r = {}
        self.dsem = None
        self.dcnt = 0
        if inherit:
            for d in inherit:
                self._addr(d)

    def _addr(self, d):
        k = d[0]
        if k not in self.r or self.r[k][2] < d[2]:
            self.r[k] = d

    def alldeps(self):
        out = list(self.r.values())
        if self.w is not None:
            out.append(self.w)
        return out


class KB:
    def __init__(self, nc):
        self.nc = nc
        self.semkey = 0
        self.eng = {}
        for name, h in (("pe", nc.tensor), ("act", nc.scalar), ("dve", nc.vector),
                        ("pool", nc.gpsimd), ("sp", nc.sync)):
            self.eng[name] = dict(h=h, sem=nc.alloc_semaphore(name="sem_" + name), cnt=0,
                                  waited={}, key=self._newkey())
        self.final = []
        self.nsem = 5

    def _newkey(self):
        self.semkey += 1
        return self.semkey

    def _collect(self, engname, reads, writes):
        deps = {}

        def add(d, skip_same):
            if d is None:
                return
            if d[3] == engname and (skip_same or engname == "pe"):
                return
            k = d[0]
            if k not in deps or deps[k][2] < d[2]:
                deps[k] = d

        for t in reads:
            add(t.w, False)
        for t in writes:
            add(t.w, not t.strict)
            for d in t.r.values():
                add(d, not t.strict)
        return deps

    def _wait(self, engname, deps):
        e = self.eng[engname]
        for k, d in deps.items():
            if e["waited"].get(k, 0) < d[2]:
                e["h"].wait_ge(d[1], d[2])
                e["waited"][k] = d[2]

    def emit(self, engname, fn, reads=(), writes=(), signal=True):
        e = self.eng[engname]
        self._wait(engname, self._collect(engname, reads, writes))
        inst = fn()
        if signal:
            e["cnt"] += 1
            inst.then_inc(e["sem"], 1)
            val = e["cnt"]
        else:
            val = e["cnt"] + 1
        d = (e["key"], e["sem"], val, engname)
        for t in reads:
            t._addr(d)
        for t in writes:
            t.w = d
            t.r = {}
        return inst

    def dma(self, queue, pairs, reads=(), writes=(), owner=None, final=False, after=()):
        e = self.eng[queue]
        deps = self._collect(queue + "_dma", reads, writes)
        for d in after:
            if d is not None and (d[0] not in deps or deps[d[0]][2] < d[2]):
                deps[d[0]] = d
        self._wait(queue, deps)
        if owner is None:
            owner = writes[0] if writes else reads[0]
        if owner.dsem is None:
            owner.dsem = self.nc.alloc_semaphore(name="dsem_%s_%d" % (owner.name, self.nsem))
            owner.dkey = self._newkey()
            self.nsem += 1
        for (o, i) in pairs:
            e["h"].dma_start(out=o, in_=i).then_inc(owner.dsem, 16)
            owner.dcnt += 1
        d = (owner.dkey, owner.dsem, 16 * owner.dcnt, "dma")
        for t in reads:
            t._addr(d)
        for t in writes:
            t.w = d
            t.r = {}
        if final:
            self.final.append(d)

    def finish(self):
        deps = {}
        for d in self.final:
            if d[0] not in deps or deps[d[0]][2] < d[2]:
                deps[d[0]] = d
        for k, d in deps.items():
            self.eng["sp"]["h"].wait_ge(d[1], d[2])


class Rot:
    def __init__(self, items):
        self.items = items
        self.i = 0

    def next(self):
        x = self.items[self.i % len(self.items)]
        self.i += 1
        return x


def build_program():
    nc = bass.Bass("TRN2", target_bir_lowering=False)
    kb = KB(nc)
    try:
        _build(nc, kb)
    except _Stop:
        pass
    kb.finish()
    return nc


def _build(nc, kb):

    def din(name, shape):
        return nc.dram_tensor(name, list(shape), F32, kind="ExternalInput").ap()

    def dout(name, shape):
        return nc.dram_tensor(name, list(shape), F32, kind="ExternalOutput").ap()

    xT_d = din("xT", [D, NT])
    xcT_d = din("xcT", [D, NP])
    xc32_d = din("xc32", [D, 32])
    xres_d = din("xres", [NT, D])
    w_in_d = din("w_in", [D, DIN])
    w_out_d = din("w_out", [D, D])
    w_up_d = din("w_up", [D, DFF])
    w_down_d = din("w_down", [DFF, D])
    pch_d = din("pch", [128, 8, 34])
    lngb_d = din("lngb", [128, 4, D])
    mown_d = din("mown", [128, 8 * 128])
    mctx_d = din("mctx", [128, 15 * 128])
    msam_d = din("msam", [128, 17 * 8])
    ident_d = din("ident", [128, 128])
    ln1T_d = din("ln1T", [128, 2, 16])
    ckT_d = din("ckT", [1024, 2048])
    ck_d = din("ck", [2048, 1024])
    cv_d = din("cv", [2048, 1024])
    scT_d = din("scT", [1024, 30])
    sc_d = din("sc", [30, 1024])

    y_d = dout("y", [NT, D])
    kout_d = dout("kout", [NP, 1024])
    vout_d = dout("vout", [NP, 1024])
    convp_d = dout("convp", [30, 1024])
    ks_d = dout("ks", [2048, 1024])
    vs_d = dout("vs", [2048, 1024])
    convs_d = dout("convs", [30, 1024])

    base = (nc.sbuf_base + 31) // 32 * 32
    cur = [base]
    cnt = [0]

    def region(nbytes):
        o = cur[0]
        cur[0] += (nbytes + 31) // 32 * 32
        return o

    def at(off, shape, dt):
        cnt[0] += 1
        return nc.alloc_sbuf_tensor_at("sb%d" % cnt[0], list(shape), dt, offset=off)

    W_OFF = region(NSLOT * SLOT_BYTES)
    BIG = region(73728)
    RX = region(33280)
    RQ = region(17664)
    RM = region(33024)
    TMP = region(13312)
    SM = region(8192)
    assert cur[0] <= nc.sbuf_top, (cur[0], nc.sbuf_top)

    slots = [at(W_OFF + i * SLOT_BYTES, [128, 8192], BF16) for i in range(NSLOT)]
    slot_t = [Tk("slot%d" % i) for i in range(NSLOT)]
    slot_rot = Rot(list(range(NSLOT)))

    so = [SM]

    def small(shape, dt, nbytes):
        t = at(so[0], shape, dt)
        so[0] += (nbytes + 31) // 32 * 32
        return t

    ident_b = small([128, 128], BF16, 256); ident_b_t = Tk("identb")
    ident_f = small([128, 128], F32, 512); ident_f_t = Tk("identf")
    ones_f = small([128, 128], F32, 512); ones_t = Tk("ones")
    pch = small([128, 8, 34], F32, 1088); pch_t = Tk("pch")
    eps_c = small([128, 1], F32, 4); eps_t = Tk("eps")
    xc32_b = small([128, 16, 32], BF16, 1024); xc32_t = Tk("xc32")
    u32 = small([128, 8, 40], F32, 1280); u32_t = Tk("u32")
    KsT = small([128, 8, 8], BF16, 128); KsT_t = Tk("KsT")
    Vs_new = small([8, 8, 129], BF16, 2064 + 16); Vs_t = Tk("Vsnew")
    st6 = [small([128, 4, 6], F32, 96) for _ in range(4)]
    st6_t = [Tk("st6_%d" % i, strict=True) for i in range(4)]
    mv = [small([128, 2], F32, 8) for _ in range(4)]
    mv_t = [Tk("mv%d" % i, strict=True) for i in range(4)]
    sd = [small([128, 2], F32, 8) for _ in range(4)]
    sd_t = [Tk("sd%d" % i, strict=True) for i in range(4)]
    nm = [small([128, 1], F32, 4) for _ in range(4)]
    nm_t = [Tk("nm%d" % i, strict=True) for i in range(4)]
    ln1T = small([128, 2, 16], F32, 128); ln1T_t = Tk("ln1T")
    sd9 = small([128, 9, 2], F32, 72)
    nm9 = small([128, 9], F32, 36)
    sd9_t = [Tk("sd9_%d" % i, strict=True) for i in range(9)]
    nm9_t = [Tk("nm9_%d" % i, strict=True) for i in range(9)]
    rc = [small([128, 1], F32, 4) for _ in range(4)]
    rc_t = [Tk("rc%d" % i, strict=True) for i in range(4)]
    assert so[0] <= SM + 8192, so[0]

    banks = [nc.alloc_psum_tensor("psb%d" % i, [128, 512], F32) for i in range(8)]
    bank_t = [Tk("bank%d" % i) for i in range(8)]

    pe, act, dve, pool = nc.tensor, nc.scalar, nc.vector, nc.gpsimd

    kb.dma("sp", [(ident_f.ap(), ident_d)], writes=[ident_f_t])
    kb.dma("pool", [(ident_b.ap(), ident_d)], writes=[ident_b_t])
    kb.dma("sp", [(pch.ap(), pch_d)], writes=[pch_t])
    kb.dma("sp", [(ln1T.ap(), ln1T_d)], writes=[ln1T_t])
    kb.emit("dve", lambda: dve.memset(ones_f.ap(), 1.0), writes=[ones_t])
    kb.emit("dve", lambda: dve.memset(eps_c.ap(), EPS), writes=[eps_t])
    kb.emit("dve", lambda: dve.memset(Vs_new.ap()[:, :, 128:129], 1.0), writes=[Vs_t])

    copy_t = Tk("dramcopy")

    def cache_copies_part(i):
        prs = [(dst[i * 255:(i + 1) * 255, :], src[8 + i * 255: 8 + (i + 1) * 255, :])
               for (dst, src) in ((ks_d, ck_d), (vs_d, cv_d))]
        if i == 0:
            prs.append((convs_d[0:22, :], sc_d[8:30, :]))
        kb.dma("sp", prs, owner=copy_t, final=True)

    KT = at(BIG, [128, 8, 2048], BF16); KT_t = Tk("KT")
    Vall = at(BIG + 32768, [128, 16, 8, 129], BF16)
    V_t = [Tk("V%d" % i) for i in range(16)]
    mown = at(BIG + 65792, [128, 8 * 128], BF16); mown_t = Tk("mown")
    mctx = at(BIG + 67840, [128, 15 * 128], BF16); mctx_t = Tk("mctx")
    msam = at(BIG + 71680, [128, 17 * 8], BF16); msam_t = Tk("msam")
    kb.emit("dve", lambda: dve.memset(Vall.ap()[:, :, :, 128:129], 1.0), writes=V_t)

    xcT_b = at(RM, [128, 16, NP], BF16); xcT_t = [Tk("xcT0"), Tk("xcT1")]
    xcv = xcT_d.rearrange("(c p) n -> p c n", p=128)

    w_in_v = w_in_d.rearrange("(c p) n -> p c n", p=128)
    w_out_v = w_out_d.rearrange("(c p) n -> p c n", p=128)
    w_up_v = w_up_d.rearrange("(c p) n -> p c n", p=128)
    w_down_v = w_down_d.rearrange("(c p) n -> p c n", p=128)

    def load_slot(pairs_fn):
        si = slot_rot.next()
        sv = slots[si].ap()
        kb.dma("pool", pairs_fn(sv), writes=[slot_t[si]])
        return si

    def load_w512(view, col0):
        def f(sv):
            s3 = sv.rearrange("p (c n) -> p c n", c=16)
            return [(s3[:, 0:8, :], view[:, 0:8, col0:col0 + 512]),
                    (s3[:, 8:16, :], view[:, 8:16, col0:col0 + 512])]
        si = load_slot(f)
        return si, slots[si].ap().rearrange("p (c n) -> p c n", c=16)

    mm_rot = Rot([0, 1, 2, 3, 4, 5])

    def add_slot(off, inherit):
        slots.append(at(off, [128, 8192], BF16))
        slot_t.append(Tk("slotx%d" % len(slots), inherit=inherit))
        slot_rot.items = list(range(len(slots)))

    def drop_slots():
        dead = sum([t.alldeps() for t in slot_t[NSLOT:]], [])
        del slots[NSLOT:]
        del slot_t[NSLOT:]
        slot_rot.items = list(range(NSLOT))
        return dead


    gcount = [0]

    def group(bank_i, out_ap, pairs, extra_reads):
        gcount[0] += 1
        if gcount[0] > GLIMIT:
            raise _Stop()
        n = len(pairs)
        for i, (l, r) in enumerate(pairs):
            kb.emit("pe", lambda l=l, r=r, i=i: pe.matmul(out_ap, l, r, start=(i == 0), stop=(i == n - 1)),
                    reads=extra_reads, writes=[bank_t[bank_i]], signal=(i == n - 1))

    if STOP <= 0:
        return
    pre_si = slot_rot.next()
    pre_sv = slots[pre_si].ap().rearrange("p (c n) -> p c n", c=16)
    pre_ct_t = [Tk("pre_ct%d" % i) for i in range(4)]

    def pre_piece(ct):
        kb.dma("pool", [(pre_sv[:, :, ct * 128:(ct + 1) * 128], w_in_v[:, :, 1024 + ct * 128:1024 + (ct + 1) * 128])],
               writes=[pre_ct_t[ct]])

    def xc_half(hf):
        kb.dma("pool", [(xcT_b.ap()[:, 0:8, hf * 512:(hf + 1) * 512], xcv[:, 0:8, hf * 512:(hf + 1) * 512]),
                        (xcT_b.ap()[:, 8:16, hf * 512:(hf + 1) * 512], xcv[:, 8:16, hf * 512:(hf + 1) * 512])],
               writes=[xcT_t[hf]])

    pre_piece(0)
    xc_half(0)
    pre_piece(1)
    xc_half(1)
    pre_piece(2)
    pre_piece(3)
    pre_blk = (pre_si, pre_sv)
    kb.dma("pool", [(mown.ap(), mown_d)], writes=[mown_t])
    kb.dma("pool", [(mctx.ap(), mctx_d)], writes=[mctx_t])
    kb.dma("pool", [(msam.ap(), msam_d)], writes=[msam_t])
    kb.dma("pool", [(xc32_b.ap(), xc32_d.rearrange("(c p) n -> p c n", p=128))], writes=[xc32_t])
    xT_b = at(RX, [128, 16, NT], BF16); xT_t = Tk("xT")
    xv = xT_d.rearrange("(c p) n -> p c n", p=128)

    def load_xT():
        kb.dma("pool", [(xT_b.ap()[:, 0:8, :], xv[:, 0:8, :]), (xT_b.ap()[:, 8:16, :], xv[:, 8:16, :])],
               writes=[xT_t])
    xT_loaded = [False]

    for (kind, col0) in (("k", 1024), ("k", 1536), ("v", 2048), ("v", 2560)):
        si, sv = pre_blk if col0 == 1024 else load_w512(w_in_v, col0)
        if col0 == 1536:
            load_xT()
        if kind == "k":
            for ct in range(4):
                head = (col0 - 1024) // 128 + ct
                for (t0, n) in ((0, 512), (512, 512)):
                    bi = mm_rot.next()
                    wt = pre_ct_t[ct] if col0 == 1024 else slot_t[si]
                    group(bi, banks[bi].ap()[:, :n],
                          [(sv[:, c, ct * 128:(ct + 1) * 128], xcT_b.ap()[:, c, t0:t0 + n]) for c in range(16)],
                          [wt, xcT_t[t0 // 512]])
                    kb.emit("act", lambda bi=bi, head=head, t0=t0, n=n: act.copy(
                        out=KT.ap()[:, head, t0:t0 + n], in_=banks[bi].ap()[:, :n]),
                        reads=[bank_t[bi]], writes=[KT_t])
            if col0 == 1024:
                for t_ in pre_ct_t:
                    for d in t_.alldeps():
                        slot_t[si]._addr(d)
        else:
            h0 = (col0 - 2048) // 128
            for tt in range(8):
                bi = mm_rot.next()
                group(bi, banks[bi].ap()[:, :512],
                      [(xcT_b.ap()[:, c, tt * 128:(tt + 1) * 128], sv[:, c, :]) for c in range(16)],
                      [slot_t[si], xcT_t[tt // 4]])
                kb.emit("dve", lambda bi=bi, tt=tt, h0=h0: dve.tensor_copy(
                    out=Vall.ap()[:, tt, h0:h0 + 4, 0:128],
                    in_=banks[bi].ap().rearrange("p (h d) -> p h d", h=4)),
                    reads=[bank_t[bi]], writes=[V_t[tt]])

    if STOP <= 1:
        return
    QT = at(RQ, [128, 8, NT], BF16); QT_t = Tk("QT")
    kvst = [at(TMP + i * 2048, [128, 512], F32) for i in range(4)]
    kvst_t = [Tk("kvst%d" % i) for i in range(4)]
    kvst_rot = Rot(list(range(4)))
    kbf = [at(TMP + 8192 + i * 1024, [128, 512], BF16) for i in range(4)]
    kbf_t = [Tk("kbf%d" % i) for i in range(4)]
    kbf_rot = Rot(list(range(4)))
    ktr_rot = Rot([6, 7])

    for (kind, col0) in (("k", 1024), ("k", 1536), ("v", 2048), ("v", 2560), ("q", 0), ("q", 512)):
        si, sv = load_w512(w_in_v, col0)
        if kind == "q":
            for ct in range(4):
                head = (col0 % 1024) // 128 + ct
                for (t0, n) in TB:
                    bi = mm_rot.next()
                    group(bi, banks[bi].ap()[:, :n],
                          [(sv[:, c, ct * 128:(ct + 1) * 128], xT_b.ap()[:, c, t0:t0 + n]) for c in range(16)],
                          [slot_t[si], xT_t])
                    if kind == "q":
                        dst, dt_ = QT.ap()[:, head, t0:t0 + n], QT_t
                    elif t0 < NP:
                        dst, dt_ = KT.ap()[:, head, NP + t0:NP + t0 + n], KT_t
                    else:
                        dst, dt_ = KsT.ap()[:, head, 0:8], KsT_t
                    kb.emit("act", lambda bi=bi, dst=dst, n=n: act.copy(out=dst, in_=banks[bi].ap()[:, :n]),
                            reads=[bank_t[bi]], writes=[dt_])
        if kind in ("k", "v"):
            cb = (col0 % 1024)
            h0 = cb // 128
            pend = []

            def k_transposes(ti, t0, rows, qi, h0=h0):
                tb_ = ktr_rot.next()
                trv = banks[tb_].ap().bitcast(BF16)
                for hh in range(4):
                    kb.emit("pe", lambda hh=hh: pe.transpose(trv[:, hh * 128:hh * 128 + rows],
                                                             kbf[qi].ap()[:rows, hh * 128:(hh + 1) * 128],
                                                             ident_b.ap()[:rows, :rows]),
                            reads=[kbf_t[qi], ident_b_t], writes=[bank_t[tb_]], signal=(hh == 3))
                src = trv[:, 0:512].rearrange("p (h q) -> p h q", h=4)[:, :, :rows]
                if ti < 8:
                    dst, dt_ = KT.ap()[:, h0:h0 + 4, NP + t0:NP + t0 + rows], KT_t
                else:
                    dst, dt_ = KsT.ap()[:, h0:h0 + 4, 0:8], KsT_t
                kb.emit("act", lambda: act.copy(out=dst, in_=src), reads=[bank_t[tb_]], writes=[dt_])

            for ti, (t0, rows) in enumerate(TT):
                bi = mm_rot.next()
                group(bi, banks[bi].ap()[:rows, :512],
                      [(xT_b.ap()[:, c, t0:t0 + rows], sv[:, c, :]) for c in range(16)],
                      [slot_t[si], xT_t])
                if kind == "v":
                    if ti < 8:
                        dst, dt_ = Vall.ap()[:, 8 + ti, h0:h0 + 4, 0:128], V_t[8 + ti]
                    else:
                        dst, dt_ = Vs_new.ap()[0:8, h0:h0 + 4, 0:128], Vs_t
                ki = kvst_rot.next()
                kb.emit("act", lambda bi=bi, ki=ki, rows=rows: act.copy(
                    out=kvst[ki].ap()[:rows, :], in_=banks[bi].ap()[:rows, :]),
                    reads=[bank_t[bi]], writes=[kvst_t[ki]])
                if kind == "v":
                    kb.emit("dve", lambda ki=ki, dst=dst, rows=rows: dve.tensor_copy(
                        out=dst, in_=kvst[ki].ap()[:rows, :].rearrange("p (h d) -> p h d", h=4)),
                        reads=[kvst_t[ki]], writes=[dt_])
                if ti < 8:
                    od = (kout_d if kind == "k" else vout_d)[t0:t0 + 128, cb:cb + 512]
                else:
                    od = (ks_d if kind == "k" else vs_d)[2040:2048, cb:cb + 512]
                kb.dma("sp", [(od, kvst[ki].ap()[:rows, :])], reads=[kvst_t[ki]], final=True)
                if kind == "k":
                    qi = kbf_rot.next()
                    kb.emit("dve", lambda ki=ki, qi=qi, rows=rows: dve.tensor_copy(
                        out=kbf[qi].ap()[:rows, :], in_=kvst[ki].ap()[:rows, :]),
                        reads=[kvst_t[ki]], writes=[kbf_t[qi]])
                    pend.append((ti, t0, rows, qi))
                    if len(pend) > 2:
                        k_transposes(*pend.pop(0))
            while pend:
                k_transposes(*pend.pop(0))

    if STOP <= 2:
        return
    mixed = at(RM, [128, 16, NT], BF16); mixed_t = Tk("mixed", inherit=drop_slots() + xcT_t[0].alldeps() + xcT_t[1].alldeps())
    EP = [at(TMP + 6656 + i * 1024, [128, 512], BF16) for i in range(6)]
    EP_t = [Tk("EP%d" % i, inherit=sum([k.alldeps() for k in kvst_t + kbf_t], [])) for i in range(6)]
    EP_rot = Rot(list(range(6)))
    ao = [at(TMP + i * 2048, [128, 1024], BF16) for i in range(2)]
    ao_t = [Tk("ao%d" % i, inherit=sum([k.alldeps() for k in kvst_t + kbf_t], [])) for i in range(2)]
    S_rot = Rot([0, 1, 2, 3])
    O_banks = [4, 5]
    TR_bank = 6

    units = []
    for t in range(8):
        g = 8 + t
        for h in range(8):
            groups = [(0, 3, "c"), (4, 7, "c"), (8, min(11, g), "o")]
            if g >= 12:
                groups.append((12, g, "o"))
            for gi, (j0, j1, kind) in enumerate(groups):
                units.append(dict(t=t, g=g, h=h, j0=j0, j1=j1, kind=kind, first=(gi == 0),
                                  last=(gi == len(groups) - 1)))

    def emit_S(u):
        bi = S_rot.next()
        u["sb"] = bi
        n = u["j1"] - u["j0"] + 1
        t, h = u["t"], u["h"]
        for jj in range(n):
            j = u["j0"] + jj
            kb.emit("pe", lambda jj=jj, j=j: pe.matmul(
                banks[bi].ap()[:, jj * 128:(jj + 1) * 128], KT.ap()[:, h, j * 128:(j + 1) * 128],
                QT.ap()[:, h, t * 128:(t + 1) * 128], start=True, stop=True),
                reads=[KT_t, QT_t], writes=[bank_t[bi]], signal=(jj == n - 1))
        ei = EP_rot.next()
        u["ep"] = ei
        kb.emit("act", lambda: act.activation(out=EP[ei].ap()[:, :n * 128], in_=banks[bi].ap()[:, :n * 128],
                                              func=AF.Exp, scale=SCALE),
                reads=[bank_t[bi]], writes=[EP_t[ei]])
        if u["kind"] == "o":
            m0 = 7 - u["g"] + u["j0"]
            msk, mt = mown.ap()[:, m0 * 128:(m0 + n) * 128], mown_t
        else:
            m0 = 15 - u["g"] + u["j0"]
            msk, mt = mctx.ap()[:, m0 * 128:(m0 + n) * 128], mctx_t
        kb.emit("dve", lambda: dve.tensor_tensor(out=EP[ei].ap()[:, :n * 128], in0=EP[ei].ap()[:, :n * 128],
                                                 in1=msk, op=ALU.mult),
                reads=[EP_t[ei], mt], writes=[EP_t[ei]])

    def emit_PV(u):
        t, h = u["t"], u["h"]
        ob = O_banks[(t * 8 + h) % 2]
        n = u["j1"] - u["j0"] + 1
        ei = u["ep"]
        for jj in range(n):
            j = u["j0"] + jj
            kb.emit("pe", lambda jj=jj, j=j: pe.matmul(
                banks[ob].ap()[:, 0:129], EP[ei].ap()[:, jj * 128:(jj + 1) * 128], Vall.ap()[:, j, h, :],
                start=(u["first"] and jj == 0), stop=(u["last"] and jj == n - 1)),
                reads=[EP_t[ei], V_t[j]], writes=[bank_t[ob]], signal=(jj == n - 1))
        if u["last"]:
            ri = (t * 8 + h) % 4
            kb.emit("dve", lambda: dve.reciprocal(out=rc[ri].ap(), in_=banks[ob].ap()[:, 128:129]),
                    reads=[bank_t[ob]], writes=[rc_t[ri]])
            kb.emit("dve", lambda: dve.tensor_scalar(out=ao[t % 2].ap()[:, h * 128:(h + 1) * 128],
                                                     in0=banks[ob].ap()[:, 0:128], scalar1=rc[ri].ap()[:, 0:1],
                                                     scalar2=None, op0=ALU.mult),
                    reads=[bank_t[ob], rc_t[ri]], writes=[ao_t[t % 2]])
            if h == 7:
                trv = banks[TR_bank].ap().bitcast(BF16)
                for hh in range(8):
                    kb.emit("pe", lambda hh=hh: pe.transpose(trv[:, hh * 128:(hh + 1) * 128],
                                                             ao[t % 2].ap()[:, hh * 128:(hh + 1) * 128],
                                                             ident_b.ap()),
                            reads=[ao_t[t % 2], ident_b_t], writes=[bank_t[TR_bank]], signal=(hh == 7))
                kb.emit("act", lambda: act.copy(out=mixed.ap()[:, 0:8, t * 128:(t + 1) * 128],
                                                in_=trv.rearrange("p (h q) -> p h q", h=8)),
                        reads=[bank_t[TR_bank]], writes=[mixed_t])

    def prompt_attention():
        LAG = 4
        sample_load(0)
        for i, u in enumerate(units):
            emit_S(u)
            if i >= LAG:
                emit_PV(units[i - LAG])
            if u["h"] == 7 and u["last"]:
                sample_head(u["t"])
                if u["t"] < 7:
                    sample_load(u["t"] + 1)
        for u in units[len(units) - LAG:]:
            emit_PV(u)

    kc = at(W_OFF + SLOT_BYTES, [128, 2048], BF16)
    vc = at(W_OFF + SLOT_BYTES + 4096, [128, 16, 129], BF16)
    kv_t = slot_t[1]
    kb.emit("dve", lambda: dve.memset(vc.ap()[:, :, 128:129], 1.0), writes=[kv_t])
    Es = at(TMP + 4096, [128, 144], BF16)
    Es_t = Tk("Es", inherit=sum([k.alldeps() for k in kvst_t + kbf_t], []))
    ao_s = at(TMP + 4416, [8, 1024], BF16)
    ao_s_t = Tk("aos", inherit=sum([k.alldeps() for k in kvst_t + kbf_t], []))
    cv_v = cv_d.rearrange("(t p) (h d) -> p t h d", p=128, h=8)
    SO_bank = 7

    def sample_load(h):
        kb.dma("pool", [(kc.ap(), ckT_d[h * 128:(h + 1) * 128, :]),
                        (vc.ap()[:, :, 0:128], cv_v[:, :, h, :])], writes=[kv_t])

    def sample_head(h):
        bi = S_rot.next()
        for j in range(16):
            kb.emit("pe", lambda j=j: pe.matmul(banks[bi].ap()[:, j * 8:(j + 1) * 8],
                                                kc.ap()[:, j * 128:(j + 1) * 128],
                                                QT.ap()[:, h, NP:NT], start=True, stop=True),
                    reads=[kv_t, QT_t], writes=[bank_t[bi]], signal=False)
        kb.emit("pe", lambda: pe.matmul(banks[bi].ap()[0:8, 128:136], KsT.ap()[:, h, 0:8],
                                        QT.ap()[:, h, NP:NT], start=True, stop=True),
                reads=[KsT_t, QT_t], writes=[bank_t[bi]])
        kb.emit("act", lambda: act.activation(out=Es.ap()[:, 0:128], in_=banks[bi].ap()[:, 0:128],
                                              func=AF.Exp, scale=SCALE),
                reads=[bank_t[bi]], writes=[Es_t])
        kb.emit("act", lambda: act.activation(out=Es.ap()[0:8, 128:136], in_=banks[bi].ap()[0:8, 128:136],
                                              func=AF.Exp, scale=SCALE),
                reads=[bank_t[bi]], writes=[Es_t])
        kb.emit("dve", lambda: dve.tensor_tensor(out=Es.ap()[:, 0:128], in0=Es.ap()[:, 0:128],
                                                 in1=msam.ap()[:, 0:128], op=ALU.mult),
                reads=[Es_t, msam_t], writes=[Es_t])
        kb.emit("dve", lambda: dve.tensor_tensor(out=Es.ap()[0:8, 128:136], in0=Es.ap()[0:8, 128:136],
                                                 in1=msam.ap()[0:8, 128:136], op=ALU.mult),
                reads=[Es_t, msam_t], writes=[Es_t])
        ob = SO_bank
        for j in range(16):
            kb.emit("pe", lambda j=j: pe.matmul(banks[ob].ap()[0:8, 0:129], Es.ap()[:, j * 8:(j + 1) * 8],
                                                vc.ap()[:, j, :], start=(j == 0), stop=False),
                    reads=[Es_t, kv_t], writes=[bank_t[ob]], signal=False)
        kb.emit("pe", lambda: pe.matmul(banks[ob].ap()[0:8, 0:129], Es.ap()[0:8, 128:136],
                                        Vs_new.ap()[0:8, h, :], start=False, stop=True),
                reads=[Es_t, Vs_t], writes=[bank_t[ob]])
        ri = h % 4
        kb.emit("dve", lambda: dve.reciprocal(out=rc[ri].ap()[0:8, :], in_=banks[ob].ap()[0:8, 128:129]),
                reads=[bank_t[ob]], writes=[rc_t[ri]])
        kb.emit("dve", lambda: dve.tensor_scalar(out=ao_s.ap()[0:8, h * 128:(h + 1) * 128],
                                                 in0=banks[ob].ap()[0:8, 0:128], scalar1=rc[ri].ap()[0:8, 0:1],
                                                 scalar2=None, op0=ALU.mult),
                reads=[bank_t[ob], rc_t[ri]], writes=[ao_s_t])

    def sample_finish():
        trv = banks[TR_bank].ap().bitcast(BF16)
        for hh in range(8):
            kb.emit("pe", lambda hh=hh: pe.transpose(trv[:, hh * 8:(hh + 1) * 8],
                                                     ao_s.ap()[0:8, hh * 128:(hh + 1) * 128],
                                                     ident_b.ap()[0:8, 0:8]),
                    reads=[ao_s_t, ident_b_t], writes=[bank_t[TR_bank]], signal=(hh == 7))
        kb.emit("act", lambda: act.copy(out=mixed.ap()[:, 0:8, NP:NT],
                                        in_=trv[:, 0:64].rearrange("p (h q) -> p h q", h=8)),
                reads=[bank_t[TR_bank]], writes=[mixed_t])

    if STOP <= 3:
        return
    prompt_attention()
    sample_finish()
    if STOP <= 4:
        return
    u_bf = at(RQ, [128, 8, 1062], BF16)
    us_ext = at(RQ + 16992, [128, 8, 38], BF16)
    ubf_t = [Tk("ubf%d" % i, inherit=QT_t.alldeps()) for i in range(8)]
    us_t = Tk("usext", inherit=QT_t.alldeps())
    big_dead = sum([k.alldeps() for k in [KT_t, mown_t, mctx_t, msam_t] + V_t], [])
    diag = [at(BIG + i * 7936, [128, 31, 128], BF16) for i in range(2)]
    diag_t = [Tk("diag%d" % i, inherit=big_dead) for i in range(2)]
    sig = [at(BIG + 16384 + i * 2080, [128, 520], F32) for i in range(2)]
    sig_t = [Tk("sig%d" % i, inherit=big_dead) for i in range(2)]
    sig_rot = Rot([0, 1])
    sq = [at(BIG + 20736 + i * 2048, [128, 512], F32) for i in range(2)]
    sq_t = [Tk("sq%d" % i, inherit=big_dead) for i in range(2)]
    mean_s = at(BIG + 24832, [128, 512], F32); mean_t = Tk("mean", inherit=big_dead)
    var_s = at(BIG + 26880, [128, 512], F32); var_t = Tk("var", inherit=big_dead)
    rstd_s = at(BIG + 28928, [128, 512], F32); rstd_t = Tk("rstd", inherit=big_dead)
    t1 = [at(BIG + 30976 + i * 2048, [128, 512], F32) for i in range(2)]
    t1_t = [Tk("t1_%d" % i, inherit=big_dead) for i in range(2)]
    cst = at(BIG + 35072, [32, 1024], F32); cst_t = Tk("cst", inherit=big_dead)
    cst_s = at(BIG + 39168, [8, 1024], F32); csts_t = Tk("csts", inherit=big_dead)
    add_slot(BIG + 44032, big_dead)

    for blk in range(4):
        def f(sv, blk=blk):
            s3 = sv.rearrange("p (c n) -> p c n", c=16)
            return [(s3[:, :, 0:256], w_in_v[:, :, 3072 + 256 * blk: 3072 + 256 * blk + 256]),
                    (s3[:, :, 256:512], w_in_v[:, :, 4096 + 256 * blk: 4096 + 256 * blk + 256])]
        si = load_slot(f)
        sv = slots[si].ap().rearrange("p (c n) -> p c n", c=16)
        for ct in range(2):
            ch = 2 * blk + ct
            for (t0, n, src) in [(0, 512, "x"), (512, 512, "x"), (1024, 8, "x"), (0, 32, "c")]:
                ba, bg = mm_rot.next(), mm_rot.next()
                if src == "x":
                    rhs = lambda c: xT_b.ap()[:, c, t0:t0 + n]
                    rt = xT_t
                else:
                    rhs = lambda c: xc32_b.ap()[:, c, 0:32]
                    rt = xc32_t
                group(ba, banks[ba].ap()[:, :n],
                      [(sv[:, c, ct * 128:(ct + 1) * 128], rhs(c)) for c in range(16)], [slot_t[si], rt])
                group(bg, banks[bg].ap()[:, :n],
                      [(sv[:, c, 256 + ct * 128:256 + (ct + 1) * 128], rhs(c)) for c in range(16)],
                      [slot_t[si], rt])
                sgi = sig_rot.next()
                kb.emit("act", lambda: act.activation(out=sig[sgi].ap()[:, :n], in_=banks[bg].ap()[:, :n],
                                                      func=AF.Sigmoid),
                        reads=[bank_t[bg]], writes=[sig_t[sgi]])
                if src == "c":
                    kb.emit("dve", lambda: dve.tensor_tensor(out=u_bf.ap()[:, ch, 0:30], in0=banks[ba].ap()[:, 2:32],
                                                             in1=sig[sgi].ap()[:, 2:32], op=ALU.mult),
                            reads=[bank_t[ba], sig_t[sgi]], writes=[ubf_t[ch]])
                elif t0 < NP:
                    kb.emit("dve", lambda: dve.tensor_tensor(out=u_bf.ap()[:, ch, 30 + t0:30 + t0 + n],
                                                             in0=banks[ba].ap()[:, :n], in1=sig[sgi].ap()[:, :n],
                                                             op=ALU.mult),
                            reads=[bank_t[ba], sig_t[sgi]], writes=[ubf_t[ch]])
                    if t0 == 512:
                        kb.emit("dve", lambda: dve.tensor_tensor(out=u32.ap()[:, ch, 0:32],
                                                                 in0=banks[ba].ap()[:, 480:512],
                                                                 in1=sig[sgi].ap()[:, 480:512], op=ALU.mult),
                                reads=[bank_t[ba], sig_t[sgi]], writes=[u32_t])
                else:
                    kb.emit("dve", lambda: dve.tensor_tensor(out=u32.ap()[:, ch, 32:40], in0=banks[ba].ap()[:, :8],
                                                             in1=sig[sgi].ap()[:, :8], op=ALU.mult),
                            reads=[bank_t[ba], sig_t[sgi]], writes=[u32_t])
                    kb.emit("dve", lambda: dve.tensor_tensor(out=us_ext.ap()[:, ch, 30:38],
                                                             in0=banks[ba].ap()[:, :8],
                                                             in1=sig[sgi].ap()[:, :8], op=ALU.mult),
                            reads=[bank_t[ba], sig_t[sgi]], writes=[us_t])

    if STOP <= 5:
        return
    for ch in range(8):
        bi = 6 + ch // 4
        kb.emit("pe", lambda ch=ch, bi=bi: pe.transpose(banks[bi].ap()[0:32, (ch % 4) * 128:(ch % 4 + 1) * 128],
                                                        u32.ap()[:, ch, 0:32], ident_f.ap()),
                reads=[u32_t, ident_f_t], writes=[bank_t[bi]], signal=(ch % 4 == 3))
    for half in range(2):
        kb.emit("act", lambda half=half: act.copy(out=cst.ap()[0:32, half * 512:(half + 1) * 512],
                                                  in_=banks[6 + half].ap()[0:32, :]),
                reads=[bank_t[6 + half]], writes=[cst_t])
    kb.dma("sp", [(convp_d[0:30, :], cst.ap()[2:32, :])], reads=[cst_t], final=True)
    for ch in range(8):
        bi = 6 + ch // 4
        kb.emit("pe", lambda ch=ch, bi=bi: pe.transpose(banks[bi].ap()[0:8, (ch % 4) * 128:(ch % 4 + 1) * 128],
                                                        u32.ap()[:, ch, 32:40], ident_f.ap()),
                reads=[u32_t, ident_f_t], writes=[bank_t[bi]], signal=(ch % 4 == 3))
    for half in range(2):
        kb.emit("act", lambda half=half: act.copy(out=cst_s.ap()[0:8, half * 512:(half + 1) * 512],
                                                  in_=banks[6 + half].ap()[0:8, :]),
                reads=[bank_t[6 + half]], writes=[csts_t])
    kb.dma("sp", [(convs_d[22:30, :], cst_s.ap()[0:8, :])], reads=[csts_t], final=True)

    if STOP <= 6:
        return
    kb.dma("pool", [(us_ext.ap()[:, :, 0:30], scT_d.rearrange("(c p) j -> p c j", p=128))], writes=[us_t])
    cacc = at(RX, [128, 8, NT], F32); cacc_t = [Tk("cacc%d" % i, inherit=xT_t.alldeps()) for i in range(8)]
    for ch in range(8):
        dg = diag[ch % 2]
        dgt = diag_t[ch % 2]
        for j in range(31):
            kb.emit("dve", lambda j=j: dve.tensor_scalar(out=dg.ap()[:, j, :], in0=ident_b.ap(),
                                                         scalar1=pch.ap()[:, ch, j:j + 1], scalar2=None,
                                                         op0=ALU.mult),
                    reads=[ident_b_t, pch_t], writes=[dgt])
        for (t0, n) in TB:
            bi = mm_rot.next()
            if t0 < NP:
                prs = [(dg.ap()[:, j, :], u_bf.ap()[:, ch, t0 + j:t0 + j + n]) for j in range(31)]
                rr = [dgt, ubf_t[ch]]
            else:
                prs = [(dg.ap()[:, j, :], us_ext.ap()[:, ch, j:j + 8]) for j in range(31)]
                rr = [dgt, us_t]
            group(bi, banks[bi].ap()[:, :n], prs, rr)
            kb.emit("act", lambda bi=bi, t0=t0, n=n: act.activation(out=cacc.ap()[:, ch, t0:t0 + n],
                                                                    in_=banks[bi].ap()[:, :n], func=AF.Identity,
                                                                    bias=pch.ap()[:, ch, 31:32]),
                    reads=[bank_t[bi], pch_t], writes=[cacc_t[ch]])

    if STOP <= 7:
        return
    sq_rot = Rot([0, 1])
    t1_rot = Rot([0, 1])
    stat_sets = [(mean_s, var_s, rstd_s, mean_t, var_t, rstd_t)]
    m1 = at(BIG + 60416, [128, 512], F32); v1 = at(BIG + 62464, [128, 512], F32); r1 = at(BIG + 64512, [128, 512], F32)
    stat_sets.append((m1, v1, r1, Tk("mean1", inherit=big_dead), Tk("var1", inherit=big_dead),
                      Tk("rstd1", inherit=big_dead)))
    m2 = at(BIG + 66560, [128, 8], F32); v2 = at(BIG + 66592, [128, 8], F32); r2 = at(BIG + 66624, [128, 8], F32)
    stat_sets.append((m2, v2, r2, Tk("mean2", inherit=big_dead), Tk("var2", inherit=big_dead),
                      Tk("rstd2", inherit=big_dead)))
    stat_banks = [(6, 7), (0, 1), (2, 3)]

    def cln_stats(i):
        (t0, n) = TB[i]
        b1, b2_ = stat_banks[i]
        for ch in range(8):
            qi = sq_rot.next()
            kb.emit("act", lambda: act.activation(out=sq[qi].ap()[:, :n], in_=cacc.ap()[:, ch, t0:t0 + n],
                                                  func=AF.Square),
                    reads=[cacc_t[ch]], writes=[sq_t[qi]])
            kb.emit("pe", lambda: pe.matmul(banks[b1].ap()[:, :n], ones_f.ap(), cacc.ap()[:, ch, t0:t0 + n],
                                            start=(ch == 0), stop=(ch == 7)),
                    reads=[ones_t, cacc_t[ch]], writes=[bank_t[b1]])
            kb.emit("pe", lambda: pe.matmul(banks[b2_].ap()[:, :n], ones_f.ap(), sq[qi].ap()[:, :n],
                                            start=(ch == 0), stop=(ch == 7)),
                    reads=[ones_t, sq_t[qi]], writes=[bank_t[b2_]])

    def cln_chain(i):
        (t0, n) = TB[i]
        b1, b2_ = stat_banks[i]
        mean_x, var_x, rstd_x, mean_xt, var_xt, rstd_xt = stat_sets[i]
        kb.emit("act", lambda: act.mul(out=mean_x.ap()[:, :n], in_=banks[b1].ap()[:, :n], mul=1.0 / 1024.0),
                reads=[bank_t[b1]], writes=[mean_xt])
        kb.emit("dve", lambda: dve.tensor_tensor(out=var_x.ap()[:, :n], in0=mean_x.ap()[:, :n],
                                                 in1=mean_x.ap()[:, :n], op=ALU.mult),
                reads=[mean_xt], writes=[var_xt])
        kb.emit("dve", lambda: dve.scalar_tensor_tensor(out=var_x.ap()[:, :n], in0=banks[b2_].ap()[:, :n],
                                                        scalar=1.0 / 1024.0, in1=var_x.ap()[:, :n],
                                                        op0=ALU.mult, op1=ALU.subtract),
                reads=[bank_t[b2_], var_xt], writes=[var_xt])
        kb.emit("act", lambda: act.activation(out=rstd_x.ap()[:, :n], in_=var_x.ap()[:, :n], func=AF.Sqrt,
                                              bias=eps_c.ap()[:, 0:1]),
                reads=[var_xt, eps_t], writes=[rstd_xt])
        kb.emit("dve", lambda: dve.reciprocal(out=rstd_x.ap()[:, :n], in_=rstd_x.ap()[:, :n]),
                reads=[rstd_xt], writes=[rstd_xt])

    def cln_norm(i):
        (t0, n) = TB[i]
        mean_x, var_x, rstd_x, mean_xt, var_xt, rstd_xt = stat_sets[i]
        for ch in range(8):
            ti = t1_rot.next()
            kb.emit("dve", lambda: dve.tensor_tensor(out=t1[ti].ap()[:, :n], in0=cacc.ap()[:, ch, t0:t0 + n],
                                                     in1=mean_x.ap()[:, :n], op=ALU.subtract),
                    reads=[cacc_t[ch], mean_xt], writes=[t1_t[ti]])
            kb.emit("dve", lambda: dve.tensor_tensor(out=t1[ti].ap()[:, :n], in0=t1[ti].ap()[:, :n],
                                                     in1=rstd_x.ap()[:, :n], op=ALU.mult),
                    reads=[t1_t[ti], rstd_xt], writes=[t1_t[ti]])
            kb.emit("act", lambda: act.activation(out=mixed.ap()[:, 8 + ch, t0:t0 + n], in_=t1[ti].ap()[:, :n],
                                                  func=AF.Silu, bias=pch.ap()[:, ch, 33:34],
                                                  scale=pch.ap()[:, ch, 32:33]),
                    reads=[t1_t[ti], pch_t], writes=[mixed_t])

    cln_stats(0)
    cln_chain(0)
    cln_stats(1)
    cln_norm(0)
    cln_chain(1)
    cln_stats(2)
    cln_norm(1)
    cln_chain(2)
    cln_norm(2)

    if STOP <= 8:
        return
    big_dead2 = sum([k.alldeps() for k in diag_t + sig_t + sq_t + [mean_t, var_t, rstd_t, cst_t, csts_t] + t1_t
                     + [x for st in stat_sets[1:] for x in st[3:]]], [])
    big_dead2 = big_dead2 + drop_slots()
    add_slot(RQ, sum([k.alldeps() for k in ubf_t + [us_t]], []))
    acc = at(BIG, [128, 9, D], F32)
    acc_t = [Tk("acc%d" % i, inherit=big_dead + big_dead2) for i in range(9)]
    for ti, (t0, rows) in enumerate(TT):
        kb.dma("sp", [(acc.ap()[:rows, ti, :], xres_d[t0:t0 + rows, :])], writes=[acc_t[ti]])
    rx_dead2 = sum([k.alldeps() for k in cacc_t], [])
    hidden = at(RX, [128, 8, NT], BF16); hidden_t = Tk("hidden", inherit=rx_dead2)
    gb = at(RX + 16512, [128, 2, D], F32); gb_t = Tk("gb", inherit=rx_dead2)
    kb.dma("sp", [(gb.ap(), lngb_d[:, 0:2, :])], writes=[gb_t])
    hb = [at(TMP + i * 4096, [128, D], BF16) for i in range(2)]
    tmp_dead = sum([k.alldeps() for k in ao_t + EP_t + [Es_t, ao_s_t]], [])
    hb_t = [Tk("hb%d" % i, inherit=tmp_dead) for i in range(2)]

    class LNPipe:
        NPH = 5
        OFF = [0, 1, 2, 3, 3]

        def __init__(self, tail, gmul_eng):
            self.q = []
            self.step = 0
            self.tail = tail
            self.gmul_eng = gmul_eng

        def push(self, ti, rows, t0):
            self.q.append((ti, rows, t0))
            self._step()

        def flush(self):
            for _ in range(self.OFF[-1]):
                self._step()

        def _step(self):
            sidx = self.step
            self.step += 1
            for p in sorted(range(self.NPH), key=lambda p: (-self.OFF[p], p)):
                idx = sidx - self.OFF[p]
                if 0 <= idx < len(self.q):
                    self._phase(p, *self.q[idx])

        def _phase(self, p, ti, rows, t0):
            k = ti % 4
            a = acc.ap()[:rows, ti, :]
            if p == 0:
                for q in range(4):
                    kb.emit("dve", lambda q=q: dve.bn_stats(out=st6[k].ap()[:rows, q, :],
                                                            in_=acc.ap()[:rows, ti, q * 512:(q + 1) * 512]),
                            reads=[acc_t[ti]], writes=[st6_t[k]])
                kb.emit("dve", lambda: dve.bn_aggr(out=mv[k].ap()[:rows, :],
                                                   in_=st6[k].ap()[:rows].rearrange("p q s -> p (q s)")),
                        reads=[st6_t[k]], writes=[mv_t[k]])
                kb.emit("act", lambda: act.activation(out=sd[k].ap()[:rows, 0:1], in_=mv[k].ap()[:rows, 1:2],
                                                      func=AF.Sqrt, bias=eps_c.ap()[:rows, 0:1]),
                        reads=[mv_t[k], eps_t], writes=[sd_t[k]])
                kb.emit("dve", lambda: dve.reciprocal(out=sd[k].ap()[:rows, 1:2], in_=sd[k].ap()[:rows, 0:1]),
                        reads=[sd_t[k]], writes=[sd_t[k]])
                kb.emit("dve", lambda: dve.tensor_scalar(out=nm[k].ap()[:rows, 0:1], in0=mv[k].ap()[:rows, 0:1],
                                                         scalar1=sd[k].ap()[:rows, 1:2], scalar2=-1.0,
                                                         op0=ALU.mult, op1=ALU.mult),
                        reads=[mv_t[k], sd_t[k]], writes=[nm_t[k]])
            elif p == 1:
                kb.emit("act", lambda: act.activation(out=a, in_=a, func=AF.Identity,
                                                      scale=sd[k].ap()[:rows, 1:2], bias=nm[k].ap()[:rows, 0:1]),
                        reads=[acc_t[ti], nm_t[k], sd_t[k]], writes=[acc_t[ti]])
            elif p == 2:
                ge = pool if self.gmul_eng == "pool" else dve
                kb.emit(self.gmul_eng, lambda: ge.tensor_tensor(out=a, in0=a, in1=gb.ap()[:rows, 0, :],
                                                                op=ALU.mult),
                        reads=[acc_t[ti], gb_t], writes=[acc_t[ti]])
            elif p == 3:
                en = "pool" if ti % 2 == 0 else "dve"
                be = pool if en == "pool" else dve
                kb.emit(en, lambda: be.tensor_tensor(out=a, in0=a, in1=gb.ap()[:rows, 1, :], op=ALU.add),
                        reads=[acc_t[ti], gb_t], writes=[acc_t[ti]])
            else:
                self.tail(ti, rows, t0)

    hT = at(RM, [128, 16, NT], BF16)
    hT_t = Tk("hT", inherit=mixed_t.alldeps())

    class LN1Pipe:
        OFF = [0, 1, 2, 3]

        def __init__(self):
            self.q = []
            self.step = 0

        def push(self, ti, rows, t0):
            self.q.append((ti, rows, t0))
            self._step()

        def flush(self):
            for _ in range(self.OFF[-1]):
                self._step()

        def _step(self):
            sidx = self.step
            self.step += 1
            for p in sorted(range(len(self.OFF)), key=lambda p: (-self.OFF[p], p)):
                idx = sidx - self.OFF[p]
                if 0 <= idx < len(self.q):
                    self._phase(p, *self.q[idx])

        def _phase(self, p, ti, rows, t0):
            k = ti % 4
            k2 = ti % 2
            if p == 0:
                for q in range(4):
                    kb.emit("dve", lambda q=q: dve.bn_stats(out=st6[k].ap()[:rows, q, :],
                                                            in_=acc.ap()[:rows, ti, q * 512:(q + 1) * 512]),
                            reads=[acc_t[ti]], writes=[st6_t[k]])
                kb.emit("dve", lambda: dve.bn_aggr(out=mv[k].ap()[:rows, :],
                                                   in_=st6[k].ap()[:rows].rearrange("p q s -> p (q s)")),
                        reads=[st6_t[k]], writes=[mv_t[k]])
            elif p == 1:
                kb.emit("act", lambda: act.activation(out=sd9.ap()[:rows, ti, 0:1], in_=mv[k].ap()[:rows, 1:2],
                                                      func=AF.Sqrt, bias=eps_c.ap()[:rows, 0:1]),
                        reads=[mv_t[k], eps_t], writes=[sd9_t[ti]])
                kb.emit("dve", lambda: dve.reciprocal(out=sd9.ap()[:rows, ti, 1:2], in_=sd9.ap()[:rows, ti, 0:1]),
                        reads=[sd9_t[ti]], writes=[sd9_t[ti]])
                kb.emit("dve", lambda: dve.tensor_scalar(out=nm9.ap()[:rows, ti:ti + 1], in0=mv[k].ap()[:rows, 0:1],
                                                         scalar1=sd9.ap()[:rows, ti, 1:2], scalar2=-1.0,
                                                         op0=ALU.mult, op1=ALU.mult),
                        reads=[mv_t[k], sd9_t[ti]], writes=[nm9_t[ti]])
            elif p == 2:
                kb.emit("act", lambda: act.activation(out=hb[k2].ap()[:rows, :], in_=acc.ap()[:rows, ti, :],
                                                      func=AF.Identity, scale=sd9.ap()[:rows, ti, 1:2],
                                                      bias=nm9.ap()[:rows, ti:ti + 1]),
                        reads=[acc_t[ti], sd9_t[ti], nm9_t[ti]], writes=[hb_t[k2]])
            else:
                for half in range(2):
                    bi = 6 + half
                    trv = banks[bi].ap().bitcast(BF16)
                    for q in range(8):
                        cidx = half * 8 + q
                        kb.emit("pe", lambda q=q, cidx=cidx, trv=trv: pe.transpose(
                            trv[:, q * 128:q * 128 + rows], hb[k2].ap()[:rows, cidx * 128:(cidx + 1) * 128],
                            ident_b.ap()[:rows, :rows]),
                            reads=[hb_t[k2], ident_b_t], writes=[bank_t[bi]], signal=(q == 7))
                    for d in mixed_t.alldeps():
                        hT_t._addr(d)
                    for q in range(8):
                        cidx = half * 8 + q
                        src = trv[:, q * 128:q * 128 + rows]
                        dst = hT.ap()[:, cidx, t0:t0 + rows]
                        gcol = ln1T.ap()[:, 0, cidx:cidx + 1]
                        bcol = ln1T.ap()[:, 1, cidx:cidx + 1]
                        if half == 0:
                            kb.emit("act", lambda src=src, dst=dst, gcol=gcol, bcol=bcol: act.activation(
                                out=dst, in_=src, func=AF.Identity, scale=gcol, bias=bcol),
                                reads=[bank_t[bi], ln1T_t], writes=[hT_t])
                        else:
                            kb.emit("dve", lambda src=src, dst=dst, gcol=gcol, bcol=bcol: dve.tensor_scalar(
                                out=dst, in0=src, scalar1=gcol, scalar2=bcol, op0=ALU.mult, op1=ALU.add),
                                reads=[bank_t[bi], ln1T_t], writes=[hT_t])

    def ln1_deferred(ti, rows):
        a = acc.ap()[:rows, ti, :]
        kb.emit("act", lambda: act.activation(out=a, in_=a, func=AF.Identity, scale=sd9.ap()[:rows, ti, 1:2],
                                              bias=nm9.ap()[:rows, ti:ti + 1]),
                reads=[acc_t[ti], sd9_t[ti], nm9_t[ti]], writes=[acc_t[ti]])
        kb.emit("dve", lambda: dve.tensor_tensor(out=a, in0=a, in1=gb.ap()[:rows, 0, :], op=ALU.mult),
                reads=[acc_t[ti], gb_t], writes=[acc_t[ti]])
        kb.emit("dve", lambda: dve.tensor_tensor(out=a, in0=a, in1=gb.ap()[:rows, 1, :], op=ALU.add),
                reads=[acc_t[ti], gb_t], writes=[acc_t[ti]])

    ln1 = LN1Pipe()
    for cb in range(4):
        si, sv = load_w512(w_out_v, cb * 512)
        for ti, (t0, rows) in enumerate(TT):
            bi = mm_rot.next()
            group(bi, banks[bi].ap()[:rows, :512],
                  [(mixed.ap()[:, e, t0:t0 + rows], sv[:, e, :]) for e in range(16)], [slot_t[si], mixed_t])
            kb.emit("dve", lambda bi=bi, ti=ti, rows=rows, cb=cb: dve.scalar_tensor_tensor(
                out=acc.ap()[:rows, ti, cb * 512:(cb + 1) * 512], in0=acc.ap()[:rows, ti, cb * 512:(cb + 1) * 512],
                scalar=ALPHA, in1=banks[bi].ap()[:rows, :512], op0=ALU.mult, op1=ALU.add),
                reads=[bank_t[bi], acc_t[ti]], writes=[acc_t[ti]])
            if cb == 3:
                ln1.push(ti, rows, t0)
    ln1.flush()

    def ln2_tail(ti, rows, t0):
        kb.dma("sp", [(y_d[t0:t0 + rows, :], acc.ap()[:rows, ti, :])], reads=[acc_t[ti]], final=True)

    ln2 = LNPipe(ln2_tail, "pool")

    if STOP <= 9:
        return
    rt = [at(TMP + 8192 + i * 2048, [128, 512], F32) for i in range(2)]
    rt_t = [Tk("rt%d" % i, inherit=tmp_dead) for i in range(2)]
    rt_rot = Rot([0, 1])
    for b in range(8):
        cache_copies_part(b)
        if b == 1:
            kb.dma("sp", [(gb.ap(), lngb_d[:, 2:4, :])], writes=[gb_t])
        for ub in range(2):
            si, sv = load_w512(w_up_v, b * 1024 + ub * 512)
            for ft in range(4):
                fc = ub * 4 + ft
                for (t0, n) in TB:
                    bi = mm_rot.next()
                    group(bi, banks[bi].ap()[:, :n],
                          [(sv[:, c, ft * 128:(ft + 1) * 128], hT.ap()[:, c, t0:t0 + n]) for c in range(16)],
                          [slot_t[si], hT_t])
                    ri = rt_rot.next()
                    kb.emit("act", lambda bi=bi, ri=ri, n=n: act.activation(out=rt[ri].ap()[:, :n],
                                                                            in_=banks[bi].ap()[:, :n], func=AF.Relu),
                            reads=[bank_t[bi]], writes=[rt_t[ri]])
                    kb.emit("dve", lambda ri=ri, fc=fc, t0=t0, n=n: dve.tensor_tensor(
                        out=hidden.ap()[:, fc, t0:t0 + n], in0=rt[ri].ap()[:, :n], in1=rt[ri].ap()[:, :n],
                        op=ALU.mult),
                        reads=[rt_t[ri]], writes=[hidden_t])
                if b == 0:
                    ln1_deferred(fc, TT[fc][1])
                    if fc == 7:
                        ln1_deferred(8, TT[8][1])
        def load_down(dh, b=b):
            def f(sv):
                s3 = sv.rearrange("p (c n) -> p c n", c=8)
                return [(s3[:, 0:4, :], w_down_v[:, b * 8:b * 8 + 4, dh * 1024:(dh + 1) * 1024]),
                        (s3[:, 4:8, :], w_down_v[:, b * 8 + 4:b * 8 + 8, dh * 1024:(dh + 1) * 1024])]
            si = load_slot(f)
            return si, slots[si].ap().rearrange("p (c n) -> p c n", c=8)

        def down_group(si, sv, dh, cg, ti, t0, rows, b=b):
            bi = mm_rot.next()
            group(bi, banks[bi].ap()[:rows, :512],
                  [(hidden.ap()[:, fc, t0:t0 + rows], sv[:, fc, cg * 512:(cg + 1) * 512]) for fc in range(8)],
                  [slot_t[si], hidden_t])
            c0 = dh * 1024 + cg * 512
            if b == 0:
                kb.emit("dve", lambda: dve.scalar_tensor_tensor(
                    out=acc.ap()[:rows, ti, c0:c0 + 512], in0=acc.ap()[:rows, ti, c0:c0 + 512],
                    scalar=ALPHA, in1=banks[bi].ap()[:rows, :512], op0=ALU.mult, op1=ALU.add),
                    reads=[bank_t[bi], acc_t[ti]], writes=[acc_t[ti]])
            else:
                kb.emit("dve", lambda: dve.tensor_tensor(
                    out=acc.ap()[:rows, ti, c0:c0 + 512], in0=banks[bi].ap()[:rows, :512],
                    in1=acc.ap()[:rows, ti, c0:c0 + 512], op=ALU.add),
                    reads=[bank_t[bi], acc_t[ti]], writes=[acc_t[ti]])

        if b < 7:
            for dh in range(2):
                si, sv = load_down(dh)
                for ti, (t0, rows) in enumerate(TT):
                    for cg in range(2):
                        down_group(si, sv, dh, cg, ti, t0, rows)
        else:
            dl = [load_down(0), load_down(1)]
            for ti, (t0, rows) in enumerate(TT):
                for dh in range(2):
                    for cg in range(2):
                        down_group(dl[dh][0], dl[dh][1], dh, cg, ti, t0, rows)
                ln2.push(ti, rows, t0)

    ln2.flush()


def _mult(delta):
    delta = np.asarray(delta)
    m = ((delta >= 0) & (delta <= 128)).astype(np.float32)
    m += ((delta >= 0) & (delta <= 512) & (delta % 4 == 0)).astype(np.float32)
    m += ((delta >= 0) & (delta <= 2048) & (delta % 16 == 0)).astype(np.float32)
    return m


def _masks():
    p = np.arange(128)[:, None]
    c = np.arange(128)[None, :]
    mown = np.zeros((128, 8, 128), np.float32)
    for dl in range(8):
        mown[:, 7 - dl, :] = _mult(dl * 128 + c - p)
    mctx = np.zeros((128, 15, 128), np.float32)
    for dl in range(1, 16):
        mctx[:, 15 - dl, :] = _mult(dl * 128 + c - p)
    msam = np.zeros((128, 17, 8), np.float32)
    q = np.arange(8)[None, :]
    for j in range(16):
        msam[:, j, :] = _mult(2048 + q - (j * 128 + p))
    msam[:, 16, :] = _mult(q - p) * (p < 8)
    return mown.reshape(128, -1), mctx.reshape(128, -1), msam.reshape(128, -1)


def kernel(x_prompt, x_sample, cache_k, cache_v, state_conv, w_in, w_dw, b_dw, ln_conv_g, ln_conv_b,
           w_out, ln1_g, ln1_b, w_up, w_down, ln2_g, ln2_b):
    f = lambda a: np.ascontiguousarray(np.asarray(a, dtype=np.float32))
    x_prompt, x_sample = f(x_prompt), f(x_sample)
    cache_k, cache_v, state_conv = f(cache_k), f(cache_v), f(state_conv)
    w_in0, w_out0, w_up0, w_down0 = f(w_in)[0], f(w_out)[0], f(w_up)[0], f(w_down)[0]
    pc = np.concatenate([f(w_dw)[0].T, f(b_dw)[0][:, None], f(ln_conv_g)[0][:, None], f(ln_conv_b)[0][:, None]],
                        axis=1)
    pch = np.ascontiguousarray(pc.reshape(8, 128, 34).transpose(1, 0, 2))
    lngb = np.stack([f(ln1_g)[0], f(ln1_b)[0], f(ln2_g)[0], f(ln2_b)[0]])
    lngb = np.ascontiguousarray(np.broadcast_to(lngb[None], (128, 4, D)))
    mown, mctx, msam = _masks()
    ident = np.eye(128, dtype=np.float32)
    ln1T = np.ascontiguousarray(np.stack([f(ln1_g)[0].reshape(16, 128).T, f(ln1_b)[0].reshape(16, 128).T], axis=1))

    in_maps = []
    for c in range(8):
        b, half = c // 2, c % 2
        own = x_prompt[b, half * NP:(half + 1) * NP]
        xs = x_sample[c]
        xres = np.concatenate([own, xs], axis=0)
        xT = np.ascontiguousarray(xres.T)
        if half == 1:
            ctx = x_prompt[b, 0:NP]
            xcT = np.ascontiguousarray(ctx.T)
            mc = mctx
        else:
            xcT = np.zeros((D, NP), np.float32)
            mc = np.zeros_like(mctx)
        ck = cache_k[0, c].reshape(2048, 1024)
        cv = cache_v[0, c].reshape(2048, 1024)
        in_maps.append(dict(
            xT=xT, xcT=xcT, xc32=np.ascontiguousarray(xcT[:, NP - 32:NP]), xres=xres,
            w_in=w_in0, w_out=w_out0, w_up=w_up0, w_down=w_down0, pch=pch, lngb=lngb,
            mown=mown, mctx=mc, msam=msam, ident=ident, ln1T=ln1T,
            ckT=np.ascontiguousarray(ck.T), ck=ck, cv=cv,
            scT=np.ascontiguousarray(state_conv[0, c].T), sc=state_conv[0, c],
        ))
    nc = build_program()
    res = run_bass_kernel_spmd(nc, in_maps, core_ids=list(range(8)))
    R = res.results
    y_prompt = np.zeros((4, 2048, D), np.float32)
    y_sample = np.zeros((8, 8, D), np.float32)
    kp = np.zeros((1, 4, 2048, 8, 128), np.float32)
    vp = np.zeros((1, 4, 2048, 8, 128), np.float32)
    cp = np.zeros((1, 4, 30, 1024), np.float32)
    ksw = np.zeros((1, 8, 2048, 8, 128), np.float32)
    vsw = np.zeros((1, 8, 2048, 8, 128), np.float32)
    cs = np.zeros((1, 8, 30, 1024), np.float32)
    for c in range(8):
        b, half = c // 2, c % 2
        r = R[c]
        y = np.asarray(r["y"])
        y_prompt[b, half * NP:(half + 1) * NP] = y[:NP]
        y_sample[c] = y[NP:NT]
        kp[0, b, half * NP:(half + 1) * NP] = np.asarray(r["kout"]).reshape(NP, 8, 128)
        vp[0, b, half * NP:(half + 1) * NP] = np.asarray(r["vout"]).reshape(NP, 8, 128)
        if half == 1:
            cp[0, b] = np.asarray(r["convp"])
        ksw[0, c] = np.asarray(r["ks"]).reshape(2048, 8, 128)
        vsw[0, c] = np.asarray(r["vs"]).reshape(2048, 8, 128)
        cs[0, c] = np.asarray(r["convs"])
    return (y_prompt, y_sample, kp, vp, cp, ksw, vsw, cs)
```

```python
import numpy as np
import concourse.bass as bass
import concourse.mybir as mybir
from concourse.bass_utils import run_bass_kernel_spmd

F32 = mybir.dt.float32
BF16 = mybir.dt.bfloat16
AF = mybir.ActivationFunctionType
ALU = mybir.AluOpType

D = 2048
NP = 1024
NS = 8
NT = NP + NS
DIN = 5120
DFF = 8192
ALPHA = float(2.0 ** 0.25)
EPS = 1e-5
SCALE = float(128.0 ** -0.5)
TB = [(0, 512), (512, 512), (1024, 8)]
TT = [(i * 128, 128) for i in range(8)] + [(1024, 8)]
NSLOT = 2
STOP = 99
GLIMIT = 10 ** 9


class _Stop(Exception):
    pass
SLOT_BYTES = 16384


class Tk:
    def __init__(self, name, inherit=None, strict=False):
        self.name = name
        self.strict = strict
        self.w = None
        self.r = {}
        self.dsem = None
        self.dcnt = 0
        if inherit:
            for d in inherit:
                self._addr(d)

    def _addr(self, d):
        k = d[0]
        if k not in self.r or self.r[k][2] < d[2]:
            self.r[k] = d

    def alldeps(self):
        out = list(self.r.values())
        if self.w is not None:
            out.append(self.w)
        return out


class KB:
    def __init__(self, nc):
        self.nc = nc
        self.semkey = 0
        self.eng = {}
        for name, h in (("pe", nc.tensor), ("act", nc.scalar), ("dve", nc.vector),
                        ("pool", nc.gpsimd), ("sp", nc.sync)):
            self.eng[name] = dict(h=h, sem=nc.alloc_semaphore(name="sem_" + name), cnt=0,
                                  waited={}, key=self._newkey())
        self.final = []
        self.nsem = 5

    def _newkey(self):
        self.semkey += 1
        return self.semkey

    def _collect(self, engname, reads, writes):
        deps = {}

        def add(d, skip_same):
            if d is None:
                return
            if d[3] == engname and (skip_same or engname == "pe"):
                return
            k = d[0]
            if k not in deps or deps[k][2] < d[2]:
                deps[k] = d

        for t in reads:
            add(t.w, False)
        for t in writes:
            add(t.w, not t.strict)
            for d in t.r.values():
                add(d, not t.strict)
        return deps

    def _wait(self, engname, deps):
        e = self.eng[engname]
        for k, d in deps.items():
            if e["waited"].get(k, 0) < d[2]:
                e["h"].wait_ge(d[1], d[2])
                e["waited"][k] = d[2]

    def emit(self, engname, fn, reads=(), writes=(), signal=True):
        e = self.eng[engname]
        self._wait(engname, self._collect(engname, reads, writes))
        inst = fn()
        if signal:
            e["cnt"] += 1
            inst.then_inc(e["sem"], 1)
            val = e["cnt"]
        else:
            val = e["cnt"] + 1
        d = (e["key"], e["sem"], val, engname)
        for t in reads:
            t._addr(d)
        for t in writes:
            t.w = d
            t.r = {}
        return inst

    def dma(self, queue, pairs, reads=(), writes=(), owner=None, final=False, after=()):
        e = self.eng[queue]
        deps = self._collect(queue + "_dma", reads, writes)
        for d in after:
            if d is not None and (d[0] not in deps or deps[d[0]][2] < d[2]):
                deps[d[0]] = d
        self._wait(queue, deps)
        if owner is None:
            owner = writes[0] if writes else reads[0]
        if owner.dsem is None:
            owner.dsem = self.nc.alloc_semaphore(name="dsem_%s_%d" % (owner.name, self.nsem))
            owner.dkey = self._newkey()
            self.nsem += 1
        for (o, i) in pairs:
            e["h"].dma_start(out=o, in_=i).then_inc(owner.dsem, 16)
            owner.dcnt += 1
        d = (owner.dkey, owner.dsem, 16 * owner.dcnt, "dma")
        for t in reads:
            t._addr(d)
        for t in writes:
            t.w = d
            t.r = {}
        if final:
            self.final.append(d)

    def finish(self):
        deps = {}
        for d in self.final:
            if d[0] not in deps or deps[d[0]][2] < d[2]:
                deps[d[0]] = d
        for k, d in deps.items():
            self.eng["sp"]["h"].wait_ge(d[1], d[2])


class Rot:
    def __init__(self, items):
        self.items = items
        self.i = 0

    def next(self):
        x = self.items[self.i % len(self.items)]
        self.i += 1
        return x


def build_program():
    nc = bass.Bass("TRN2", target_bir_lowering=False)
    kb = KB(nc)
    try:
        _build(nc, kb)
    except _Stop:
        pass
    kb.finish()
    return nc


def _build(nc, kb):

    def din(name, shape):
        return nc.dram_tensor(name, list(shape), F32, kind="ExternalInput").ap()

    def dout(name, shape):
        return nc.dram_tensor(name, list(shape), F32, kind="ExternalOutput").ap()

    xT_d = din("xT", [D, NT])
    xcT_d = din("xcT", [D, NP])
    xc32_d = din("xc32", [D, 32])
    xres_d = din("xres", [NT, D])
    w_in_d = din("w_in", [D, DIN])
    w_out_d = din("w_out", [D, D])
    w_up_d = din("w_up", [D, DFF])
    w_down_d = din("w_down", [DFF, D])
    pch_d = din("pch", [128, 8, 34])
    lngb_d = din("lngb", [128, 4, D])
    mown_d = din("mown", [128, 8 * 128])
    mctx_d = din("mctx", [128, 15 * 128])
    msam_d = din("msam", [128, 17 * 8])
    ident_d = din("ident", [128, 128])
    ln1T_d = din("ln1T", [128, 2, 16])
    ckT_d = din("ckT", [1024, 2048])
    ck_d = din("ck", [2048, 1024])
    cv_d = din("cv", [2048, 1024])
    scT_d = din("scT", [1024, 30])
    sc_d = din("sc", [30, 1024])

    y_d = dout("y", [NT, D])
    kout_d = dout("kout", [NP, 1024])
    vout_d = dout("vout", [NP, 1024])
    convp_d = dout("convp", [30, 1024])
    ks_d = dout("ks", [2048, 1024])
    vs_d = dout("vs", [2048, 1024])
    convs_d = dout("convs", [30, 1024])

    base = (nc.sbuf_base + 31) // 32 * 32
    cur = [base]
    cnt = [0]

    def region(nbytes):
        o = cur[0]
        cur[0] += (nbytes + 31) // 32 * 32
        return o

    def at(off, shape, dt):
        cnt[0] += 1
        return nc.alloc_sbuf_tensor_at("sb%d" % cnt[0], list(shape), dt, offset=off)

    W_OFF = region(NSLOT * SLOT_BYTES)
    BIG = region(73728)
    RX = region(33280)
    RQ = region(17664)
    RM = region(33024)
    TMP = region(13312)
    SM = region(8192)
    assert cur[0] <= nc.sbuf_top, (cur[0], nc.sbuf_top)

    slots = [at(W_OFF + i * SLOT_BYTES, [128, 8192], BF16) for i in range(NSLOT)]
    slot_t = [Tk("slot%d" % i) for i in range(NSLOT)]
    slot_rot = Rot(list(range(NSLOT)))

    so = [SM]

    def small(shape, dt, nbytes):
        t = at(so[0], shape, dt)
        so[0] += (nbytes + 31) // 32 * 32
        return t

    ident_b = small([128, 128], BF16, 256); ident_b_t = Tk("identb")
    ident_f = small([128, 128], F32, 512); ident_f_t = Tk("identf")
    ones_f = small([128, 128], F32, 512); ones_t = Tk("ones")
    pch = small([128, 8, 34], F32, 1088); pch_t = Tk("pch")
    eps_c = small([128, 1], F32, 4); eps_t = Tk("eps")
    xc32_b = small([128, 16, 32], BF16, 1024); xc32_t = Tk("xc32")
    u32 = small([128, 8, 40], F32, 1280); u32_t = Tk("u32")
    KsT = small([128, 8, 8], BF16, 128); KsT_t = Tk("KsT")
    Vs_new = small([8, 8, 129], BF16, 2064 + 16); Vs_t = Tk("Vsnew")
    st6 = [small([128, 4, 6], F32, 96) for _ in range(4)]
    st6_t = [Tk("st6_%d" % i, strict=True) for i in range(4)]
    mv = [small([128, 2], F32, 8) for _ in range(4)]
    mv_t = [Tk("mv%d" % i, strict=True) for i in range(4)]
    sd = [small([128, 2], F32, 8) for _ in range(4)]
    sd_t = [Tk("sd%d" % i, strict=True) for i in range(4)]
    nm = [small([128, 1], F32, 4) for _ in range(4)]
    nm_t = [Tk("nm%d" % i, strict=True) for i in range(4)]
    ln1T = small([128, 2, 16], F32, 128); ln1T_t = Tk("ln1T")
    sd9 = small([128, 9, 2], F32, 72)
    nm9 = small([128, 9], F32, 36)
    sd9_t = [Tk("sd9_%d" % i, strict=True) for i in range(9)]
    nm9_t = [Tk("nm9_%d" % i, strict=True) for i in range(9)]
    rc = [small([128, 1], F32, 4) for _ in range(4)]
    rc_t = [Tk("rc%d" % i, strict=True) for i in range(4)]
    assert so[0] <= SM + 8192, so[0]

    banks = [nc.alloc_psum_tensor("psb%d" % i, [128, 512], F32) for i in range(8)]
    bank_t = [Tk("bank%d" % i) for i in range(8)]

    pe, act, dve, pool = nc.tensor, nc.scalar, nc.vector, nc.gpsimd

    kb.dma("sp", [(ident_f.ap(), ident_d)], writes=[ident_f_t])
    kb.dma("pool", [(ident_b.ap(), ident_d)], writes=[ident_b_t])
    kb.dma("sp", [(pch.ap(), pch_d)], writes=[pch_t])
    kb.dma("sp", [(ln1T.ap(), ln1T_d)], writes=[ln1T_t])
    kb.emit("dve", lambda: dve.memset(ones_f.ap(), 1.0), writes=[ones_t])
    kb.emit("dve", lambda: dve.memset(eps_c.ap(), EPS), writes=[eps_t])
    kb.emit("dve", lambda: dve.memset(Vs_new.ap()[:, :, 128:129], 1.0), writes=[Vs_t])

    copy_t = Tk("dramcopy")

    def cache_copies_part(i):
        prs = [(dst[i * 255:(i + 1) * 255, :], src[8 + i * 255: 8 + (i + 1) * 255, :])
               for (dst, src) in ((ks_d, ck_d), (vs_d, cv_d))]
        if i == 0:
            prs.append((convs_d[0:22, :], sc_d[8:30, :]))
        kb.dma("sp", prs, owner=copy_t, final=True)

    KT = at(BIG, [128, 8, 2048], BF16); KT_t = Tk("KT")
    Vall = at(BIG + 32768, [128, 16, 8, 129], BF16)
    V_t = [Tk("V%d" % i) for i in range(16)]
    mown = at(BIG + 65792, [128, 8 * 128], BF16); mown_t = Tk("mown")
    mctx = at(BIG + 67840, [128, 15 * 128], BF16); mctx_t = Tk("mctx")
    msam = at(BIG + 71680, [128, 17 * 8], BF16); msam_t = Tk("msam")
    kb.emit("dve", lambda: dve.memset(Vall.ap()[:, :, :, 128:129], 1.0), writes=V_t)

    xcT_b = at(RM, [128, 16, NP], BF16); xcT_t = [Tk("xcT0"), Tk("xcT1")]
    xcv = xcT_d.rearrange("(c p) n -> p c n", p=128)

    w_in_v = w_in_d.rearrange("(c p) n -> p c n", p=128)
    w_out_v = w_out_d.rearrange("(c p) n -> p c n", p=128)
    w_up_v = w_up_d.rearrange("(c p) n -> p c n", p=128)
    w_down_v = w_down_d.rearrange("(c p) n -> p c n", p=128)

    def load_slot(pairs_fn):
        si = slot_rot.next()
        sv = slots[si].ap()
        kb.dma("pool", pairs_fn(sv), writes=[slot_t[si]])
        return si

    def load_w512(view, col0):
        def f(sv):
            s3 = sv.rearrange("p (c n) -> p c n", c=16)
            return [(s3[:, 0:8, :], view[:, 0:8, col0:col0 + 512]),
                    (s3[:, 8:16, :], view[:, 8:16, col0:col0 + 512])]
        si = load_slot(f)
        return si, slots[si].ap().rearrange("p (c n) -> p c n", c=16)

    mm_rot = Rot([0, 1, 2, 3, 4, 5])

    def add_slot(off, inherit):
        slots.append(at(off, [128, 8192], BF16))
        slot_t.append(Tk("slotx%d" % len(slots), inherit=inherit))
        slot_rot.items = list(range(len(slots)))

    def drop_slots():
        dead = sum([t.alldeps() for t in slot_t[NSLOT:]], [])
        del slots[NSLOT:]
        del slot_t[NSLOT:]
        slot_rot.items = list(range(NSLOT))
        return dead


    gcount = [0]

    def group(bank_i, out_ap, pairs, extra_reads):
        gcount[0] += 1
        if gcount[0] > GLIMIT:
            raise _Stop()
        n = len(pairs)
        for i, (l, r) in enumerate(pairs):
            kb.emit("pe", lambda l=l, r=r, i=i: pe.matmul(out_ap, l, r, start=(i == 0), stop=(i == n - 1)),
                    reads=extra_reads, writes=[bank_t[bank_i]], signal=(i == n - 1))

    if STOP <= 0:
        return
    pre_si = slot_rot.next()
    pre_sv = slots[pre_si].ap().rearrange("p (c n) -> p c n", c=16)
    pre_ct_t = [Tk("pre_ct%d" % i) for i in range(4)]

    def pre_piece(ct):
        kb.dma("pool", [(pre_sv[:, :, ct * 128:(ct + 1) * 128], w_in_v[:, :, 1024 + ct * 128:1024 + (ct + 1) * 128])],
               writes=[pre_ct_t[ct]])

    def xc_half(hf):
        kb.dma("pool", [(xcT_b.ap()[:, 0:8, hf * 512:(hf + 1) * 512], xcv[:, 0:8, hf * 512:(hf + 1) * 512]),
                        (xcT_b.ap()[:, 8:16, hf * 512:(hf + 1) * 512], xcv[:, 8:16, hf * 512:(hf + 1) * 512])],
               writes=[xcT_t[hf]])

    pre_piece(0)
    xc_half(0)
    pre_piece(1)
    xc_half(1)
    pre_piece(2)
    pre_piece(3)
    pre_blk = (pre_si, pre_sv)
    kb.dma("pool", [(mown.ap(), mown_d)], writes=[mown_t])
    kb.dma("pool", [(mctx.ap(), mctx_d)], writes=[mctx_t])
    kb.dma("pool", [(msam.ap(), msam_d)], writes=[msam_t])
    kb.dma("pool", [(xc32_b.ap(), xc32_d.rearrange("(c p) n -> p c n", p=128))], writes=[xc32_t])
    xT_b = at(RX, [128, 16, NT], BF16); xT_t = Tk("xT")
    xv = xT_d.rearrange("(c p) n -> p c n", p=128)

    def load_xT():
        kb.dma("pool", [(xT_b.ap()[:, 0:8, :], xv[:, 0:8, :]), (xT_b.ap()[:, 8:16, :], xv[:, 8:16, :])],
               writes=[xT_t])
    xT_loaded = [False]

    for (kind, col0) in (("k", 1024), ("k", 1536), ("v", 2048), ("v", 2560)):
        si, sv = pre_blk if col0 == 1024 else load_w512(w_in_v, col0)
        if col0 == 1536:
            load_xT()
        if kind == "k":
            for ct in range(4):
                head = (col0 - 1024) // 128 + ct
                for (t0, n) in ((0, 512), (512, 512)):
                    bi = mm_rot.next()
                    wt = pre_ct_t[ct] if col0 == 1024 else slot_t[si]
                    group(bi, banks[bi].ap()[:, :n],
                          [(sv[:, c, ct * 128:(ct + 1) * 128], xcT_b.ap()[:, c, t0:t0 + n]) for c in range(16)],
                          [wt, xcT_t[t0 // 512]])
                    kb.emit("act", lambda bi=bi, head=head, t0=t0, n=n: act.copy(
                        out=KT.ap()[:, head, t0:t0 + n], in_=banks[bi].ap()[:, :n]),
                        reads=[bank_t[bi]], writes=[KT_t])
            if col0 == 1024:
                for t_ in pre_ct_t:
                    for d in t_.alldeps():
                        slot_t[si]._addr(d)
        else:
            h0 = (col0 - 2048) // 128
            for tt in range(8):
                bi = mm_rot.next()
                group(bi, banks[bi].ap()[:, :512],
                      [(xcT_b.ap()[:, c, tt * 128:(tt + 1) * 128], sv[:, c, :]) for c in range(16)],
                      [slot_t[si], xcT_t[tt // 4]])
                kb.emit("dve", lambda bi=bi, tt=tt, h0=h0: dve.tensor_copy(
                    out=Vall.ap()[:, tt, h0:h0 + 4, 0:128],
                    in_=banks[bi].ap().rearrange("p (h d) -> p h d", h=4)),
                    reads=[bank_t[bi]], writes=[V_t[tt]])

    if STOP <= 1:
        return
    QT = at(RQ, [128, 8, NT], BF16); QT_t = Tk("QT")
    kvst = [at(TMP + i * 2048, [128, 512], F32) for i in range(4)]
    kvst_t = [Tk("kvst%d" % i) for i in range(4)]
    kvst_rot = Rot(list(range(4)))
    kbf = [at(TMP + 8192 + i * 1024, [128, 512], BF16) for i in range(4)]
    kbf_t = [Tk("kbf%d" % i) for i in range(4)]
    kbf_rot = Rot(list(range(4)))
    ktr_rot = Rot([6, 7])

    for (kind, col0) in (("k", 1024), ("k", 1536), ("v", 2048), ("v", 2560), ("q", 0), ("q", 512)):
        si, sv = load_w512(w_in_v, col0)
        if kind == "q":
            for ct in range(4):
                head = (col0 % 1024) // 128 + ct
                for (t0, n) in TB:
                    bi = mm_rot.next()
                    group(bi, banks[bi].ap()[:, :n],
                          [(sv[:, c, ct * 128:(ct + 1) * 128], xT_b.ap()[:, c, t0:t0 + n]) for c in range(16)],
                          [slot_t[si], xT_t])
                    if kind == "q":
                        dst, dt_ = QT.ap()[:, head, t0:t0 + n], QT_t
                    elif t0 < NP:
                        dst, dt_ = KT.ap()[:, head, NP + t0:NP + t0 + n], KT_t
                    else:
                        dst, dt_ = KsT.ap()[:, head, 0:8], KsT_t
                    kb.emit("act", lambda bi=bi, dst=dst, n=n: act.copy(out=dst, in_=banks[bi].ap()[:, :n]),
                            reads=[bank_t[bi]], writes=[dt_])
        if kind in ("k", "v"):
            cb = (col0 % 1024)
            h0 = cb // 128
            pend = []

            def k_transposes(ti, t0, rows, qi, h0=h0):
                tb_ = ktr_rot.next()
                trv = banks[tb_].ap().bitcast(BF16)
                for hh in range(4):
                    kb.emit("pe", lambda hh=hh: pe.transpose(trv[:, hh * 128:hh * 128 + rows],
                                                             kbf[qi].ap()[:rows, hh * 128:(hh + 1) * 128],
                                                             ident_b.ap()[:rows, :rows]),
                            reads=[kbf_t[qi], ident_b_t], writes=[bank_t[tb_]], signal=(hh == 3))
                src = trv[:, 0:512].rearrange("p (h q) -> p h q", h=4)[:, :, :rows]
                if ti < 8:
                    dst, dt_ = KT.ap()[:, h0:h0 + 4, NP + t0:NP + t0 + rows], KT_t
                else:
                    dst, dt_ = KsT.ap()[:, h0:h0 + 4, 0:8], KsT_t
                kb.emit("act", lambda: act.copy(out=dst, in_=src), reads=[bank_t[tb_]], writes=[dt_])

            for ti, (t0, rows) in enumerate(TT):
                bi = mm_rot.next()
                group(bi, banks[bi].ap()[:rows, :512],
                      [(xT_b.ap()[:, c, t0:t0 + rows], sv[:, c, :]) for c in range(16)],
                      [slot_t[si], xT_t])
                if kind == "v":
                    if ti < 8:
                        dst, dt_ = Vall.ap()[:, 8 + ti, h0:h0 + 4, 0:128], V_t[8 + ti]
                    else:
                        dst, dt_ = Vs_new.ap()[0:8, h0:h0 + 4, 0:128], Vs_t
                ki = kvst_rot.next()
                kb.emit("act", lambda bi=bi, ki=ki, rows=rows: act.copy(
                    out=kvst[ki].ap()[:rows, :], in_=banks[bi].ap()[:rows, :]),
                    reads=[bank_t[bi]], writes=[kvst_t[ki]])
                if kind == "v":
                    kb.emit("dve", lambda ki=ki, dst=dst, rows=rows: dve.tensor_copy(
                        out=dst, in_=kvst[ki].ap()[:rows, :].rearrange("p (h d) -> p h d", h=4)),
                        reads=[kvst_t[ki]], writes=[dt_])
                if ti < 8:
                    od = (kout_d if kind == "k" else vout_d)[t0:t0 + 128, cb:cb + 512]
                else:
                    od = (ks_d if kind == "k" else vs_d)[2040:2048, cb:cb + 512]
                kb.dma("sp", [(od, kvst[ki].ap()[:rows, :])], reads=[kvst_t[ki]], final=True)
                if kind == "k":
                    qi = kbf_rot.next()
                    kb.emit("dve", lambda ki=ki, qi=qi, rows=rows: dve.tensor_copy(
                        out=kbf[qi].ap()[:rows, :], in_=kvst[ki].ap()[:rows, :]),
                        reads=[kvst_t[ki]], writes=[kbf_t[qi]])
                    pend.append((ti, t0, rows, qi))
                    if len(pend) > 2:
                        k_transposes(*pend.pop(0))
            while pend:
                k_transposes(*pend.pop(0))

    if STOP <= 2:
        return
    mixed = at(RM, [128, 16, NT], BF16); mixed_t = Tk("mixed", inherit=drop_slots() + xcT_t[0].alldeps() + xcT_t[1].alldeps())
    EP = [at(TMP + 6656 + i * 1024, [128, 512], BF16) for i in range(6)]
    EP_t = [Tk("EP%d" % i, inherit=sum([k.alldeps() for k in kvst_t + kbf_t], [])) for i in range(6)]
    EP_rot = Rot(list(range(6)))
    ao = [at(TMP + i * 2048, [128, 1024], BF16) for i in range(2)]
    ao_t = [Tk("ao%d" % i, inherit=sum([k.alldeps() for k in kvst_t + kbf_t], [])) for i in range(2)]
    S_rot = Rot([0, 1, 2, 3])
    O_banks = [4, 5]
    TR_bank = 6

    units = []
    for t in range(8):
        g = 8 + t
        for h in range(8):
            groups = [(0, 3, "c"), (4, 7, "c"), (8, min(11, g), "o")]
            if g >= 12:
                groups.append((12, g, "o"))
            for gi, (j0, j1, kind) in enumerate(groups):
                units.append(dict(t=t, g=g, h=h, j0=j0, j1=j1, kind=kind, first=(gi == 0),
                                  last=(gi == len(groups) - 1)))

    def emit_S(u):
        bi = S_rot.next()
        u["sb"] = bi
        n = u["j1"] - u["j0"] + 1
        t, h = u["t"], u["h"]
        for jj in range(n):
            j = u["j0"] + jj
            kb.emit("pe", lambda jj=jj, j=j: pe.matmul(
                banks[bi].ap()[:, jj * 128:(jj + 1) * 128], KT.ap()[:, h, j * 128:(j + 1) * 128],
                QT.ap()[:, h, t * 128:(t + 1) * 128], start=True, stop=True),
                reads=[KT_t, QT_t], writes=[bank_t[bi]], signal=(jj == n - 1))
        ei = EP_rot.next()
        u["ep"] = ei
        kb.emit("act", lambda: act.activation(out=EP[ei].ap()[:, :n * 128], in_=banks[bi].ap()[:, :n * 128],
                                              func=AF.Exp, scale=SCALE),
                reads=[bank_t[bi]], writes=[EP_t[ei]])
        if u["kind"] == "o":
            m0 = 7 - u["g"] + u["j0"]
            msk, mt = mown.ap()[:, m0 * 128:(m0 + n) * 128], mown_t
        else:
            m0 = 15 - u["g"] + u["j0"]
            msk, mt = mctx.ap()[:, m0 * 128:(m0 + n) * 128], mctx_t
        kb.emit("dve", lambda: dve.tensor_tensor(out=EP[ei].ap()[:, :n * 128], in0=EP[ei].ap()[:, :n * 128],
                                                 in1=msk, op=ALU.mult),
                reads=[EP_t[ei], mt], writes=[EP_t[ei]])

    def emit_PV(u):
        t, h = u["t"], u["h"]
        ob = O_banks[(t * 8 + h) % 2]
        n = u["j1"] - u["j0"] + 1
        ei = u["ep"]
        for jj in range(n):
            j = u["j0"] + jj
            kb.emit("pe", lambda jj=jj, j=j: pe.matmul(
                banks[ob].ap()[:, 0:129], EP[ei].ap()[:, jj * 128:(jj + 1) * 128], Vall.ap()[:, j, h, :],
                start=(u["first"] and jj == 0), stop=(u["last"] and jj == n - 1)),
                reads=[EP_t[ei], V_t[j]], writes=[bank_t[ob]], signal=(jj == n - 1))
        if u["last"]:
            ri = (t * 8 + h) % 4
            kb.emit("dve", lambda: dve.reciprocal(out=rc[ri].ap(), in_=banks[ob].ap()[:, 128:129]),
                    reads=[bank_t[ob]], writes=[rc_t[ri]])
            kb.emit("dve", lambda: dve.tensor_scalar(out=ao[t % 2].ap()[:, h * 128:(h + 1) * 128],
                                                     in0=banks[ob].ap()[:, 0:128], scalar1=rc[ri].ap()[:, 0:1],
                                                     scalar2=None, op0=ALU.mult),
                    reads=[bank_t[ob], rc_t[ri]], writes=[ao_t[t % 2]])
            if h == 7:
                trv = banks[TR_bank].ap().bitcast(BF16)
                for hh in range(8):
                    kb.emit("pe", lambda hh=hh: pe.transpose(trv[:, hh * 128:(hh + 1) * 128],
                                                             ao[t % 2].ap()[:, hh * 128:(hh + 1) * 128],
                                                             ident_b.ap()),
                            reads=[ao_t[t % 2], ident_b_t], writes=[bank_t[TR_bank]], signal=(hh == 7))
                kb.emit("act", lambda: act.copy(out=mixed.ap()[:, 0:8, t * 128:(t + 1) * 128],
                                                in_=trv.rearrange("p (h q) -> p h q", h=8)),
                        reads=[bank_t[TR_bank]], writes=[mixed_t])

    def prompt_attention():
        LAG = 4
        sample_load(0)
        for i, u in enumerate(units):
            emit_S(u)
            if i >= LAG:
                emit_PV(units[i - LAG])
            if u["h"] == 7 and u["last"]:
                sample_head(u["t"])
                if u["t"] < 7:
                    sample_load(u["t"] + 1)
        for u in units[len(units) - LAG:]:
            emit_PV(u)

    kc = at(W_OFF + SLOT_BYTES, [128, 2048], BF16)
    vc = at(W_OFF + SLOT_BYTES + 4096, [128, 16, 129], BF16)
    kv_t = slot_t[1]
    kb.emit("dve", lambda: dve.memset(vc.ap()[:, :, 128:129], 1.0), writes=[kv_t])
    Es = at(TMP + 4096, [128, 144], BF16)
    Es_t = Tk("Es", inherit=sum([k.alldeps() for k in kvst_t + kbf_t], []))
    ao_s = at(TMP + 4416, [8, 1024], BF16)
    ao_s_t = Tk("aos", inherit=sum([k.alldeps() for k in kvst_t + kbf_t], []))
    cv_v = cv_d.rearrange("(t p) (h d) -> p t h d", p=128, h=8)
    SO_bank = 7

    def sample_load(h):
        kb.dma("pool", [(kc.ap(), ckT_d[h * 128:(h + 1) * 128, :]),
                        (vc.ap()[:, :, 0:128], cv_v[:, :, h, :])], writes=[kv_t])

    def sample_head(h):
        bi = S_rot.next()
        for j in range(16):
            kb.emit("pe", lambda j=j: pe.matmul(banks[bi].ap()[:, j * 8:(j + 1) * 8],
                                                kc.ap()[:, j * 128:(j + 1) * 128],
                                                QT.ap()[:, h, NP:NT], start=True, stop=True),
                    reads=[kv_t, QT_t], writes=[bank_t[bi]], signal=False)
        kb.emit("pe", lambda: pe.matmul(banks[bi].ap()[0:8, 128:136], KsT.ap()[:, h, 0:8],
                                        QT.ap()[:, h, NP:NT], start=True, stop=True),
                reads=[KsT_t, QT_t], writes=[bank_t[bi]])
        kb.emit("act", lambda: act.activation(out=Es.ap()[:, 0:128], in_=banks[bi].ap()[:, 0:128],
                                              func=AF.Exp, scale=SCALE),
                reads=[bank_t[bi]], writes=[Es_t])
        kb.emit("act", lambda: act.activation(out=Es.ap()[0:8, 128:136], in_=banks[bi].ap()[0:8, 128:136],
                                              func=AF.Exp, scale=SCALE),
                reads=[bank_t[bi]], writes=[Es_t])
        kb.emit("dve", lambda: dve.tensor_tensor(out=Es.ap()[:, 0:128], in0=Es.ap()[:, 0:128],
                                                 in1=msam.ap()[:, 0:128], op=ALU.mult),
                reads=[Es_t, msam_t], writes=[Es_t])
        kb.emit("dve", lambda: dve.tensor_tensor(out=Es.ap()[0:8, 128:136], in0=Es.ap()[0:8, 128:136],
                                                 in1=msam.ap()[0:8, 128:136], op=ALU.mult),
                reads=[Es_t, msam_t], writes=[Es_t])
        ob = SO_bank
        for j in range(16):
            kb.emit("pe", lambda j=j: pe.matmul(banks[ob].ap()[0:8, 0:129], Es.ap()[:, j * 8:(j + 1) * 8],
                                                vc.ap()[:, j, :], start=(j == 0), stop=False),
                    reads=[Es_t, kv_t], writes=[bank_t[ob]], signal=False)
        kb.emit("pe", lambda: pe.matmul(banks[ob].ap()[0:8, 0:129], Es.ap()[0:8, 128:136],
                                        Vs_new.ap()[0:8, h, :], start=False, stop=True),
                reads=[Es_t, Vs_t], writes=[bank_t[ob]])
        ri = h % 4
        kb.emit("dve", lambda: dve.reciprocal(out=rc[ri].ap()[0:8, :], in_=banks[ob].ap()[0:8, 128:129]),
                reads=[bank_t[ob]], writes=[rc_t[ri]])
        kb.emit("dve", lambda: dve.tensor_scalar(out=ao_s.ap()[0:8, h * 128:(h + 1) * 128],
                                                 in0=banks[ob].ap()[0:8, 0:128], scalar1=rc[ri].ap()[0:8, 0:1],
                                                 scalar2=None, op0=ALU.mult),
                reads=[bank_t[ob], rc_t[ri]], writes=[ao_s_t])

    def sample_finish():
        trv = banks[TR_bank].ap().bitcast(BF16)
        for hh in range(8):
            kb.emit("pe", lambda hh=hh: pe.transpose(trv[:, hh * 8:(hh + 1) * 8],
                                                     ao_s.ap()[0:8, hh * 128:(hh + 1) * 128],
                                                     ident_b.ap()[0:8, 0:8]),
                    reads=[ao_s_t, ident_b_t], writes=[bank_t[TR_bank]], signal=(hh == 7))
        kb.emit("act", lambda: act.copy(out=mixed.ap()[:, 0:8, NP:NT],
                                        in_=trv[:, 0:64].rearrange("p (h q) -> p h q", h=8)),
                reads=[bank_t[TR_bank]], writes=[mixed_t])

    if STOP <= 3:
        return
    prompt_attention()
    sample_finish()
    if STOP <= 4:
        return
    u_bf = at(RQ, [128, 8, 1062], BF16)
    us_ext = at(RQ + 16992, [128, 8, 38], BF16)
    ubf_t = [Tk("ubf%d" % i, inherit=QT_t.alldeps()) for i in range(8)]
    us_t = Tk("usext", inherit=QT_t.alldeps())
    big_dead = sum([k.alldeps() for k in [KT_t, mown_t, mctx_t, msam_t] + V_t], [])
    diag = [at(BIG + i * 7936, [128, 31, 128], BF16) for i in range(2)]
    diag_t = [Tk("diag%d" % i, inherit=big_dead) for i in range(2)]
    sig = [at(BIG + 16384 + i * 2080, [128, 520], F32) for i in range(2)]
    sig_t = [Tk("sig%d" % i, inherit=big_dead) for i in range(2)]
    sig_rot = Rot([0, 1])
    sq = [at(BIG + 20736 + i * 2048, [128, 512], F32) for i in range(2)]
    sq_t = [Tk("sq%d" % i, inherit=big_dead) for i in range(2)]
    mean_s = at(BIG + 24832, [128, 512], F32); mean_t = Tk("mean", inherit=big_dead)
    var_s = at(BIG + 26880, [128, 512], F32); var_t = Tk("var", inherit=big_dead)
    rstd_s = at(BIG + 28928, [128, 512], F32); rstd_t = Tk("rstd", inherit=big_dead)
    t1 = [at(BIG + 30976 + i * 2048, [128, 512], F32) for i in range(2)]
    t1_t = [Tk("t1_%d" % i, inherit=big_dead) for i in range(2)]
    cst = at(BIG + 35072, [32, 1024], F32); cst_t = Tk("cst", inherit=big_dead)
    cst_s = at(BIG + 39168, [8, 1024], F32); csts_t = Tk("csts", inherit=big_dead)
    add_slot(BIG + 44032, big_dead)

    for blk in range(4):
        def f(sv, blk=blk):
            s3 = sv.rearrange("p (c n) -> p c n", c=16)
            return [(s3[:, :, 0:256], w_in_v[:, :, 3072 + 256 * blk: 3072 + 256 * blk + 256]),
                    (s3[:, :, 256:512], w_in_v[:, :, 4096 + 256 * blk: 4096 + 256 * blk + 256])]
        si = load_slot(f)
        sv = slots[si].ap().rearrange("p (c n) -> p c n", c=16)
        for ct in range(2):
            ch = 2 * blk + ct
            for (t0, n, src) in [(0, 512, "x"), (512, 512, "x"), (1024, 8, "x"), (0, 32, "c")]:
                ba, bg = mm_rot.next(), mm_rot.next()
                if src == "x":
                    rhs = lambda c: xT_b.ap()[:, c, t0:t0 + n]
                    rt = xT_t
                else:
                    rhs = lambda c: xc32_b.ap()[:, c, 0:32]
                    rt = xc32_t
                group(ba, banks[ba].ap()[:, :n],
                      [(sv[:, c, ct * 128:(ct + 1) * 128], rhs(c)) for c in range(16)], [slot_t[si], rt])
                group(bg, banks[bg].ap()[:, :n],
                      [(sv[:, c, 256 + ct * 128:256 + (ct + 1) * 128], rhs(c)) for c in range(16)],
                      [slot_t[si], rt])
                sgi = sig_rot.next()
                kb.emit("act", lambda: act.activation(out=sig[sgi].ap()[:, :n], in_=banks[bg].ap()[:, :n],
                                                      func=AF.Sigmoid),
                        reads=[bank_t[bg]], writes=[sig_t[sgi]])
                if src == "c":
                    kb.emit("dve", lambda: dve.tensor_tensor(out=u_bf.ap()[:, ch, 0:30], in0=banks[ba].ap()[:, 2:32],
                                                             in1=sig[sgi].ap()[:, 2:32], op=ALU.mult),
                            reads=[bank_t[ba], sig_t[sgi]], writes=[ubf_t[ch]])
                elif t0 < NP:
                    kb.emit("dve", lambda: dve.tensor_tensor(out=u_bf.ap()[:, ch, 30 + t0:30 + t0 + n],
                                                             in0=banks[ba].ap()[:, :n], in1=sig[sgi].ap()[:, :n],
                                                             op=ALU.mult),
                            reads=[bank_t[ba], sig_t[sgi]], writes=[ubf_t[ch]])
                    if t0 == 512:
                        kb.emit("dve", lambda: dve.tensor_tensor(out=u32.ap()[:, ch, 0:32],
                                                                 in0=banks[ba].ap()[:, 480:512],
                                                                 in1=sig[sgi].ap()[:, 480:512], op=ALU.mult),
                                reads=[bank_t[ba], sig_t[sgi]], writes=[u32_t])
                else:
                    kb.emit("dve", lambda: dve.tensor_tensor(out=u32.ap()[:, ch, 32:40], in0=banks[ba].ap()[:, :8],
                                                             in1=sig[sgi].ap()[:, :8], op=ALU.mult),
                            reads=[bank_t[ba], sig_t[sgi]], writes=[u32_t])
                    kb.emit("dve", lambda: dve.tensor_tensor(out=us_ext.ap()[:, ch, 30:38],
                                                             in0=banks[ba].ap()[:, :8],
                                                             in1=sig[sgi].ap()[:, :8], op=ALU.mult),
                            reads=[bank_t[ba], sig_t[sgi]], writes=[us_t])

    if STOP <= 5:
        return
    for ch in range(8):
        bi = 6 + ch // 4
        kb.emit("pe", lambda ch=ch, bi=bi: pe.transpose(banks[bi].ap()[0:32, (ch % 4) * 128:(ch % 4 + 1) * 128],
                                                        u32.ap()[:, ch, 0:32], ident_f.ap()),
                reads=[u32_t, ident_f_t], writes=[bank_t[bi]], signal=(ch % 4 == 3))
    for half in range(2):
        kb.emit("act", lambda half=half: act.copy(out=cst.ap()[0:32, half * 512:(half + 1) * 512],
                                                  in_=banks[6 + half].ap()[0:32, :]),
                reads=[bank_t[6 + half]], writes=[cst_t])
    kb.dma("sp", [(convp_d[0:30, :], cst.ap()[2:32, :])], reads=[cst_t], final=True)
    for ch in range(8):
        bi = 6 + ch // 4
        kb.emit("pe", lambda ch=ch, bi=bi: pe.transpose(banks[bi].ap()[0:8, (ch % 4) * 128:(ch % 4 + 1) * 128],
                                                        u32.ap()[:, ch, 32:40], ident_f.ap()),
                reads=[u32_t, ident_f_t], writes=[bank_t[bi]], signal=(ch % 4 == 3))
    for half in range(2):
        kb.emit("act", lambda half=half: act.copy(out=cst_s.ap()[0:8, half * 512:(half + 1) * 512],
                                                  in_=banks[6 + half].ap()[0:8, :]),
                reads=[bank_t[6 + half]], writes=[csts_t])
    kb.dma("sp", [(convs_d[22:30, :], cst_s.ap()[0:8, :])], reads=[csts_t], final=True)

    if STOP <= 6:
        return
    kb.dma("pool", [(us_ext.ap()[:, :, 0:30], scT_d.rearrange("(c p) j -> p c j", p=128))], writes=[us_t])
    cacc = at(RX, [128, 8, NT], F32); cacc_t = [Tk("cacc%d" % i, inherit=xT_t.alldeps()) for i in range(8)]
    for ch in range(8):
        dg = diag[ch % 2]
        dgt = diag_t[ch % 2]
        for j in range(31):
            kb.emit("dve", lambda j=j: dve.tensor_scalar(out=dg.ap()[:, j, :], in0=ident_b.ap(),
                                                         scalar1=pch.ap()[:, ch, j:j + 1], scalar2=None,
                                                         op0=ALU.mult),
                    reads=[ident_b_t, pch_t], writes=[dgt])
        for (t0, n) in TB:
            bi = mm_rot.next()
            if t0 < NP:
                prs = [(dg.ap()[:, j, :], u_bf.ap()[:, ch, t0 + j:t0 + j + n]) for j in range(31)]
                rr = [dgt, ubf_t[ch]]
            else:
                prs = [(dg.ap()[:, j, :], us_ext.ap()[:, ch, j:j + 8]) for j in range(31)]
                rr = [dgt, us_t]
            group(bi, banks[bi].ap()[:, :n], prs, rr)
            kb.emit("act", lambda bi=bi, t0=t0, n=n: act.activation(out=cacc.ap()[:, ch, t0:t0 + n],
                                                                    in_=banks[bi].ap()[:, :n], func=AF.Identity,
                                                                    bias=pch.ap()[:, ch, 31:32]),
                    reads=[bank_t[bi], pch_t], writes=[cacc_t[ch]])

    if STOP <= 7:
        return
    sq_rot = Rot([0, 1])
    t1_rot = Rot([0, 1])
    stat_sets = [(mean_s, var_s, rstd_s, mean_t, var_t, rstd_t)]
    m1 = at(BIG + 60416, [128, 512], F32); v1 = at(BIG + 62464, [128, 512], F32); r1 = at(BIG + 64512, [128, 512], F32)
    stat_sets.append((m1, v1, r1, Tk("mean1", inherit=big_dead), Tk("var1", inherit=big_dead),
                      Tk("rstd1", inherit=big_dead)))
    m2 = at(BIG + 66560, [128, 8], F32); v2 = at(BIG + 66592, [128, 8], F32); r2 = at(BIG + 66624, [128, 8], F32)
    stat_sets.append((m2, v2, r2, Tk("mean2", inherit=big_dead), Tk("var2", inherit=big_dead),
                      Tk("rstd2", inherit=big_dead)))
    stat_banks = [(6, 7), (0, 1), (2, 3)]

    def cln_stats(i):
        (t0, n) = TB[i]
        b1, b2_ = stat_banks[i]
        for ch in range(8):
            qi = sq_rot.next()
            kb.emit("act", lambda: act.activation(out=sq[qi].ap()[:, :n], in_=cacc.ap()[:, ch, t0:t0 + n],
                                                  func=AF.Square),
                    reads=[cacc_t[ch]], writes=[sq_t[qi]])
            kb.emit("pe", lambda: pe.matmul(banks[b1].ap()[:, :n], ones_f.ap(), cacc.ap()[:, ch, t0:t0 + n],
                                            start=(ch == 0), stop=(ch == 7)),
                    reads=[ones_t, cacc_t[ch]], writes=[bank_t[b1]])
            kb.emit("pe", lambda: pe.matmul(banks[b2_].ap()[:, :n], ones_f.ap(), sq[qi].ap()[:, :n],
                                            start=(ch == 0), stop=(ch == 7)),
                    reads=[ones_t, sq_t[qi]], writes=[bank_t[b2_]])

    def cln_chain(i):
        (t0, n) = TB[i]
        b1, b2_ = stat_banks[i]
        mean_x, var_x, rstd_x, mean_xt, var_xt, rstd_xt = stat_sets[i]
        kb.emit("act", lambda: act.mul(out=mean_x.ap()[:, :n], in_=banks[b1].ap()[:, :n], mul=1.0 / 1024.0),
                reads=[bank_t[b1]], writes=[mean_xt])
        kb.emit("dve", lambda: dve.tensor_tensor(out=var_x.ap()[:, :n], in0=mean_x.ap()[:, :n],
                                                 in1=mean_x.ap()[:, :n], op=ALU.mult),
                reads=[mean_xt], writes=[var_xt])
        kb.emit("dve", lambda: dve.scalar_tensor_tensor(out=var_x.ap()[:, :n], in0=banks[b2_].ap()[:, :n],
                                                        scalar=1.0 / 1024.0, in1=var_x.ap()[:, :n],
                                                        op0=ALU.mult, op1=ALU.subtract),
                reads=[bank_t[b2_], var_xt], writes=[var_xt])
        kb.emit("act", lambda: act.activation(out=rstd_x.ap()[:, :n], in_=var_x.ap()[:, :n], func=AF.Sqrt,
                                              bias=eps_c.ap()[:, 0:1]),
                reads=[var_xt, eps_t], writes=[rstd_xt])
        kb.emit("dve", lambda: dve.reciprocal(out=rstd_x.ap()[:, :n], in_=rstd_x.ap()[:, :n]),
                reads=[rstd_xt], writes=[rstd_xt])

    def cln_norm(i):
        (t0, n) = TB[i]
        mean_x, var_x, rstd_x, mean_xt, var_xt, rstd_xt = stat_sets[i]
        for ch in range(8):
            ti = t1_rot.next()
            kb.emit("dve", lambda: dve.tensor_tensor(out=t1[ti].ap()[:, :n], in0=cacc.ap()[:, ch, t0:t0 + n],
                                                     in1=mean_x.ap()[:, :n], op=ALU.subtract),
                    reads=[cacc_t[ch], mean_xt], writes=[t1_t[ti]])
            kb.emit("dve", lambda: dve.tensor_tensor(out=t1[ti].ap()[:, :n], in0=t1[ti].ap()[:, :n],
                                                     in1=rstd_x.ap()[:, :n], op=ALU.mult),
                    reads=[t1_t[ti], rstd_xt], writes=[t1_t[ti]])
            kb.emit("act", lambda: act.activation(out=mixed.ap()[:, 8 + ch, t0:t0 + n], in_=t1[ti].ap()[:, :n],
                                                  func=AF.Silu, bias=pch.ap()[:, ch, 33:34],
                                                  scale=pch.ap()[:, ch, 32:33]),
                    reads=[t1_t[ti], pch_t], writes=[mixed_t])

    cln_stats(2)
    cln_stats(0)
    cln_chain(2)
    cln_norm(2)
    cln_chain(0)
    cln_stats(1)
    cln_norm(0)
    cln_chain(1)
    cln_norm(1)

    if STOP <= 8:
        return
    big_dead2 = sum([k.alldeps() for k in diag_t + sig_t + sq_t + [mean_t, var_t, rstd_t, cst_t, csts_t] + t1_t
                     + [x for st in stat_sets[1:] for x in st[3:]]], [])
    big_dead2 = big_dead2 + drop_slots()
    add_slot(RQ, sum([k.alldeps() for k in ubf_t + [us_t]], []))
    acc = at(BIG, [128, 9, D], F32)
    acc_t = [Tk("acc%d" % i, inherit=big_dead + big_dead2) for i in range(9)]
    for ti, (t0, rows) in enumerate(TT):
        kb.dma("sp", [(acc.ap()[:rows, ti, :], xres_d[t0:t0 + rows, :])], writes=[acc_t[ti]])
    rx_dead2 = sum([k.alldeps() for k in cacc_t], [])
    hidden = at(RX, [128, 8, NT], BF16); hidden_t = Tk("hidden", inherit=rx_dead2)
    gb = at(RX + 16512, [128, 2, D], F32); gb_t = Tk("gb", inherit=rx_dead2)
    kb.dma("sp", [(gb.ap(), lngb_d[:, 0:2, :])], writes=[gb_t])
    hb = [at(TMP + i * 4096, [128, D], BF16) for i in range(2)]
    tmp_dead = sum([k.alldeps() for k in ao_t + EP_t + [Es_t, ao_s_t]], [])
    hb_t = [Tk("hb%d" % i, inherit=tmp_dead) for i in range(2)]

    class LNPipe:
        NPH = 5
        OFF = [0, 1, 2, 3, 3]

        def __init__(self, tail, gmul_eng):
            self.q = []
            self.step = 0
            self.tail = tail
            self.gmul_eng = gmul_eng

        def push(self, ti, rows, t0):
            self.q.append((ti, rows, t0))
            self._step()

        def flush(self):
            for _ in range(self.OFF[-1]):
                self._step()

        def _step(self):
            sidx = self.step
            self.step += 1
            for p in sorted(range(self.NPH), key=lambda p: (-self.OFF[p], p)):
                idx = sidx - self.OFF[p]
                if 0 <= idx < len(self.q):
                    self._phase(p, *self.q[idx])

        def _phase(self, p, ti, rows, t0):
            k = ti % 4
            a = acc.ap()[:rows, ti, :]
            if p == 0:
                for q in range(4):
                    kb.emit("dve", lambda q=q: dve.bn_stats(out=st6[k].ap()[:rows, q, :],
                                                            in_=acc.ap()[:rows, ti, q * 512:(q + 1) * 512]),
                            reads=[acc_t[ti]], writes=[st6_t[k]])
                kb.emit("dve", lambda: dve.bn_aggr(out=mv[k].ap()[:rows, :],
                                                   in_=st6[k].ap()[:rows].rearrange("p q s -> p (q s)")),
                        reads=[st6_t[k]], writes=[mv_t[k]])
                kb.emit("act", lambda: act.activation(out=sd[k].ap()[:rows, 0:1], in_=mv[k].ap()[:rows, 1:2],
                                                      func=AF.Sqrt, bias=eps_c.ap()[:rows, 0:1]),
                        reads=[mv_t[k], eps_t], writes=[sd_t[k]])
                kb.emit("dve", lambda: dve.reciprocal(out=sd[k].ap()[:rows, 1:2], in_=sd[k].ap()[:rows, 0:1]),
                        reads=[sd_t[k]], writes=[sd_t[k]])
                kb.emit("dve", lambda: dve.tensor_scalar(out=nm[k].ap()[:rows, 0:1], in0=mv[k].ap()[:rows, 0:1],
                                                         scalar1=sd[k].ap()[:rows, 1:2], scalar2=-1.0,
                                                         op0=ALU.mult, op1=ALU.mult),
                        reads=[mv_t[k], sd_t[k]], writes=[nm_t[k]])
            elif p == 1:
                kb.emit("act", lambda: act.activation(out=a, in_=a, func=AF.Identity,
                                                      scale=sd[k].ap()[:rows, 1:2], bias=nm[k].ap()[:rows, 0:1]),
                        reads=[acc_t[ti], nm_t[k], sd_t[k]], writes=[acc_t[ti]])
            elif p == 2:
                ge = pool if self.gmul_eng == "pool" else dve
                kb.emit(self.gmul_eng, lambda: ge.tensor_tensor(out=a, in0=a, in1=gb.ap()[:rows, 0, :],
                                                                op=ALU.mult),
                        reads=[acc_t[ti], gb_t], writes=[acc_t[ti]])
            elif p == 3:
                en = "pool" if ti % 2 == 0 else "dve"
                be = pool if en == "pool" else dve
                kb.emit(en, lambda: be.tensor_tensor(out=a, in0=a, in1=gb.ap()[:rows, 1, :], op=ALU.add),
                        reads=[acc_t[ti], gb_t], writes=[acc_t[ti]])
            else:
                self.tail(ti, rows, t0)

    hT = at(RM, [128, 16, NT], BF16)
    hT_t = Tk("hT", inherit=mixed_t.alldeps())

    class LN1Pipe:
        OFF = [0, 1, 2, 3]

        def __init__(self):
            self.q = []
            self.step = 0

        def push(self, ti, rows, t0):
            self.q.append((ti, rows, t0))
            self._step()

        def flush(self):
            for _ in range(self.OFF[-1]):
                self._step()

        def _step(self):
            sidx = self.step
            self.step += 1
            for p in sorted(range(len(self.OFF)), key=lambda p: (-self.OFF[p], p)):
                idx = sidx - self.OFF[p]
                if 0 <= idx < len(self.q):
                    self._phase(p, *self.q[idx])

        def _phase(self, p, ti, rows, t0):
            k = ti % 4
            k2 = ti % 2
            if p == 0:
                for q in range(4):
                    kb.emit("dve", lambda q=q: dve.bn_stats(out=st6[k].ap()[:rows, q, :],
                                                            in_=acc.ap()[:rows, ti, q * 512:(q + 1) * 512]),
                            reads=[acc_t[ti]], writes=[st6_t[k]])
                kb.emit("dve", lambda: dve.bn_aggr(out=mv[k].ap()[:rows, :],
                                                   in_=st6[k].ap()[:rows].rearrange("p q s -> p (q s)")),
                        reads=[st6_t[k]], writes=[mv_t[k]])
            elif p == 1:
                kb.emit("act", lambda: act.activation(out=sd9.ap()[:rows, ti, 0:1], in_=mv[k].ap()[:rows, 1:2],
                                                      func=AF.Sqrt, bias=eps_c.ap()[:rows, 0:1]),
                        reads=[mv_t[k], eps_t], writes=[sd9_t[ti]])
                kb.emit("dve", lambda: dve.reciprocal(out=sd9.ap()[:rows, ti, 1:2], in_=sd9.ap()[:rows, ti, 0:1]),
                        reads=[sd9_t[ti]], writes=[sd9_t[ti]])
                kb.emit("dve", lambda: dve.tensor_scalar(out=nm9.ap()[:rows, ti:ti + 1], in0=mv[k].ap()[:rows, 0:1],
                                                         scalar1=sd9.ap()[:rows, ti, 1:2], scalar2=-1.0,
                                                         op0=ALU.mult, op1=ALU.mult),
                        reads=[mv_t[k], sd9_t[ti]], writes=[nm9_t[ti]])
            elif p == 2:
                kb.emit("act", lambda: act.activation(out=hb[k2].ap()[:rows, :], in_=acc.ap()[:rows, ti, :],
                                                      func=AF.Identity, scale=sd9.ap()[:rows, ti, 1:2],
                                                      bias=nm9.ap()[:rows, ti:ti + 1]),
                        reads=[acc_t[ti], sd9_t[ti], nm9_t[ti]], writes=[hb_t[k2]])
            else:
                for half in range(2):
                    bi = 6 + half
                    trv = banks[bi].ap().bitcast(BF16)
                    for q in range(8):
                        cidx = half * 8 + q
                        kb.emit("pe", lambda q=q, cidx=cidx, trv=trv: pe.transpose(
                            trv[:, q * 128:q * 128 + rows], hb[k2].ap()[:rows, cidx * 128:(cidx + 1) * 128],
                            ident_b.ap()[:rows, :rows]),
                            reads=[hb_t[k2], ident_b_t], writes=[bank_t[bi]], signal=(q == 7))
                    for d in mixed_t.alldeps():
                        hT_t._addr(d)
                    for q in range(8):
                        cidx = half * 8 + q
                        src = trv[:, q * 128:q * 128 + rows]
                        dst = hT.ap()[:, cidx, t0:t0 + rows]
                        gcol = ln1T.ap()[:, 0, cidx:cidx + 1]
                        bcol = ln1T.ap()[:, 1, cidx:cidx + 1]
                        if half == 0:
                            kb.emit("act", lambda src=src, dst=dst, gcol=gcol, bcol=bcol: act.activation(
                                out=dst, in_=src, func=AF.Identity, scale=gcol, bias=bcol),
                                reads=[bank_t[bi], ln1T_t], writes=[hT_t])
                        else:
                            kb.emit("dve", lambda src=src, dst=dst, gcol=gcol, bcol=bcol: dve.tensor_scalar(
                                out=dst, in0=src, scalar1=gcol, scalar2=bcol, op0=ALU.mult, op1=ALU.add),
                                reads=[bank_t[bi], ln1T_t], writes=[hT_t])

    def ln1_deferred(ti, rows):
        a = acc.ap()[:rows, ti, :]
        kb.emit("act", lambda: act.activation(out=a, in_=a, func=AF.Identity, scale=sd9.ap()[:rows, ti, 1:2],
                                              bias=nm9.ap()[:rows, ti:ti + 1]),
                reads=[acc_t[ti], sd9_t[ti], nm9_t[ti]], writes=[acc_t[ti]])
        kb.emit("dve", lambda: dve.tensor_tensor(out=a, in0=a, in1=gb.ap()[:rows, 0, :], op=ALU.mult),
                reads=[acc_t[ti], gb_t], writes=[acc_t[ti]])
        kb.emit("dve", lambda: dve.tensor_tensor(out=a, in0=a, in1=gb.ap()[:rows, 1, :], op=ALU.add),
                reads=[acc_t[ti], gb_t], writes=[acc_t[ti]])

    ln1 = LN1Pipe()
    for cb in range(4):
        si, sv = load_w512(w_out_v, cb * 512)
        for ti, (t0, rows) in enumerate(TT):
            bi = mm_rot.next()
            group(bi, banks[bi].ap()[:rows, :512],
                  [(mixed.ap()[:, e, t0:t0 + rows], sv[:, e, :]) for e in range(16)], [slot_t[si], mixed_t])
            kb.emit("dve", lambda bi=bi, ti=ti, rows=rows, cb=cb: dve.scalar_tensor_tensor(
                out=acc.ap()[:rows, ti, cb * 512:(cb + 1) * 512], in0=acc.ap()[:rows, ti, cb * 512:(cb + 1) * 512],
                scalar=ALPHA, in1=banks[bi].ap()[:rows, :512], op0=ALU.mult, op1=ALU.add),
                reads=[bank_t[bi], acc_t[ti]], writes=[acc_t[ti]])
            if cb == 3:
                ln1.push(ti, rows, t0)
    ln1.flush()

    def ln2_tail(ti, rows, t0):
        kb.dma("sp", [(y_d[t0:t0 + rows, :], acc.ap()[:rows, ti, :])], reads=[acc_t[ti]], final=True)

    ln2 = LNPipe(ln2_tail, "pool")

    if STOP <= 9:
        return
    rt = [at(TMP + 8192 + i * 2048, [128, 512], F32) for i in range(2)]
    rt_t = [Tk("rt%d" % i, inherit=tmp_dead) for i in range(2)]
    rt_rot = Rot([0, 1])
    for b in range(8):
        cache_copies_part(b)
        if b == 1:
            kb.dma("sp", [(gb.ap(), lngb_d[:, 2:4, :])], writes=[gb_t])
        for ub in range(2):
            si, sv = load_w512(w_up_v, b * 1024 + ub * 512)
            for ft in range(4):
                fc = ub * 4 + ft
                for (t0, n) in TB:
                    bi = mm_rot.next()
                    group(bi, banks[bi].ap()[:, :n],
                          [(sv[:, c, ft * 128:(ft + 1) * 128], hT.ap()[:, c, t0:t0 + n]) for c in range(16)],
                          [slot_t[si], hT_t])
                    ri = rt_rot.next()
                    kb.emit("act", lambda bi=bi, ri=ri, n=n: act.activation(out=rt[ri].ap()[:, :n],
                                                                            in_=banks[bi].ap()[:, :n], func=AF.Relu),
                            reads=[bank_t[bi]], writes=[rt_t[ri]])
                    kb.emit("dve", lambda ri=ri, fc=fc, t0=t0, n=n: dve.tensor_tensor(
                        out=hidden.ap()[:, fc, t0:t0 + n], in0=rt[ri].ap()[:, :n], in1=rt[ri].ap()[:, :n],
                        op=ALU.mult),
                        reads=[rt_t[ri]], writes=[hidden_t])
                if b == 0:
                    ln1_deferred(fc, TT[fc][1])
                    if fc == 7:
                        ln1_deferred(8, TT[8][1])
        def load_down(dh, b=b):
            def f(sv):
                s3 = sv.rearrange("p (c n) -> p c n", c=8)
                return [(s3[:, 0:4, :], w_down_v[:, b * 8:b * 8 + 4, dh * 1024:(dh + 1) * 1024]),
                        (s3[:, 4:8, :], w_down_v[:, b * 8 + 4:b * 8 + 8, dh * 1024:(dh + 1) * 1024])]
            si = load_slot(f)
            return si, slots[si].ap().rearrange("p (c n) -> p c n", c=8)

        def down_group(si, sv, dh, cg, ti, t0, rows, b=b):
            bi = mm_rot.next()
            group(bi, banks[bi].ap()[:rows, :512],
                  [(hidden.ap()[:, fc, t0:t0 + rows], sv[:, fc, cg * 512:(cg + 1) * 512]) for fc in range(8)],
                  [slot_t[si], hidden_t])
            c0 = dh * 1024 + cg * 512
            if b == 0:
                kb.emit("dve", lambda: dve.scalar_tensor_tensor(
                    out=acc.ap()[:rows, ti, c0:c0 + 512], in0=acc.ap()[:rows, ti, c0:c0 + 512],
                    scalar=ALPHA, in1=banks[bi].ap()[:rows, :512], op0=ALU.mult, op1=ALU.add),
                    reads=[bank_t[bi], acc_t[ti]], writes=[acc_t[ti]])
            else:
                kb.emit("dve", lambda: dve.tensor_tensor(
                    out=acc.ap()[:rows, ti, c0:c0 + 512], in0=banks[bi].ap()[:rows, :512],
                    in1=acc.ap()[:rows, ti, c0:c0 + 512], op=ALU.add),
                    reads=[bank_t[bi], acc_t[ti]], writes=[acc_t[ti]])

        if b < 7:
            for dh in range(2):
                si, sv = load_down(dh)
                for ti, (t0, rows) in enumerate(TT):
                    for cg in range(2):
                        down_group(si, sv, dh, cg, ti, t0, rows)
        else:
            dl = [load_down(0), load_down(1)]
            for ti, (t0, rows) in enumerate(TT):
                for dh in range(2):
                    for cg in range(2):
                        down_group(dl[dh][0], dl[dh][1], dh, cg, ti, t0, rows)
                ln2.push(ti, rows, t0)

    ln2.flush()


def _mult(delta):
    delta = np.asarray(delta)
    m = ((delta >= 0) & (delta <= 128)).astype(np.float32)
    m += ((delta >= 0) & (delta <= 512) & (delta % 4 == 0)).astype(np.float32)
    m += ((delta >= 0) & (delta <= 2048) & (delta % 16 == 0)).astype(np.float32)
    return m


def _masks():
    p = np.arange(128)[:, None]
    c = np.arange(128)[None, :]
    mown = np.zeros((128, 8, 128), np.float32)
    for dl in range(8):
        mown[:, 7 - dl, :] = _mult(dl * 128 + c - p)
    mctx = np.zeros((128, 15, 128), np.float32)
    for dl in range(1, 16):
        mctx[:, 15 - dl, :] = _mult(dl * 128 + c - p)
    msam = np.zeros((128, 17, 8), np.float32)
    q = np.arange(8)[None, :]
    for j in range(16):
        msam[:, j, :] = _mult(2048 + q - (j * 128 + p))
    msam[:, 16, :] = _mult(q - p) * (p < 8)
    return mown.reshape(128, -1), mctx.reshape(128, -1), msam.reshape(128, -1)


def kernel(x_prompt, x_sample, cache_k, cache_v, state_conv, w_in, w_dw, b_dw, ln_conv_g, ln_conv_b,
           w_out, ln1_g, ln1_b, w_up, w_down, ln2_g, ln2_b):
    f = lambda a: np.ascontiguousarray(np.asarray(a, dtype=np.float32))
    x_prompt, x_sample = f(x_prompt), f(x_sample)
    cache_k, cache_v, state_conv = f(cache_k), f(cache_v), f(state_conv)
    w_in0, w_out0, w_up0, w_down0 = f(w_in)[0], f(w_out)[0], f(w_up)[0], f(w_down)[0]
    pc = np.concatenate([f(w_dw)[0].T, f(b_dw)[0][:, None], f(ln_conv_g)[0][:, None], f(ln_conv_b)[0][:, None]],
                        axis=1)
    pch = np.ascontiguousarray(pc.reshape(8, 128, 34).transpose(1, 0, 2))
    lngb = np.stack([f(ln1_g)[0], f(ln1_b)[0], f(ln2_g)[0], f(ln2_b)[0]])
    lngb = np.ascontiguousarray(np.broadcast_to(lngb[None], (128, 4, D)))
    mown, mctx, msam = _masks()
    ident = np.eye(128, dtype=np.float32)
    ln1T = np.ascontiguousarray(np.stack([f(ln1_g)[0].reshape(16, 128).T, f(ln1_b)[0].reshape(16, 128).T], axis=1))

    in_maps = []
    for c in range(8):
        b, half = c // 2, c % 2
        own = x_prompt[b, half * NP:(half + 1) * NP]
        xs = x_sample[c]
        xres = np.concatenate([own, xs], axis=0)
        xT = np.ascontiguousarray(xres.T)
        if half == 1:
            ctx = x_prompt[b, 0:NP]
            xcT = np.ascontiguousarray(ctx.T)
            mc = mctx
        else:
            xcT = np.zeros((D, NP), np.float32)
            mc = np.zeros_like(mctx)
        ck = cache_k[0, c].reshape(2048, 1024)
        cv = cache_v[0, c].reshape(2048, 1024)
        in_maps.append(dict(
            xT=xT, xcT=xcT, xc32=np.ascontiguousarray(xcT[:, NP - 32:NP]), xres=xres,
            w_in=w_in0, w_out=w_out0, w_up=w_up0, w_down=w_down0, pch=pch, lngb=lngb,
            mown=mown, mctx=mc, msam=msam, ident=ident, ln1T=ln1T,
            ckT=np.ascontiguousarray(ck.T), ck=ck, cv=cv,
            scT=np.ascontiguousarray(state_conv[0, c].T), sc=state_conv[0, c],
        ))
    nc = build_program()
    res = run_bass_kernel_spmd(nc, in_maps, core_ids=list(range(8)))
    R = res.results
    y_prompt = np.zeros((4, 2048, D), np.float32)
    y_sample = np.zeros((8, 8, D), np.float32)
    kp = np.zeros((1, 4, 2048, 8, 128), np.float32)
    vp = np.zeros((1, 4, 2048, 8, 128), np.float32)
    cp = np.zeros((1, 4, 30, 1024), np.float32)
    ksw = np.zeros((1, 8, 2048, 8, 128), np.float32)
    vsw = np.zeros((1, 8, 2048, 8, 128), np.float32)
    cs = np.zeros((1, 8, 30, 1024), np.float32)
    for c in range(8):
        b, half = c // 2, c % 2
        r = R[c]
        y = np.asarray(r["y"])
        y_prompt[b, half * NP:(half + 1) * NP] = y[:NP]
        y_sample[c] = y[NP:NT]
        kp[0, b, half * NP:(half + 1) * NP] = np.asarray(r["kout"]).reshape(NP, 8, 128)
        vp[0, b, half * NP:(half + 1) * NP] = np.asarray(r["vout"]).reshape(NP, 8, 128)
        if half == 1:
            cp[0, b] = np.asarray(r["convp"])
        ksw[0, c] = np.asarray(r["ks"]).reshape(2048, 8, 128)
        vsw[0, c] = np.asarray(r["vs"]).reshape(2048, 8, 128)
        cs[0, c] = np.asarray(r["convs"])
    return (y_prompt, y_sample, kp, vp, cp, ksw, vsw, cs)
```

```python
import numpy as np
import concourse.bass as bass
import concourse.mybir as mybir
from concourse.bass_utils import run_bass_kernel_spmd

F32 = mybir.dt.float32
BF16 = mybir.dt.bfloat16
AF = mybir.ActivationFunctionType
ALU = mybir.AluOpType

D = 2048
NP = 1024
NS = 8
NT = NP + NS
DIN = 5120
DFF = 8192
ALPHA = float(2.0 ** 0.25)
EPS = 1e-5
SCALE = float(128.0 ** -0.5)
TB = [(0, 512), (512, 512), (1024, 8)]
TT = [(i * 128, 128) for i in range(8)] + [(1024, 8)]
NSLOT = 2
STOP = 99
GLIMIT = 10 ** 9


class _Stop(Exception):
    pass
SLOT_BYTES = 16384


class Tk:
    def __init__(self, name, inherit=None, strict=False):
        self.name = name
        self.strict = strict
        self.w = None
        self.r = {}
        self.dsem = None
        self.dcnt = 0
        if inherit:
            for d in inherit:
                self._addr(d)

    def _addr(self, d):
        k = d[0]
        if k not in self.r or self.r[k][2] < d[2]:
            self.r[k] = d

    def alldeps(self):
        out = list(self.r.values())
        if self.w is not None:
            out.append(self.w)
        return out


class KB:
    def __init__(self, nc):
        self.nc = nc
        self.semkey = 0
        self.eng = {}
        for name, h in (("pe", nc.tensor), ("act", nc.scalar), ("dve", nc.vector),
                        ("pool", nc.gpsimd), ("sp", nc.sync)):
            self.eng[name] = dict(h=h, sem=nc.alloc_semaphore(name="sem_" + name), cnt=0,
                                  waited={}, key=self._newkey())
        self.final = []
        self.nsem = 5

    def _newkey(self):
        self.semkey += 1
        return self.semkey

    def _collect(self, engname, reads, writes):
        deps = {}

        def add(d, skip_same):
            if d is None:
                return
            if d[3] == engname and (skip_same or engname == "pe"):
                return
            k = d[0]
            if k not in deps or deps[k][2] < d[2]:
                deps[k] = d

        for t in reads:
            add(t.w, False)
        for t in writes:
            add(t.w, not t.strict)
            for d in t.r.values():
                add(d, not t.strict)
        return deps

    def _wait(self, engname, deps):
        e = self.eng[engname]
        for k, d in deps.items():
            if e["waited"].get(k, 0) < d[2]:
                e["h"].wait_ge(d[1], d[2])
                e["waited"][k] = d[2]

    def emit(self, engname, fn, reads=(), writes=(), signal=True):
        e = self.eng[engname]
        self._wait(engname, self._collect(engname, reads, writes))
        inst = fn()
        if signal:
            e["cnt"] += 1
            inst.then_inc(e["sem"], 1)
            val = e["cnt"]
        else:
            val = e["cnt"] + 1
        d = (e["key"], e["sem"], val, engname)
        for t in reads:
            t._addr(d)
        for t in writes:
            t.w = d
            t.r = {}
        return inst

    def dma(self, queue, pairs, reads=(), writes=(), owner=None, final=False, after=()):
        e = self.eng[queue]
        deps = self._collect(queue + "_dma", reads, writes)
        for d in after:
            if d is not None and (d[0] not in deps or deps[d[0]][2] < d[2]):
                deps[d[0]] = d
        self._wait(queue, deps)
        if owner is None:
            owner = writes[0] if writes else reads[0]
        if owner.dsem is None:
            owner.dsem = self.nc.alloc_semaphore(name="dsem_%s_%d" % (owner.name, self.nsem))
            owner.dkey = self._newkey()
            self.nsem += 1
        for (o, i) in pairs:
            e["h"].dma_start(out=o, in_=i).then_inc(owner.dsem, 16)
            owner.dcnt += 1
        d = (owner.dkey, owner.dsem, 16 * owner.dcnt, "dma")
        for t in reads:
            t._addr(d)
        for t in writes:
            t.w = d
            t.r = {}
        if final:
            self.final.append(d)

    def finish(self):
        deps = {}
        for d in self.final:
            if d[0] not in deps or deps[d[0]][2] < d[2]:
                deps[d[0]] = d
        for k, d in deps.items():
            self.eng["sp"]["h"].wait_ge(d[1], d[2])


class Rot:
    def __init__(self, items):
        self.items = items
        self.i = 0

    def next(self):
        x = self.items[self.i % len(self.items)]
        self.i += 1
        return x


def build_program():
    nc = bass.Bass("TRN2", target_bir_lowering=False)
    kb = KB(nc)
    try:
        _build(nc, kb)
    except _Stop:
        pass
    kb.finish()
    return nc


def _build(nc, kb):

    def din(name, shape):
        return nc.dram_tensor(name, list(shape), F32, kind="ExternalInput").ap()

    def dout(name, shape):
        return nc.dram_tensor(name, list(shape), F32, kind="ExternalOutput").ap()

    xT_d = din("xT", [D, NT])
    xcT_d = din("xcT", [D, NP])
    xc32_d = din("xc32", [D, 32])
    xres_d = din("xres", [NT, D])
    w_in_d = din("w_in", [D, DIN])
    w_out_d = din("w_out", [D, D])
    w_up_d = din("w_up", [D, DFF])
    w_down_d = din("w_down", [DFF, D])
    pch_d = din("pch", [128, 8, 34])
    lngb_d = din("lngb", [128, 4, D])
    mown_d = din("mown", [128, 8 * 128])
    mctx_d = din("mctx", [128, 15 * 128])
    msam_d = din("msam", [128, 17 * 8])
    ident_d = din("ident", [128, 128])
    ln1T_d = din("ln1T", [128, 2, 16])
    ckT_d = din("ckT", [1024, 2048])
    ck_d = din("ck", [2048, 1024])
    cv_d = din("cv", [2048, 1024])
    scT_d = din("scT", [1024, 30])
    sc_d = din("sc", [30, 1024])

    y_d = dout("y", [NT, D])
    kout_d = dout("kout", [NP, 1024])
    vout_d = dout("vout", [NP, 1024])
    convp_d = dout("convp", [30, 1024])
    ks_d = dout("ks", [2048, 1024])
    vs_d = dout("vs", [2048, 1024])
    convs_d = dout("convs", [30, 1024])

    base = (nc.sbuf_base + 31) // 32 * 32
    cur = [base]
    cnt = [0]

    def region(nbytes):
        o = cur[0]
        cur[0] += (nbytes + 31) // 32 * 32
        return o

    def at(off, shape, dt):
        cnt[0] += 1
        return nc.alloc_sbuf_tensor_at("sb%d" % cnt[0], list(shape), dt, offset=off)

    W_OFF = region(NSLOT * SLOT_BYTES)
    BIG = region(73728)
    RX = region(33280)
    RQ = region(17664)
    RM = region(33024)
    TMP = region(13312)
    SM = region(8192)
    assert cur[0] <= nc.sbuf_top, (cur[0], nc.sbuf_top)

    slots = [at(W_OFF + i * SLOT_BYTES, [128, 8192], BF16) for i in range(NSLOT)]
    slot_t = [Tk("slot%d" % i) for i in range(NSLOT)]
    slot_rot = Rot(list(range(NSLOT)))

    so = [SM]

    def small(shape, dt, nbytes):
        t = at(so[0], shape, dt)
        so[0] += (nbytes + 31) // 32 * 32
        return t

    ident_b = small([128, 128], BF16, 256); ident_b_t = Tk("identb")
    ident_f = small([128, 128], F32, 512); ident_f_t = Tk("identf")
    ones_f = small([128, 128], F32, 512); ones_t = Tk("ones")
    pch = small([128, 8, 34], F32, 1088); pch_t = Tk("pch")
    eps_c = small([128, 1], F32, 4); eps_t = Tk("eps")
    xc32_b = small([128, 16, 32], BF16, 1024); xc32_t = Tk("xc32")
    u32 = small([128, 8, 40], F32, 1280); u32_t = Tk("u32")
    KsT = small([128, 8, 8], BF16, 128); KsT_t = Tk("KsT")
    Vs_new = small([8, 8, 129], BF16, 2064 + 16); Vs_t = Tk("Vsnew")
    st6 = [small([128, 4, 6], F32, 96) for _ in range(4)]
    st6_t = [Tk("st6_%d" % i, strict=True) for i in range(4)]
    mv = [small([128, 2], F32, 8) for _ in range(4)]
    mv_t = [Tk("mv%d" % i, strict=True) for i in range(4)]
    sd = [small([128, 2], F32, 8) for _ in range(4)]
    sd_t = [Tk("sd%d" % i, strict=True) for i in range(4)]
    nm = [small([128, 1], F32, 4) for _ in range(4)]
    nm_t = [Tk("nm%d" % i, strict=True) for i in range(4)]
    ln1T = small([128, 2, 16], F32, 128); ln1T_t = Tk("ln1T")
    sd9 = small([128, 9, 2], F32, 72)
    nm9 = small([128, 9], F32, 36)
    sd9_t = [Tk("sd9_%d" % i, strict=True) for i in range(9)]
    nm9_t = [Tk("nm9_%d" % i, strict=True) for i in range(9)]
    rc = [small([128, 1], F32, 4) for _ in range(4)]
    rc_t = [Tk("rc%d" % i, strict=True) for i in range(4)]
    assert so[0] <= SM + 8192, so[0]

    banks = [nc.alloc_psum_tensor("psb%d" % i, [128, 512], F32) for i in range(8)]
    bank_t = [Tk("bank%d" % i) for i in range(8)]

    pe, act, dve, pool = nc.tensor, nc.scalar, nc.vector, nc.gpsimd

    kb.dma("sp", [(ident_f.ap(), ident_d)], writes=[ident_f_t])
    kb.dma("pool", [(ident_b.ap(), ident_d)], writes=[ident_b_t])
    kb.dma("sp", [(pch.ap(), pch_d)], writes=[pch_t])
    kb.dma("sp", [(ln1T.ap(), ln1T_d)], writes=[ln1T_t])
    kb.emit("dve", lambda: dve.memset(ones_f.ap(), 1.0), writes=[ones_t])
    kb.emit("dve", lambda: dve.memset(eps_c.ap(), EPS), writes=[eps_t])
    kb.emit("dve", lambda: dve.memset(Vs_new.ap()[:, :, 128:129], 1.0), writes=[Vs_t])

    copy_t = Tk("dramcopy")

    def cache_copies_part(i):
        prs = [(dst[i * 255:(i + 1) * 255, :], src[8 + i * 255: 8 + (i + 1) * 255, :])
               for (dst, src) in ((ks_d, ck_d), (vs_d, cv_d))]
        if i == 0:
            prs.append((convs_d[0:22, :], sc_d[8:30, :]))
        kb.dma("sp", prs, owner=copy_t, final=True)

    KT = at(BIG, [128, 8, 2048], BF16); KT_t = Tk("KT")
    Vall = at(BIG + 32768, [128, 16, 8, 129], BF16)
    V_t = [Tk("V%d" % i) for i in range(16)]
    mown = at(BIG + 65792, [128, 8 * 128], BF16); mown_t = Tk("mown")
    mctx = at(BIG + 67840, [128, 15 * 128], BF16); mctx_t = Tk("mctx")
    msam = at(BIG + 71680, [128, 17 * 8], BF16); msam_t = Tk("msam")
    kb.emit("dve", lambda: dve.memset(Vall.ap()[:, :, :, 128:129], 1.0), writes=V_t)

    xcT_b = at(RM, [128, 16, NP], BF16); xcT_t = [Tk("xcT0"), Tk("xcT1")]
    xcv = xcT_d.rearrange("(c p) n -> p c n", p=128)

    w_in_v = w_in_d.rearrange("(c p) n -> p c n", p=128)
    w_out_v = w_out_d.rearrange("(c p) n -> p c n", p=128)
    w_up_v = w_up_d.rearrange("(c p) n -> p c n", p=128)
    w_down_v = w_down_d.rearrange("(c p) n -> p c n", p=128)

    def load_slot(pairs_fn):
        si = slot_rot.next()
        sv = slots[si].ap()
        kb.dma("pool", pairs_fn(sv), writes=[slot_t[si]])
        return si

    def load_w512(view, col0):
        def f(sv):
            s3 = sv.rearrange("p (c n) -> p c n", c=16)
            return [(s3[:, 0:8, :], view[:, 0:8, col0:col0 + 512]),
                    (s3[:, 8:16, :], view[:, 8:16, col0:col0 + 512])]
        si = load_slot(f)
        return si, slots[si].ap().rearrange("p (c n) -> p c n", c=16)

    mm_rot = Rot([0, 1, 2, 3, 4, 5])

    def add_slot(off, inherit):
        slots.append(at(off, [128, 8192], BF16))
        slot_t.append(Tk("slotx%d" % len(slots), inherit=inherit))
        slot_rot.items = list(range(len(slots)))

    def drop_slots():
        dead = sum([t.alldeps() for t in slot_t[NSLOT:]], [])
        del slots[NSLOT:]
        del slot_t[NSLOT:]
        slot_rot.items = list(range(NSLOT))
        return dead


    gcount = [0]

    def group(bank_i, out_ap, pairs, extra_reads):
        gcount[0] += 1
        if gcount[0] > GLIMIT:
            raise _Stop()
        n = len(pairs)
        for i, (l, r) in enumerate(pairs):
            kb.emit("pe", lambda l=l, r=r, i=i: pe.matmul(out_ap, l, r, start=(i == 0), stop=(i == n - 1)),
                    reads=extra_reads, writes=[bank_t[bank_i]], signal=(i == n - 1))

    if STOP <= 0:
        return
    pre_si = slot_rot.next()
    pre_sv = slots[pre_si].ap().rearrange("p (c n) -> p c n", c=16)
    pre_ct_t = [Tk("pre_ct%d" % i) for i in range(4)]

    def pre_piece(ct):
        kb.dma("pool", [(pre_sv[:, :, ct * 128:(ct + 1) * 128], w_in_v[:, :, 1024 + ct * 128:1024 + (ct + 1) * 128])],
               writes=[pre_ct_t[ct]])

    def xc_half(hf):
        kb.dma("pool", [(xcT_b.ap()[:, 0:8, hf * 512:(hf + 1) * 512], xcv[:, 0:8, hf * 512:(hf + 1) * 512]),
                        (xcT_b.ap()[:, 8:16, hf * 512:(hf + 1) * 512], xcv[:, 8:16, hf * 512:(hf + 1) * 512])],
               writes=[xcT_t[hf]])

    pre_piece(0)
    xc_half(0)
    pre_piece(1)
    xc_half(1)
    pre_piece(2)
    pre_piece(3)
    pre_blk = (pre_si, pre_sv)
    kb.dma("pool", [(mown.ap(), mown_d)], writes=[mown_t])
    kb.dma("pool", [(mctx.ap(), mctx_d)], writes=[mctx_t])
    kb.dma("pool", [(msam.ap(), msam_d)], writes=[msam_t])
    kb.dma("pool", [(xc32_b.ap(), xc32_d.rearrange("(c p) n -> p c n", p=128))], writes=[xc32_t])
    xT_b = at(RX, [128, 16, NT], BF16); xT_t = Tk("xT")
    xv = xT_d.rearrange("(c p) n -> p c n", p=128)

    def load_xT():
        kb.dma("pool", [(xT_b.ap()[:, 0:8, :], xv[:, 0:8, :]), (xT_b.ap()[:, 8:16, :], xv[:, 8:16, :])],
               writes=[xT_t])
    xT_loaded = [False]

    for (kind, col0) in (("k", 1024), ("k", 1536), ("v", 2048), ("v", 2560)):
        si, sv = pre_blk if col0 == 1024 else load_w512(w_in_v, col0)
        if col0 == 1536:
            load_xT()
        if kind == "k":
            for ct in range(4):
                head = (col0 - 1024) // 128 + ct
                for (t0, n) in ((0, 512), (512, 512)):
                    bi = mm_rot.next()
                    wt = pre_ct_t[ct] if col0 == 1024 else slot_t[si]
                    group(bi, banks[bi].ap()[:, :n],
                          [(sv[:, c, ct * 128:(ct + 1) * 128], xcT_b.ap()[:, c, t0:t0 + n]) for c in range(16)],
                          [wt, xcT_t[t0 // 512]])
                    kb.emit("act", lambda bi=bi, head=head, t0=t0, n=n: act.copy(
                        out=KT.ap()[:, head, t0:t0 + n], in_=banks[bi].ap()[:, :n]),
                        reads=[bank_t[bi]], writes=[KT_t])
            if col0 == 1024:
                for t_ in pre_ct_t:
                    for d in t_.alldeps():
                        slot_t[si]._addr(d)
        else:
            h0 = (col0 - 2048) // 128
            for tt in range(8):
                bi = mm_rot.next()
                group(bi, banks[bi].ap()[:, :512],
                      [(xcT_b.ap()[:, c, tt * 128:(tt + 1) * 128], sv[:, c, :]) for c in range(16)],
                      [slot_t[si], xcT_t[tt // 4]])
                kb.emit("dve", lambda bi=bi, tt=tt, h0=h0: dve.tensor_copy(
                    out=Vall.ap()[:, tt, h0:h0 + 4, 0:128],
                    in_=banks[bi].ap().rearrange("p (h d) -> p h d", h=4)),
                    reads=[bank_t[bi]], writes=[V_t[tt]])

    if STOP <= 1:
        return
    QT = at(RQ, [128, 8, NT], BF16); QT_t = Tk("QT")
    kvst = [at(TMP + i * 2048, [128, 512], F32) for i in range(4)]
    kvst_t = [Tk("kvst%d" % i) for i in range(4)]
    kvst_rot = Rot(list(range(4)))
    kbf = [at(TMP + 8192 + i * 1024, [128, 512], BF16) for i in range(4)]
    kbf_t = [Tk("kbf%d" % i) for i in range(4)]
    kbf_rot = Rot(list(range(4)))
    ktr_rot = Rot([6, 7])

    for (kind, col0) in (("k", 1024), ("k", 1536), ("v", 2048), ("v", 2560), ("q", 0), ("q", 512)):
        si, sv = load_w512(w_in_v, col0)
        if kind == "q":
            for ct in range(4):
                head = (col0 % 1024) // 128 + ct
                for (t0, n) in TB:
                    bi = mm_rot.next()
                    group(bi, banks[bi].ap()[:, :n],
                          [(sv[:, c, ct * 128:(ct + 1) * 128], xT_b.ap()[:, c, t0:t0 + n]) for c in range(16)],
                          [slot_t[si], xT_t])
                    if kind == "q":
                        dst, dt_ = QT.ap()[:, head, t0:t0 + n], QT_t
                    elif t0 < NP:
                        dst, dt_ = KT.ap()[:, head, NP + t0:NP + t0 + n], KT_t
                    else:
                        dst, dt_ = KsT.ap()[:, head, 0:8], KsT_t
                    kb.emit("act", lambda bi=bi, dst=dst, n=n: act.copy(out=dst, in_=banks[bi].ap()[:, :n]),
                            reads=[bank_t[bi]], writes=[dt_])
        if kind in ("k", "v"):
            cb = (col0 % 1024)
            h0 = cb // 128
            pend = []

            def k_transposes(ti, t0, rows, qi, h0=h0):
                tb_ = ktr_rot.next()
                trv = banks[tb_].ap().bitcast(BF16)
                for hh in range(4):
                    kb.emit("pe", lambda hh=hh: pe.transpose(trv[:, hh * 128:hh * 128 + rows],
                                                             kbf[qi].ap()[:rows, hh * 128:(hh + 1) * 128],
                                                             ident_b.ap()[:rows, :rows]),
                            reads=[kbf_t[qi], ident_b_t], writes=[bank_t[tb_]], signal=(hh == 3))
                src = trv[:, 0:512].rearrange("p (h q) -> p h q", h=4)[:, :, :rows]
                if ti < 8:
                    dst, dt_ = KT.ap()[:, h0:h0 + 4, NP + t0:NP + t0 + rows], KT_t
                else:
                    dst, dt_ = KsT.ap()[:, h0:h0 + 4, 0:8], KsT_t
                kb.emit("act", lambda: act.copy(out=dst, in_=src), reads=[bank_t[tb_]], writes=[dt_])

            for ti, (t0, rows) in enumerate(TT):
                bi = mm_rot.next()
                group(bi, banks[bi].ap()[:rows, :512],
                      [(xT_b.ap()[:, c, t0:t0 + rows], sv[:, c, :]) for c in range(16)],
                      [slot_t[si], xT_t])
                if kind == "v":
                    if ti < 8:
                        dst, dt_ = Vall.ap()[:, 8 + ti, h0:h0 + 4, 0:128], V_t[8 + ti]
                    else:
                        dst, dt_ = Vs_new.ap()[0:8, h0:h0 + 4, 0:128], Vs_t
                ki = kvst_rot.next()
                kb.emit("act", lambda bi=bi, ki=ki, rows=rows: act.copy(
                    out=kvst[ki].ap()[:rows, :], in_=banks[bi].ap()[:rows, :]),
                    reads=[bank_t[bi]], writes=[kvst_t[ki]])
                if kind == "v":
                    kb.emit("dve", lambda ki=ki, dst=dst, rows=rows: dve.tensor_copy(
                        out=dst, in_=kvst[ki].ap()[:rows, :].rearrange("p (h d) -> p h d", h=4)),
                        reads=[kvst_t[ki]], writes=[dt_])
                if ti < 8:
                    od = (kout_d if kind == "k" else vout_d)[t0:t0 + 128, cb:cb + 512]
                else:
                    od = (ks_d if kind == "k" else vs_d)[2040:2048, cb:cb + 512]
                kb.dma("sp", [(od, kvst[ki].ap()[:rows, :])], reads=[kvst_t[ki]], final=True)
                if kind == "k":
                    qi = kbf_rot.next()
                    kb.emit("dve", lambda ki=ki, qi=qi, rows=rows: dve.tensor_copy(
                        out=kbf[qi].ap()[:rows, :], in_=kvst[ki].ap()[:rows, :]),
                        reads=[kvst_t[ki]], writes=[kbf_t[qi]])
                    pend.append((ti, t0, rows, qi))
                    if len(pend) > 2:
                        k_transposes(*pend.pop(0))
            while pend:
                k_transposes(*pend.pop(0))

    if STOP <= 2:
        return
    mixed = at(RM, [128, 16, NT], BF16); mixed_t = Tk("mixed", inherit=drop_slots() + xcT_t[0].alldeps() + xcT_t[1].alldeps())
    EP = [at(TMP + 6656 + i * 1024, [128, 512], BF16) for i in range(6)]
    EP_t = [Tk("EP%d" % i, inherit=sum([k.alldeps() for k in kvst_t + kbf_t], [])) for i in range(6)]
    EP_rot = Rot(list(range(6)))
    ao = [at(TMP + i * 2048, [128, 1024], BF16) for i in range(2)]
    ao_t = [Tk("ao%d" % i, inherit=sum([k.alldeps() for k in kvst_t + kbf_t], [])) for i in range(2)]
    S_rot = Rot([0, 1, 2, 3])
    O_banks = [4, 5]
    TR_bank = 6

    units = []
    for t in range(8):
        g = 8 + t
        for h in range(8):
            groups = [(0, 3, "c"), (4, 7, "c"), (8, min(11, g), "o")]
            if g >= 12:
                groups.append((12, g, "o"))
            for gi, (j0, j1, kind) in enumerate(groups):
                units.append(dict(t=t, g=g, h=h, j0=j0, j1=j1, kind=kind, first=(gi == 0),
                                  last=(gi == len(groups) - 1)))

    def emit_S(u):
        bi = S_rot.next()
        u["sb"] = bi
        n = u["j1"] - u["j0"] + 1
        t, h = u["t"], u["h"]
        for jj in range(n):
            j = u["j0"] + jj
            kb.emit("pe", lambda jj=jj, j=j: pe.matmul(
                banks[bi].ap()[:, jj * 128:(jj + 1) * 128], KT.ap()[:, h, j * 128:(j + 1) * 128],
                QT.ap()[:, h, t * 128:(t + 1) * 128], start=True, stop=True),
                reads=[KT_t, QT_t], writes=[bank_t[bi]], signal=(jj == n - 1))
        ei = EP_rot.next()
        u["ep"] = ei
        kb.emit("act", lambda: act.activation(out=EP[ei].ap()[:, :n * 128], in_=banks[bi].ap()[:, :n * 128],
                                              func=AF.Exp, scale=SCALE),
                reads=[bank_t[bi]], writes=[EP_t[ei]])
        if u["kind"] == "o":
            m0 = 7 - u["g"] + u["j0"]
            msk, mt = mown.ap()[:, m0 * 128:(m0 + n) * 128], mown_t
        else:
            m0 = 15 - u["g"] + u["j0"]
            msk, mt = mctx.ap()[:, m0 * 128:(m0 + n) * 128], mctx_t
        kb.emit("dve", lambda: dve.tensor_tensor(out=EP[ei].ap()[:, :n * 128], in0=EP[ei].ap()[:, :n * 128],
                                                 in1=msk, op=ALU.mult),
                reads=[EP_t[ei], mt], writes=[EP_t[ei]])

    def emit_PV(u):
        t, h = u["t"], u["h"]
        ob = O_banks[(t * 8 + h) % 2]
        n = u["j1"] - u["j0"] + 1
        ei = u["ep"]
        for jj in range(n):
            j = u["j0"] + jj
            kb.emit("pe", lambda jj=jj, j=j: pe.matmul(
                banks[ob].ap()[:, 0:129], EP[ei].ap()[:, jj * 128:(jj + 1) * 128], Vall.ap()[:, j, h, :],
                start=(u["first"] and jj == 0), stop=(u["last"] and jj == n - 1)),
                reads=[EP_t[ei], V_t[j]], writes=[bank_t[ob]], signal=(jj == n - 1))
        if u["last"]:
            ri = (t * 8 + h) % 4
            kb.emit("dve", lambda: dve.reciprocal(out=rc[ri].ap(), in_=banks[ob].ap()[:, 128:129]),
                    reads=[bank_t[ob]], writes=[rc_t[ri]])
            kb.emit("dve", lambda: dve.tensor_scalar(out=ao[t % 2].ap()[:, h * 128:(h + 1) * 128],
                                                     in0=banks[ob].ap()[:, 0:128], scalar1=rc[ri].ap()[:, 0:1],
                                                     scalar2=None, op0=ALU.mult),
                    reads=[bank_t[ob], rc_t[ri]], writes=[ao_t[t % 2]])
            if h == 7:
                trv = banks[TR_bank].ap().bitcast(BF16)
                for hh in range(8):
                    kb.emit("pe", lambda hh=hh: pe.transpose(trv[:, hh * 128:(hh + 1) * 128],
                                                             ao[t % 2].ap()[:, hh * 128:(hh + 1) * 128],
                                                             ident_b.ap()),
                            reads=[ao_t[t % 2], ident_b_t], writes=[bank_t[TR_bank]], signal=(hh == 7))
                kb.emit("act", lambda: act.copy(out=mixed.ap()[:, 0:8, t * 128:(t + 1) * 128],
                                                in_=trv.rearrange("p (h q) -> p h q", h=8)),
                        reads=[bank_t[TR_bank]], writes=[mixed_t])

    def prompt_attention():
        LAG = 4
        sample_load(0)
        for i, u in enumerate(units):
            emit_S(u)
            if i >= LAG:
                emit_PV(units[i - LAG])
            if u["h"] == 7 and u["last"]:
                sample_head(u["t"])
                if u["t"] < 7:
                    sample_load(u["t"] + 1)
        for u in units[len(units) - LAG:]:
            emit_PV(u)

    kc = at(W_OFF + SLOT_BYTES, [128, 2048], BF16)
    vc = at(W_OFF + SLOT_BYTES + 4096, [128, 16, 129], BF16)
    kv_t = slot_t[1]
    kb.emit("dve", lambda: dve.memset(vc.ap()[:, :, 128:129], 1.0), writes=[kv_t])
    Es = at(TMP + 4096, [128, 144], BF16)
    Es_t = Tk("Es", inherit=sum([k.alldeps() for k in kvst_t + kbf_t], []))
    ao_s = at(TMP + 4416, [8, 1024], BF16)
    ao_s_t = Tk("aos", inherit=sum([k.alldeps() for k in kvst_t + kbf_t], []))
    cv_v = cv_d.rearrange("(t p) (h d) -> p t h d", p=128, h=8)
    SO_bank = 7

    def sample_load(h):
        kb.dma("pool", [(kc.ap(), ckT_d[h * 128:(h + 1) * 128, :]),
                        (vc.ap()[:, :, 0:128], cv_v[:, :, h, :])], writes=[kv_t])

    def sample_head(h):
        bi = S_rot.next()
        for j in range(16):
            kb.emit("pe", lambda j=j: pe.matmul(banks[bi].ap()[:, j * 8:(j + 1) * 8],
                                                kc.ap()[:, j * 128:(j + 1) * 128],
                                                QT.ap()[:, h, NP:NT], start=True, stop=True),
                    reads=[kv_t, QT_t], writes=[bank_t[bi]], signal=False)
        kb.emit("pe", lambda: pe.matmul(banks[bi].ap()[0:8, 128:136], KsT.ap()[:, h, 0:8],
                                        QT.ap()[:, h, NP:NT], start=True, stop=True),
                reads=[KsT_t, QT_t], writes=[bank_t[bi]])
        kb.emit("act", lambda: act.activation(out=Es.ap()[:, 0:128], in_=banks[bi].ap()[:, 0:128],
                                              func=AF.Exp, scale=SCALE),
                reads=[bank_t[bi]], writes=[Es_t])
        kb.emit("act", lambda: act.activation(out=Es.ap()[0:8, 128:136], in_=banks[bi].ap()[0:8, 128:136],
                                              func=AF.Exp, scale=SCALE),
                reads=[bank_t[bi]], writes=[Es_t])
        kb.emit("dve", lambda: dve.tensor_tensor(out=Es.ap()[:, 0:128], in0=Es.ap()[:, 0:128],
                                                 in1=msam.ap()[:, 0:128], op=ALU.mult),
                reads=[Es_t, msam_t], writes=[Es_t])
        kb.emit("dve", lambda: dve.tensor_tensor(out=Es.ap()[0:8, 128:136], in0=Es.ap()[0:8, 128:136],
                                                 in1=msam.ap()[0:8, 128:136], op=ALU.mult),
                reads=[Es_t, msam_t], writes=[Es_t])
        ob = SO_bank
        for j in range(16):
            kb.emit("pe", lambda j=j: pe.matmul(banks[ob].ap()[0:8, 0:129], Es.ap()[:, j * 8:(j + 1) * 8],
                                                vc.ap()[:, j, :], start=(j == 0), stop=False),
                    reads=[Es_t, kv_t], writes=[bank_t[ob]], signal=False)
        kb.emit("pe", lambda: pe.matmul(banks[ob].ap()[0:8, 0:129], Es.ap()[0:8, 128:136],
                                        Vs_new.ap()[0:8, h, :], start=False, stop=True),
                reads=[Es_t, Vs_t], writes=[bank_t[ob]])
        ri = h % 4
        kb.emit("dve", lambda: dve.reciprocal(out=rc[ri].ap()[0:8, :], in_=banks[ob].ap()[0:8, 128:129]),
                reads=[bank_t[ob]], writes=[rc_t[ri]])
        kb.emit("dve", lambda: dve.tensor_scalar(out=ao_s.ap()[0:8, h * 128:(h + 1) * 128],
                                                 in0=banks[ob].ap()[0:8, 0:128], scalar1=rc[ri].ap()[0:8, 0:1],
                                                 scalar2=None, op0=ALU.mult),
                reads=[bank_t[ob], rc_t[ri]], writes=[ao_s_t])

    def sample_finish():
        trv = banks[TR_bank].ap().bitcast(BF16)
        for hh in range(8):
            kb.emit("pe", lambda hh=hh: pe.transpose(trv[:, hh * 8:(hh + 1) * 8],
                                                     ao_s.ap()[0:8, hh * 128:(hh + 1) * 128],
                                                     ident_b.ap()[0:8, 0:8]),
                    reads=[ao_s_t, ident_b_t], writes=[bank_t[TR_bank]], signal=(hh == 7))
        kb.emit("act", lambda: act.copy(out=mixed.ap()[:, 0:8, NP:NT],
                                        in_=trv[:, 0:64].rearrange("p (h q) -> p h q", h=8)),
                reads=[bank_t[TR_bank]], writes=[mixed_t])

    if STOP <= 3:
        return
    prompt_attention()
    sample_finish()
    if STOP <= 4:
        return
    u_bf = at(RQ, [128, 8, 1062], BF16)
    us_ext = at(RQ + 16992, [128, 8, 38], BF16)
    ubf_t = [Tk("ubf%d" % i, inherit=QT_t.alldeps()) for i in range(8)]
    us_t = Tk("usext", inherit=QT_t.alldeps())
    big_dead = sum([k.alldeps() for k in [KT_t, mown_t, mctx_t, msam_t] + V_t], [])
    diag = [at(BIG + i * 7936, [128, 31, 128], BF16) for i in range(2)]
    diag_t = [Tk("diag%d" % i, inherit=big_dead) for i in range(2)]
    sig = [at(BIG + 16384 + i * 2080, [128, 520], F32) for i in range(2)]
    sig_t = [Tk("sig%d" % i, inherit=big_dead) for i in range(2)]
    sig_rot = Rot([0, 1])
    sq = [at(BIG + 20736 + i * 2048, [128, 512], F32) for i in range(2)]
    sq_t = [Tk("sq%d" % i, inherit=big_dead) for i in range(2)]
    mean_s = at(BIG + 24832, [128, 512], F32); mean_t = Tk("mean", inherit=big_dead)
    var_s = at(BIG + 26880, [128, 512], F32); var_t = Tk("var", inherit=big_dead)
    rstd_s = at(BIG + 28928, [128, 512], F32); rstd_t = Tk("rstd", inherit=big_dead)
    t1 = [at(BIG + 30976 + i * 2048, [128, 512], F32) for i in range(2)]
    t1_t = [Tk("t1_%d" % i, inherit=big_dead) for i in range(2)]
    cst = at(BIG + 35072, [32, 1024], F32); cst_t = Tk("cst", inherit=big_dead)
    cst_s = at(BIG + 39168, [8, 1024], F32); csts_t = Tk("csts", inherit=big_dead)
    add_slot(BIG + 44032, big_dead)

    for blk in range(4):
        def f(sv, blk=blk):
            s3 = sv.rearrange("p (c n) -> p c n", c=16)
            return [(s3[:, :, 0:256], w_in_v[:, :, 3072 + 256 * blk: 3072 + 256 * blk + 256]),
                    (s3[:, :, 256:512], w_in_v[:, :, 4096 + 256 * blk: 4096 + 256 * blk + 256])]
        si = load_slot(f)
        sv = slots[si].ap().rearrange("p (c n) -> p c n", c=16)
        for ct in range(2):
            ch = 2 * blk + ct
            for (t0, n, src) in [(0, 512, "x"), (512, 512, "x"), (1024, 8, "x"), (0, 32, "c")]:
                ba, bg = mm_rot.next(), mm_rot.next()
                if src == "x":
                    rhs = lambda c: xT_b.ap()[:, c, t0:t0 + n]
                    rt = xT_t
                else:
                    rhs = lambda c: xc32_b.ap()[:, c, 0:32]
                    rt = xc32_t
                group(ba, banks[ba].ap()[:, :n],
                      [(sv[:, c, ct * 128:(ct + 1) * 128], rhs(c)) for c in range(16)], [slot_t[si], rt])
                group(bg, banks[bg].ap()[:, :n],
                      [(sv[:, c, 256 + ct * 128:256 + (ct + 1) * 128], rhs(c)) for c in range(16)],
                      [slot_t[si], rt])
                sgi = sig_rot.next()
                kb.emit("act", lambda: act.activation(out=sig[sgi].ap()[:, :n], in_=banks[bg].ap()[:, :n],
                                                      func=AF.Sigmoid),
                        reads=[bank_t[bg]], writes=[sig_t[sgi]])
                if src == "c":
                    kb.emit("dve", lambda: dve.tensor_tensor(out=u_bf.ap()[:, ch, 0:30], in0=banks[ba].ap()[:, 2:32],
                                                             in1=sig[sgi].ap()[:, 2:32], op=ALU.mult),
                            reads=[bank_t[ba], sig_t[sgi]], writes=[ubf_t[ch]])
                elif t0 < NP:
                    kb.emit("dve", lambda: dve.tensor_tensor(out=u_bf.ap()[:, ch, 30 + t0:30 + t0 + n],
                                                             in0=banks[ba].ap()[:, :n], in1=sig[sgi].ap()[:, :n],
                                                             op=ALU.mult),
                            reads=[bank_t[ba], sig_t[sgi]], writes=[ubf_t[ch]])
                    if t0 == 512:
                        kb.emit("dve", lambda: dve.tensor_tensor(out=u32.ap()[:, ch, 0:32],
                                                                 in0=banks[ba].ap()[:, 480:512],
                                                                 in1=sig[sgi].ap()[:, 480:512], op=ALU.mult),
                                reads=[bank_t[ba], sig_t[sgi]], writes=[u32_t])
                else:
                    kb.emit("dve", lambda: dve.tensor_tensor(out=u32.ap()[:, ch, 32:40], in0=banks[ba].ap()[:, :8],
                                                             in1=sig[sgi].ap()[:, :8], op=ALU.mult),
                            reads=[bank_t[ba], sig_t[sgi]], writes=[u32_t])
                    kb.emit("dve", lambda: dve.tensor_tensor(out=us_ext.ap()[:, ch, 30:38],
                                                             in0=banks[ba].ap()[:, :8],
                                                             in1=sig[sgi].ap()[:, :8], op=ALU.mult),
                            reads=[bank_t[ba], sig_t[sgi]], writes=[us_t])

    if STOP <= 5:
        return
    for ch in range(8):
        bi = 6 + ch // 4
        kb.emit("pe", lambda ch=ch, bi=bi: pe.transpose(banks[bi].ap()[0:32, (ch % 4) * 128:(ch % 4 + 1) * 128],
                                                        u32.ap()[:, ch, 0:32], ident_f.ap()),
                reads=[u32_t, ident_f_t], writes=[bank_t[bi]], signal=(ch % 4 == 3))
    for half in range(2):
        kb.emit("act", lambda half=half: act.copy(out=cst.ap()[0:32, half * 512:(half + 1) * 512],
                                                  in_=banks[6 + half].ap()[0:32, :]),
                reads=[bank_t[6 + half]], writes=[cst_t])
    kb.dma("sp", [(convp_d[0:30, :], cst.ap()[2:32, :])], reads=[cst_t], final=True)
    for ch in range(8):
        bi = 6 + ch // 4
        kb.emit("pe", lambda ch=ch, bi=bi: pe.transpose(banks[bi].ap()[0:8, (ch % 4) * 128:(ch % 4 + 1) * 128],
                                                        u32.ap()[:, ch, 32:40], ident_f.ap()),
                reads=[u32_t, ident_f_t], writes=[bank_t[bi]], signal=(ch % 4 == 3))
    for half in range(2):
        kb.emit("act", lambda half=half: act.copy(out=cst_s.ap()[0:8, half * 512:(half + 1) * 512],
                                                  in_=banks[6 + half].ap()[0:8, :]),
                reads=[bank_t[6 + half]], writes=[csts_t])
    kb.dma("sp", [(convs_d[22:30, :], cst_s.ap()[0:8, :])], reads=[csts_t], final=True)

    if STOP <= 6:
        return
    kb.dma("pool", [(us_ext.ap()[:, :, 0:30], scT_d.rearrange("(c p) j -> p c j", p=128))], writes=[us_t])
    cacc = at(RX, [128, 8, NT], F32); cacc_t = [Tk("cacc%d" % i, inherit=xT_t.alldeps()) for i in range(8)]
    for ch in range(8):
        dg = diag[ch % 2]
        dgt = diag_t[ch % 2]
        for j in range(31):
            kb.emit("dve", lambda j=j: dve.tensor_scalar(out=dg.ap()[:, j, :], in0=ident_b.ap(),
                                                         scalar1=pch.ap()[:, ch, j:j + 1], scalar2=None,
                                                         op0=ALU.mult),
                    reads=[ident_b_t, pch_t], writes=[dgt])
        for (t0, n) in TB:
            bi = mm_rot.next()
            if t0 < NP:
                prs = [(dg.ap()[:, j, :], u_bf.ap()[:, ch, t0 + j:t0 + j + n]) for j in range(31)]
                rr = [dgt, ubf_t[ch]]
            else:
                prs = [(dg.ap()[:, j, :], us_ext.ap()[:, ch, j:j + 8]) for j in range(31)]
                rr = [dgt, us_t]
            group(bi, banks[bi].ap()[:, :n], prs, rr)
            kb.emit("act", lambda bi=bi, t0=t0, n=n: act.activation(out=cacc.ap()[:, ch, t0:t0 + n],
                                                                    in_=banks[bi].ap()[:, :n], func=AF.Identity,
                                                                    bias=pch.ap()[:, ch, 31:32]),
                    reads=[bank_t[bi], pch_t], writes=[cacc_t[ch]])

    if STOP <= 7:
        return
    sq_rot = Rot([0, 1])
    t1_rot = Rot([0, 1])
    stat_sets = [(mean_s, var_s, rstd_s, mean_t, var_t, rstd_t)]
    m1 = at(BIG + 60416, [128, 512], F32); v1 = at(BIG + 62464, [128, 512], F32); r1 = at(BIG + 64512, [128, 512], F32)
    stat_sets.append((m1, v1, r1, Tk("mean1", inherit=big_dead), Tk("var1", inherit=big_dead),
                      Tk("rstd1", inherit=big_dead)))
    m2 = at(BIG + 66560, [128, 8], F32); v2 = at(BIG + 66592, [128, 8], F32); r2 = at(BIG + 66624, [128, 8], F32)
    stat_sets.append((m2, v2, r2, Tk("mean2", inherit=big_dead), Tk("var2", inherit=big_dead),
                      Tk("rstd2", inherit=big_dead)))
    stat_banks = [(6, 7), (0, 1), (2, 3)]

    def cln_stats(i):
        (t0, n) = TB[i]
        b1, b2_ = stat_banks[i]
        for ch in range(8):
            qi = sq_rot.next()
            kb.emit("act", lambda: act.activation(out=sq[qi].ap()[:, :n], in_=cacc.ap()[:, ch, t0:t0 + n],
                                                  func=AF.Square),
                    reads=[cacc_t[ch]], writes=[sq_t[qi]])
            kb.emit("pe", lambda: pe.matmul(banks[b1].ap()[:, :n], ones_f.ap(), cacc.ap()[:, ch, t0:t0 + n],
                                            start=(ch == 0), stop=(ch == 7)),
                    reads=[ones_t, cacc_t[ch]], writes=[bank_t[b1]])
            kb.emit("pe", lambda: pe.matmul(banks[b2_].ap()[:, :n], ones_f.ap(), sq[qi].ap()[:, :n],
                                            start=(ch == 0), stop=(ch == 7)),
                    reads=[ones_t, sq_t[qi]], writes=[bank_t[b2_]])

    def cln_chain(i):
        (t0, n) = TB[i]
        b1, b2_ = stat_banks[i]
        mean_x, var_x, rstd_x, mean_xt, var_xt, rstd_xt = stat_sets[i]
        kb.emit("act", lambda: act.mul(out=mean_x.ap()[:, :n], in_=banks[b1].ap()[:, :n], mul=1.0 / 1024.0),
                reads=[bank_t[b1]], writes=[mean_xt])
        kb.emit("dve", lambda: dve.tensor_tensor(out=var_x.ap()[:, :n], in0=mean_x.ap()[:, :n],
                                                 in1=mean_x.ap()[:, :n], op=ALU.mult),
                reads=[mean_xt], writes=[var_xt])
        kb.emit("dve", lambda: dve.scalar_tensor_tensor(out=var_x.ap()[:, :n], in0=banks[b2_].ap()[:, :n],
                                                        scalar=1.0 / 1024.0, in1=var_x.ap()[:, :n],
                                                        op0=ALU.mult, op1=ALU.subtract),
                reads=[bank_t[b2_], var_xt], writes=[var_xt])
        kb.emit("act", lambda: act.activation(out=rstd_x.ap()[:, :n], in_=var_x.ap()[:, :n], func=AF.Sqrt,
                                              bias=eps_c.ap()[:, 0:1]),
                reads=[var_xt, eps_t], writes=[rstd_xt])
        kb.emit("dve", lambda: dve.reciprocal(out=rstd_x.ap()[:, :n], in_=rstd_x.ap()[:, :n]),
                reads=[rstd_xt], writes=[rstd_xt])

    def cln_norm(i):
        (t0, n) = TB[i]
        mean_x, var_x, rstd_x, mean_xt, var_xt, rstd_xt = stat_sets[i]
        for ch in range(8):
            ti = t1_rot.next()
            kb.emit("dve", lambda: dve.tensor_tensor(out=t1[ti].ap()[:, :n], in0=cacc.ap()[:, ch, t0:t0 + n],
                                                     in1=mean_x.ap()[:, :n], op=ALU.subtract),
                    reads=[cacc_t[ch], mean_xt], writes=[t1_t[ti]])
            kb.emit("dve", lambda: dve.tensor_tensor(out=t1[ti].ap()[:, :n], in0=t1[ti].ap()[:, :n],
                                                     in1=rstd_x.ap()[:, :n], op=ALU.mult),
                    reads=[t1_t[ti], rstd_xt], writes=[t1_t[ti]])
            kb.emit("act", lambda: act.activation(out=mixed.ap()[:, 8 + ch, t0:t0 + n], in_=t1[ti].ap()[:, :n],
                                                  func=AF.Silu, bias=pch.ap()[:, ch, 33:34],
                                                  scale=pch.ap()[:, ch, 32:33]),
                    reads=[t1_t[ti], pch_t], writes=[mixed_t])

    cln_stats(0)
    cln_chain(0)
    cln_stats(1)
    cln_norm(0)
    cln_chain(1)
    cln_stats(2)
    cln_norm(1)
    cln_chain(2)
    cln_norm(2)

    if STOP <= 8:
        return
    big_dead2 = sum([k.alldeps() for k in diag_t + sig_t + sq_t + [mean_t, var_t, rstd_t, cst_t, csts_t] + t1_t
                     + [x for st in stat_sets[1:] for x in st[3:]]], [])
    big_dead2 = big_dead2 + drop_slots()
    add_slot(RQ, sum([k.alldeps() for k in ubf_t + [us_t]], []))
    acc = at(BIG, [128, 9, D], F32)
    acc_t = [Tk("acc%d" % i, inherit=big_dead + big_dead2) for i in range(9)]
    for ti, (t0, rows) in enumerate(TT):
        kb.dma("sp", [(acc.ap()[:rows, ti, :], xres_d[t0:t0 + rows, :])], writes=[acc_t[ti]])
    rx_dead2 = sum([k.alldeps() for k in cacc_t], [])
    hidden = at(RX, [128, 8, NT], BF16); hidden_t = Tk("hidden", inherit=rx_dead2)
    gb = at(RX + 16512, [128, 2, D], F32); gb_t = Tk("gb", inherit=rx_dead2)
    kb.dma("sp", [(gb.ap(), lngb_d[:, 0:2, :])], writes=[gb_t])
    hb = [at(TMP + i * 4096, [128, D], BF16) for i in range(2)]
    tmp_dead = sum([k.alldeps() for k in ao_t + EP_t + [Es_t, ao_s_t]], [])
    hb_t = [Tk("hb%d" % i, inherit=tmp_dead) for i in range(2)]

    class LNPipe:
        NPH = 5
        OFF = [0, 1, 2, 3, 3]

        def __init__(self, tail, gmul_eng):
            self.q = []
            self.step = 0
            self.tail = tail
            self.gmul_eng = gmul_eng

        def push(self, ti, rows, t0):
            self.q.append((ti, rows, t0))
            self._step()

        def flush(self):
            for _ in range(self.OFF[-1]):
                self._step()

        def _step(self):
            sidx = self.step
            self.step += 1
            for p in sorted(range(self.NPH), key=lambda p: (-self.OFF[p], p)):
                idx = sidx - self.OFF[p]
                if 0 <= idx < len(self.q):
                    self._phase(p, *self.q[idx])

        def _phase(self, p, ti, rows, t0):
            k = ti % 4
            a = acc.ap()[:rows, ti, :]
            if p == 0:
                for q in range(4):
                    kb.emit("dve", lambda q=q: dve.bn_stats(out=st6[k].ap()[:rows, q, :],
                                                            in_=acc.ap()[:rows, ti, q * 512:(q + 1) * 512]),
                            reads=[acc_t[ti]], writes=[st6_t[k]])
                kb.emit("dve", lambda: dve.bn_aggr(out=mv[k].ap()[:rows, :],
                                                   in_=st6[k].ap()[:rows].rearrange("p q s -> p (q s)")),
                        reads=[st6_t[k]], writes=[mv_t[k]])
                kb.emit("act", lambda: act.activation(out=sd[k].ap()[:rows, 0:1], in_=mv[k].ap()[:rows, 1:2],
                                                      func=AF.Sqrt, bias=eps_c.ap()[:rows, 0:1]),
                        reads=[mv_t[k], eps_t], writes=[sd_t[k]])
                kb.emit("dve", lambda: dve.reciprocal(out=sd[k].ap()[:rows, 1:2], in_=sd[k].ap()[:rows, 0:1]),
                        reads=[sd_t[k]], writes=[sd_t[k]])
                kb.emit("dve", lambda: dve.tensor_scalar(out=nm[k].ap()[:rows, 0:1], in0=mv[k].ap()[:rows, 0:1],
                                                         scalar1=sd[k].ap()[:rows, 1:2], scalar2=-1.0,
                                                         op0=ALU.mult, op1=ALU.mult),
                        reads=[mv_t[k], sd_t[k]], writes=[nm_t[k]])
            elif p == 1:
                kb.emit("act", lambda: act.activation(out=a, in_=a, func=AF.Identity,
                                                      scale=sd[k].ap()[:rows, 1:2], bias=nm[k].ap()[:rows, 0:1]),
                        reads=[acc_t[ti], nm_t[k], sd_t[k]], writes=[acc_t[ti]])
            elif p == 2:
                ge = pool if self.gmul_eng == "pool" else dve
                kb.emit(self.gmul_eng, lambda: ge.tensor_tensor(out=a, in0=a, in1=gb.ap()[:rows, 0, :],
                                                                op=ALU.mult),
                        reads=[acc_t[ti], gb_t], writes=[acc_t[ti]])
            elif p == 3:
                en = "pool" if ti % 2 == 0 else "dve"
                be = pool if en == "pool" else dve
                kb.emit(en, lambda: be.tensor_tensor(out=a, in0=a, in1=gb.ap()[:rows, 1, :], op=ALU.add),
                        reads=[acc_t[ti], gb_t], writes=[acc_t[ti]])
            else:
                self.tail(ti, rows, t0)

    hT = at(RM, [128, 16, NT], BF16)
    hT_t = Tk("hT", inherit=mixed_t.alldeps())

    class LN1Pipe:
        OFF = [0, 1, 2, 3]

        def __init__(self):
            self.q = []
            self.step = 0

        def push(self, ti, rows, t0):
            self.q.append((ti, rows, t0))
            self._step()

        def flush(self):
            for _ in range(self.OFF[-1]):
                self._step()

        def _step(self):
            sidx = self.step
            self.step += 1
            for p in sorted(range(len(self.OFF)), key=lambda p: (-self.OFF[p], p)):
                idx = sidx - self.OFF[p]
                if 0 <= idx < len(self.q):
                    self._phase(p, *self.q[idx])

        def _phase(self, p, ti, rows, t0):
            k = ti % 4
            k2 = ti % 2
            if p == 0:
                for q in range(4):
                    kb.emit("dve", lambda q=q: dve.bn_stats(out=st6[k].ap()[:rows, q, :],
                                                            in_=acc.ap()[:rows, ti, q * 512:(q + 1) * 512]),
                            reads=[acc_t[ti]], writes=[st6_t[k]])
                kb.emit("dve", lambda: dve.bn_aggr(out=mv[k].ap()[:rows, :],
                                                   in_=st6[k].ap()[:rows].rearrange("p q s -> p (q s)")),
                        reads=[st6_t[k]], writes=[mv_t[k]])
            elif p == 1:
                kb.emit("act", lambda: act.activation(out=sd9.ap()[:rows, ti, 0:1], in_=mv[k].ap()[:rows, 1:2],
                                                      func=AF.Sqrt, bias=eps_c.ap()[:rows, 0:1]),
                        reads=[mv_t[k], eps_t], writes=[sd9_t[ti]])
                kb.emit("dve", lambda: dve.reciprocal(out=sd9.ap()[:rows, ti, 1:2], in_=sd9.ap()[:rows, ti, 0:1]),
                        reads=[sd9_t[ti]], writes=[sd9_t[ti]])
                kb.emit("dve", lambda: dve.tensor_scalar(out=nm9.ap()[:rows, ti:ti + 1], in0=mv[k].ap()[:rows, 0:1],
                                                         scalar1=sd9.ap()[:rows, ti, 1:2], scalar2=-1.0,
                                                         op0=ALU.mult, op1=ALU.mult),
                        reads=[mv_t[k], sd9_t[ti]], writes=[nm9_t[ti]])
            elif p == 2:
                kb.emit("act", lambda: act.activation(out=hb[k2].ap()[:rows, :], in_=acc.ap()[:rows, ti, :],
                                                      func=AF.Identity, scale=sd9.ap()[:rows, ti, 1:2],
                                                      bias=nm9.ap()[:rows, ti:ti + 1]),
                        reads=[acc_t[ti], sd9_t[ti], nm9_t[ti]], writes=[hb_t[k2]])
            else:
                for half in range(2):
                    bi = 6 + half
                    trv = banks[bi].ap().bitcast(BF16)
                    for q in range(8):
                        cidx = half * 8 + q
                        kb.emit("pe", lambda q=q, cidx=cidx, trv=trv: pe.transpose(
                            trv[:, q * 128:q * 128 + rows], hb[k2].ap()[:rows, cidx * 128:(cidx + 1) * 128],
                            ident_b.ap()[:rows, :rows]),
                            reads=[hb_t[k2], ident_b_t], writes=[bank_t[bi]], signal=(q == 7))
                    for d in mixed_t.alldeps():
                        hT_t._addr(d)
                    for q in range(8):
                        cidx = half * 8 + q
                        src = trv[:, q * 128:q * 128 + rows]
                        dst = hT.ap()[:, cidx, t0:t0 + rows]
                        gcol = ln1T.ap()[:, 0, cidx:cidx + 1]
                        bcol = ln1T.ap()[:, 1, cidx:cidx + 1]
                        if half == 0:
                            kb.emit("act", lambda src=src, dst=dst, gcol=gcol, bcol=bcol: act.activation(
                                out=dst, in_=src, func=AF.Identity, scale=gcol, bias=bcol),
                                reads=[bank_t[bi], ln1T_t], writes=[hT_t])
                        else:
                            kb.emit("dve", lambda src=src, dst=dst, gcol=gcol, bcol=bcol: dve.tensor_scalar(
                                out=dst, in0=src, scalar1=gcol, scalar2=bcol, op0=ALU.mult, op1=ALU.add),
                                reads=[bank_t[bi], ln1T_t], writes=[hT_t])

    def ln1_deferred(ti, rows):
        a = acc.ap()[:rows, ti, :]
        kb.emit("act", lambda: act.activation(out=a, in_=a, func=AF.Identity, scale=sd9.ap()[:rows, ti, 1:2],
                                              bias=nm9.ap()[:rows, ti:ti + 1]),
                reads=[acc_t[ti], sd9_t[ti], nm9_t[ti]], writes=[acc_t[ti]])
        kb.emit("dve", lambda: dve.tensor_tensor(out=a, in0=a, in1=gb.ap()[:rows, 0, :], op=ALU.mult),
                reads=[acc_t[ti], gb_t], writes=[acc_t[ti]])
        kb.emit("dve", lambda: dve.tensor_tensor(out=a, in0=a, in1=gb.ap()[:rows, 1, :], op=ALU.add),
                reads=[acc_t[ti], gb_t], writes=[acc_t[ti]])

    ln1 = LN1Pipe()
    for cb in range(4):
        si, sv = load_w512(w_out_v, cb * 512)
        for ti, (t0, rows) in enumerate(TT):
            bi = mm_rot.next()
            group(bi, banks[bi].ap()[:rows, :512],
                  [(mixed.ap()[:, e, t0:t0 + rows], sv[:, e, :]) for e in range(16)], [slot_t[si], mixed_t])
            kb.emit("dve", lambda bi=bi, ti=ti, rows=rows, cb=cb: dve.scalar_tensor_tensor(
                out=acc.ap()[:rows, ti, cb * 512:(cb + 1) * 512], in0=acc.ap()[:rows, ti, cb * 512:(cb + 1) * 512],
                scalar=ALPHA, in1=banks[bi].ap()[:rows, :512], op0=ALU.mult, op1=ALU.add),
                reads=[bank_t[bi], acc_t[ti]], writes=[acc_t[ti]])
            if cb == 3:
                ln1.push(ti, rows, t0)
    ln1.flush()

    def ln2_tail(ti, rows, t0):
        kb.dma("sp", [(y_d[t0:t0 + rows, :], acc.ap()[:rows, ti, :])], reads=[acc_t[ti]], final=True)

    ln2 = LNPipe(ln2_tail, "pool")

    if STOP <= 9:
        return
    rt = [at(TMP + 8192 + i * 2048, [128, 512], F32) for i in range(2)]
    rt_t = [Tk("rt%d" % i, inherit=tmp_dead) for i in range(2)]
    rt_rot = Rot([0, 1])
    mm_rot.items = list(range(8))
    for b in range(8):
        cache_copies_part(b)
        if b == 1:
            kb.dma("sp", [(gb.ap(), lngb_d[:, 2:4, :])], writes=[gb_t])
        for ub in range(2):
            si, sv = load_w512(w_up_v, b * 1024 + ub * 512)
            for ft in range(4):
                fc = ub * 4 + ft
                for (t0, n) in TB:
                    bi = mm_rot.next()
                    group(bi, banks[bi].ap()[:, :n],
                          [(sv[:, c, ft * 128:(ft + 1) * 128], hT.ap()[:, c, t0:t0 + n]) for c in range(16)],
                          [slot_t[si], hT_t])
                    ri = rt_rot.next()
                    kb.emit("act", lambda bi=bi, ri=ri, n=n: act.activation(out=rt[ri].ap()[:, :n],
                                                                            in_=banks[bi].ap()[:, :n], func=AF.Relu),
                            reads=[bank_t[bi]], writes=[rt_t[ri]])
                    kb.emit("dve", lambda ri=ri, fc=fc, t0=t0, n=n: dve.tensor_tensor(
                        out=hidden.ap()[:, fc, t0:t0 + n], in0=rt[ri].ap()[:, :n], in1=rt[ri].ap()[:, :n],
                        op=ALU.mult),
                        reads=[rt_t[ri]], writes=[hidden_t])
                if b == 0:
                    ln1_deferred(fc, TT[fc][1])
                    if fc == 7:
                        ln1_deferred(8, TT[8][1])
        def load_down(dh, b=b):
            def f(sv):
                s3 = sv.rearrange("p (c n) -> p c n", c=8)
                return [(s3[:, 0:4, :], w_down_v[:, b * 8:b * 8 + 4, dh * 1024:(dh + 1) * 1024]),
                        (s3[:, 4:8, :], w_down_v[:, b * 8 + 4:b * 8 + 8, dh * 1024:(dh + 1) * 1024])]
            si = load_slot(f)
            return si, slots[si].ap().rearrange("p (c n) -> p c n", c=8)

        def down_group(si, sv, dh, cg, ti, t0, rows, b=b):
            bi = mm_rot.next()
            group(bi, banks[bi].ap()[:rows, :512],
                  [(hidden.ap()[:, fc, t0:t0 + rows], sv[:, fc, cg * 512:(cg + 1) * 512]) for fc in range(8)],
                  [slot_t[si], hidden_t])
            c0 = dh * 1024 + cg * 512
            if b == 0:
                kb.emit("dve", lambda: dve.scalar_tensor_tensor(
                    out=acc.ap()[:rows, ti, c0:c0 + 512], in0=acc.ap()[:rows, ti, c0:c0 + 512],
                    scalar=ALPHA, in1=banks[bi].ap()[:rows, :512], op0=ALU.mult, op1=ALU.add),
                    reads=[bank_t[bi], acc_t[ti]], writes=[acc_t[ti]])
            else:
                kb.emit("dve", lambda: dve.tensor_tensor(
                    out=acc.ap()[:rows, ti, c0:c0 + 512], in0=banks[bi].ap()[:rows, :512],
                    in1=acc.ap()[:rows, ti, c0:c0 + 512], op=ALU.add),
                    reads=[bank_t[bi], acc_t[ti]], writes=[acc_t[ti]])

        if b < 7:
            for dh in range(2):
                si, sv = load_down(dh)
                for ti, (t0, rows) in enumerate(TT):
                    for cg in range(2):
                        down_group(si, sv, dh, cg, ti, t0, rows)
        else:
            dl = [load_down(0), load_down(1)]
            for ti, (t0, rows) in enumerate(TT):
                for dh in range(2):
                    for cg in range(2):
                        down_group(dl[dh][0], dl[dh][1], dh, cg, ti, t0, rows)
                ln2.push(ti, rows, t0)

    ln2.flush()


def _mult(delta):
    delta = np.asarray(delta)
    m = ((delta >= 0) & (delta <= 128)).astype(np.float32)
    m += ((delta >= 0) & (delta <= 512) & (delta % 4 == 0)).astype(np.float32)
    m += ((delta >= 0) & (delta <= 2048) & (delta % 16 == 0)).astype(np.float32)
    return m


def _masks():
    p = np.arange(128)[:, None]
    c = np.arange(128)[None, :]
    mown = np.zeros((128, 8, 128), np.float32)
    for dl in range(8):
        mown[:, 7 - dl, :] = _mult(dl * 128 + c - p)
    mctx = np.zeros((128, 15, 128), np.float32)
    for dl in range(1, 16):
        mctx[:, 15 - dl, :] = _mult(dl * 128 + c - p)
    msam = np.zeros((128, 17, 8), np.float32)
    q = np.arange(8)[None, :]
    for j in range(16):
        msam[:, j, :] = _mult(2048 + q - (j * 128 + p))
    msam[:, 16, :] = _mult(q - p) * (p < 8)
    return mown.reshape(128, -1), mctx.reshape(128, -1), msam.reshape(128, -1)


def kernel(x_prompt, x_sample, cache_k, cache_v, state_conv, w_in, w_dw, b_dw, ln_conv_g, ln_conv_b,
           w_out, ln1_g, ln1_b, w_up, w_down, ln2_g, ln2_b):
    f = lambda a: np.ascontiguousarray(np.asarray(a, dtype=np.float32))
    x_prompt, x_sample = f(x_prompt), f(x_sample)
    cache_k, cache_v, state_conv = f(cache_k), f(cache_v), f(state_conv)
    w_in0, w_out0, w_up0, w_down0 = f(w_in)[0], f(w_out)[0], f(w_up)[0], f(w_down)[0]
    pc = np.concatenate([f(w_dw)[0].T, f(b_dw)[0][:, None], f(ln_conv_g)[0][:, None], f(ln_conv_b)[0][:, None]],
                        axis=1)
    pch = np.ascontiguousarray(pc.reshape(8, 128, 34).transpose(1, 0, 2))
    lngb = np.stack([f(ln1_g)[0], f(ln1_b)[0], f(ln2_g)[0], f(ln2_b)[0]])
    lngb = np.ascontiguousarray(np.broadcast_to(lngb[None], (128, 4, D)))
    mown, mctx, msam = _masks()
    ident = np.eye(128, dtype=np.float32)
    ln1T = np.ascontiguousarray(np.stack([f(ln1_g)[0].reshape(16, 128).T, f(ln1_b)[0].reshape(16, 128).T], axis=1))

    in_maps = []
    for c in range(8):
        b, half = c // 2, c % 2
        own = x_prompt[b, half * NP:(half + 1) * NP]
        xs = x_sample[c]
        xres = np.concatenate([own, xs], axis=0)
        xT = np.ascontiguousarray(xres.T)
        if half == 1:
            ctx = x_prompt[b, 0:NP]
            xcT = np.ascontiguousarray(ctx.T)
            mc = mctx
        else:
            xcT = np.zeros((D, NP), np.float32)
            mc = np.zeros_like(mctx)
        ck = cache_k[0, c].reshape(2048, 1024)
        cv = cache_v[0, c].reshape(2048, 1024)
        in_maps.append(dict(
            xT=xT, xcT=xcT, xc32=np.ascontiguousarray(xcT[:, NP - 32:NP]), xres=xres,
            w_in=w_in0, w_out=w_out0, w_up=w_up0, w_down=w_down0, pch=pch, lngb=lngb,
            mown=mown, mctx=mc, msam=msam, ident=ident, ln1T=ln1T,
            ckT=np.ascontiguousarray(ck.T), ck=ck, cv=cv,
            scT=np.ascontiguousarray(state_conv[0, c].T), sc=state_conv[0, c],
        ))
    nc = build_program()
    res = run_bass_kernel_spmd(nc, in_maps, core_ids=list(range(8)))
    R = res.results
    y_prompt = np.zeros((4, 2048, D), np.float32)
    y_sample = np.zeros((8, 8, D), np.float32)
    kp = np.zeros((1, 4, 2048, 8, 128), np.float32)
    vp = np.zeros((1, 4, 2048, 8, 128), np.float32)
    cp = np.zeros((1, 4, 30, 1024), np.float32)
    ksw = np.zeros((1, 8, 2048, 8, 128), np.float32)
    vsw = np.zeros((1, 8, 2048, 8, 128), np.float32)
    cs = np.zeros((1, 8, 30, 1024), np.float32)
    for c in range(8):
        b, half = c // 2, c % 2
        r = R[c]
        y = np.asarray(r["y"])
        y_prompt[b, half * NP:(half + 1) * NP] = y[:NP]
        y_sample[c] = y[NP:NT]
        kp[0, b, half * NP:(half + 1) * NP] = np.asarray(r["kout"]).reshape(NP, 8, 128)
        vp[0, b, half * NP:(half + 1) * NP] = np.asarray(r["vout"]).reshape(NP, 8, 128)
        if half == 1:
            cp[0, b] = np.asarray(r["convp"])
        ksw[0, c] = np.asarray(r["ks"]).reshape(2048, 8, 128)
        vsw[0, c] = np.asarray(r["vs"]).reshape(2048, 8, 128)
        cs[0, c] = np.asarray(r["convs"])
    return (y_prompt, y_sample, kp, vp, cp, ksw, vsw, cs)
```
